# Optimizing a Trainium2 kernel written in Bass

```python
import math
import jax, jax.numpy as jnp
from jax import lax
import numpy as np

D_MODEL = 1024
BATCH = 8
SEQ = 2048
DEPTH = 1
DEC_BATCH = 32
DEC_SEQ = 8
PAST_LEN = 16384
PAGE_SIZE = 128

HEAD_DIM = 64
D_MIX = D_MODEL
D_A = D_MIX // 2
A_GROUPS = D_A // HEAD_DIM
CHUNK = 128
N_HEADS = (D_MIX - D_A) // HEAD_DIM
D_B = N_HEADS * HEAD_DIM
N_KV = 2
GQA = N_HEADS // N_KV
KV_COLS = N_KV * HEAD_DIM
CMP_BLOCK = 32
CMP_STRIDE = 16
CMP_RATIO = CMP_BLOCK // CMP_STRIDE
CMP_HIDDEN = 128
SLC_BLOCK = 64
TOP_N = 16
WINDOW = 512
ROPE_THETA = 10000.0
EPS = 1e-6
Q_BLOCK = 64
WIN_QBLOCK = 128
SCALE = HEAD_DIM ** -0.5
NEG = -1e30
FORCE = 1e9
D_IN = 3 * D_A + D_B + 6 * KV_COLS + 3 * N_HEADS + D_B

kernel_name = 'hymba_gmlp_nsa_decode_step'


def rmsnorm(x, g):
    xf = x.astype(jnp.float32)
    y = xf * lax.rsqrt(jnp.mean(xf * xf, axis=-1, keepdims=True) + EPS)
    return y.astype(x.dtype) * g


def layernorm(x, g, b):
    xf = x.astype(jnp.float32)
    mu = jnp.mean(xf, axis=-1, keepdims=True)
    var = jnp.mean(jnp.square(xf - mu), axis=-1, keepdims=True)
    return ((xf - mu) * lax.rsqrt(var + EPS)).astype(x.dtype) * g + b


def rope(x, pos):
    half = HEAD_DIM // 2
    inv = ROPE_THETA ** (-jnp.arange(half, dtype=jnp.float32) / half)
    ang = pos.astype(jnp.float32)[:, None] * inv
    shape = (pos.shape[0],) + (1,) * (x.ndim - 3) + (half,)
    cos = jnp.cos(ang).reshape(shape)
    sin = jnp.sin(ang).reshape(shape)
    xf = x.astype(jnp.float32)
    x1, x2 = xf[..., :half], xf[..., half:]
    return jnp.concatenate([x1 * cos - x2 * sin, x2 * cos + x1 * sin], axis=-1).astype(x.dtype)


def masked_softmax(s, mask):
    s = jnp.where(mask, s.astype(jnp.float32), NEG)
    e = jnp.where(mask, jnp.exp(s - jnp.max(s, axis=-1, keepdims=True)), 0.0)
    return e / jnp.maximum(jnp.sum(e, axis=-1, keepdims=True), 1.0)


def project(x, norm_g, w_in):
    B, T, _ = x.shape
    p = rmsnorm(x, norm_g) @ w_in
    cuts = np.cumsum([D_A, D_A, D_A, D_B, 6 * KV_COLS, 3 * N_HEADS]).tolist()
    u, v, z_a, q, kv, g, z_b = jnp.split(p, cuts, axis=-1)
    return (jax.nn.gelu(u), jax.nn.gelu(v), z_a,
            q.reshape(B, T, N_KV, GQA, HEAD_DIM),
            kv.reshape(B, T, 6, N_KV, HEAD_DIM),
            jax.nn.sigmoid(g.reshape(B, T, N_KV, GQA, 3)), z_b)


def chunk_gmlp(u, v, ln_g, ln_b, w_s, b_s):
    B, T, _ = v.shape
    v = layernorm(v, ln_g, ln_b)
    n_c = -(-T // CHUNK)
    vp = jnp.pad(v, ((0, 0), (0, n_c * CHUNK - T), (0, 0))).reshape(B, n_c, CHUNK, A_GROUPS, HEAD_DIM)
    w = w_s * jnp.tril(jnp.ones((CHUNK, CHUNK), w_s.dtype))
    mixed = jnp.einsum('gts,bcsgd->bctgd', w, vp) + b_s.T[None, None, :, :, None]
    mixed = mixed.reshape(B, n_c * CHUNK, D_A)[:, :T]
    return u * mixed, v


def compress(k_raw, pe, w1, b1, w2, n_cmp):
    B = k_raw.shape[0]
    n_seg = n_cmp + CMP_RATIO - 1
    seg = k_raw[:, :n_seg * CMP_STRIDE].reshape(B, n_seg, CMP_STRIDE, N_KV, HEAD_DIM)
    seg = seg.transpose(0, 1, 3, 2, 4).reshape(B, n_seg, N_KV, CMP_STRIDE * HEAD_DIM)
    w1r = w1.reshape(CMP_RATIO, CMP_STRIDE * HEAD_DIM, CMP_HIDDEN)
    pe_r = pe.reshape(CMP_RATIO, CMP_STRIDE * HEAD_DIM)
    part = jnp.einsum('bnhx,rxk->bnhrk', seg, w1r) + jnp.einsum('rx,rxk->rk', pe_r, w1r)
    acc = b1
    for r in range(CMP_RATIO):
        acc = acc + part[:, r:r + n_cmp, :, r]
    return jax.nn.gelu(acc) @ w2


def cmp_attend(q, kc, vc, tq):
    n_cmp = kc.shape[1]
    s = jnp.einsum('bthgd,bnhd->bhgtn', q, kc) * SCALE
    end = jnp.arange(n_cmp) * CMP_STRIDE + CMP_BLOCK - 1
    p = masked_softmax(s, end[None, :] <= tq[:, None])
    return jnp.einsum('bhgtn,bnhd->bthgd', p.astype(vc.dtype), vc), p


def select_blocks(p, tq, n_cmp, n_slc):
    i = jnp.arange(n_cmp)[:, None]
    j = jnp.arange(n_slc)[None, :]
    start = i * CMP_STRIDE
    overlap = ((start <= j * SLC_BLOCK + SLC_BLOCK - 1) & (start + CMP_BLOCK - 1 >= j * SLC_BLOCK)).astype(jnp.float32)
    imp = jnp.einsum('bhgtn,nj->bhtj', p, overlap)
    tb = (tq // SLC_BLOCK)[:, None]
    valid = j * SLC_BLOCK <= tq[:, None]
    forced = (j == 0) | (j == tb) | (j == tb - 1)
    score = jnp.where(forced, FORCE, jnp.where(valid, imp, NEG))
    _, sel = lax.top_k(score, min(TOP_N, n_slc))
    return sel


def sel_attend(q, kb, vb, kpos, tq):
    B, T = q.shape[:2]
    kb = kb.reshape(B, N_KV, T, -1, HEAD_DIM)
    vb = vb.reshape(B, N_KV, T, -1, HEAD_DIM)
    kpos = kpos.reshape(B, N_KV, T, -1)
    s = jnp.einsum('bthgd,bhtkd->bhgtk', q, kb) * SCALE
    p = masked_softmax(s, (kpos <= tq[None, None, :, None])[:, :, None])
    return jnp.einsum('bhgtk,bhtkd->bthgd', p.astype(vb.dtype), vb)


def win_attend(q, k, v, tq, kpos):
    s = jnp.einsum('bthgd,bkhd->bhgtk', q, k) * SCALE
    dist = tq[:, None] - kpos[None, :]
    mask = (dist >= 0) & (dist < WINDOW) & (kpos[None, :] >= 0)
    p = masked_softmax(s, mask)
    return jnp.einsum('bhgtk,bkhd->bthgd', p.astype(v.dtype), v)


def combine(g, o_cmp, o_slc, o_win):
    B, T = g.shape[:2]
    o = g[..., 0:1] * o_cmp + g[..., 1:2] * o_slc + g[..., 2:3] * o_win
    return o.reshape(B, T, D_B)


def merge(a, z_a, b, z_b, w_out):
    return jnp.concatenate([a * jax.nn.silu(z_a), b * jax.nn.silu(z_b)], axis=-1) @ w_out


def nsa_prompt(q, kv, g, pe, w1, b1, w2, win_buf):
    B, T = q.shape[:2]
    tq = jnp.arange(T)
    kc, vc, ks, vs, kw, vw = (kv[:, :, i] for i in range(6))
    qr, ks, kw = rope(q, tq), rope(ks, tq), rope(kw, tq)
    n_cmp = (T - CMP_BLOCK) // CMP_STRIDE + 1
    kcmp = compress(kc, pe[0], w1[0], b1[0], w2[0], n_cmp)
    vcmp = compress(vc, pe[1], w1[1], b1[1], w2[1], n_cmp)
    o_cmp, p = cmp_attend(q, kcmp, vcmp, tq)
    n_slc = -(-T // SLC_BLOCK)
    sel = select_blocks(p, tq, n_cmp, n_slc)
    ks_blk = ks.reshape(B, n_slc, SLC_BLOCK, N_KV, HEAD_DIM)
    vs_blk = vs.reshape(B, n_slc, SLC_BLOCK, N_KV, HEAD_DIM)
    bi = jnp.arange(B)[:, None, None, None]
    hi = jnp.arange(N_KV)[None, :, None, None]
    nq = T // Q_BLOCK

    def sel_block(args):
        qb, selb, tb = args
        kb = ks_blk[bi, selb, :, hi]
        vb = vs_blk[bi, selb, :, hi]
        kpos = selb[..., None] * SLC_BLOCK + jnp.arange(SLC_BLOCK)
        return sel_attend(qb, kb, vb, kpos, tb)

    o_slc = lax.map(sel_block, (qr.reshape(B, nq, Q_BLOCK, N_KV, GQA, HEAD_DIM).swapaxes(0, 1),
                                sel.reshape(B, N_KV, nq, Q_BLOCK, -1).transpose(2, 0, 1, 3, 4),
                                tq.reshape(nq, Q_BLOCK)))
    o_slc = o_slc.swapaxes(0, 1).reshape(B, T, N_KV, GQA, HEAD_DIM)
    kvw = jnp.stack([kw, vw], axis=2)
    kvw_pad = jnp.pad(kvw, ((0, 0), (WINDOW, 0), (0, 0), (0, 0), (0, 0)))
    nb = T // WIN_QBLOCK

    def win_block(args):
        qb, b0 = args
        kvb = lax.dynamic_slice_in_dim(kvw_pad, b0, WIN_QBLOCK + WINDOW, axis=1)
        tb = b0 + jnp.arange(WIN_QBLOCK)
        kpos = b0 - WINDOW + jnp.arange(WIN_QBLOCK + WINDOW)
        return win_attend(qb, kvb[:, :, 0], kvb[:, :, 1], tb, kpos)

    o_win = lax.map(win_block, (qr.reshape(B, nb, WIN_QBLOCK, N_KV, GQA, HEAD_DIM).swapaxes(0, 1),
                                jnp.arange(nb) * WIN_QBLOCK))
    o_win = o_win.swapaxes(0, 1).reshape(B, T, N_KV, GQA, HEAD_DIM)
    rows = jnp.stack([kc, vc, ks, vs], axis=2)
    win_state = kvw_pad[:, -win_buf:]
    return combine(g, o_cmp, o_slc, o_win), rows, win_state


def nsa_sample(q, kv, g, cache_kv, layer, page_table, win_prev, pe, w1, b1, w2):
    Bd, T = q.shape[:2]
    tq = PAST_LEN + jnp.arange(T)
    L = PAST_LEN + T
    n_pages = PAST_LEN // PAGE_SIZE
    kc, vc, ks, vs, kw, vw = (kv[:, :, i] for i in range(6))
    qr, ks, kw = rope(q, tq), rope(ks, tq), rope(kw, tq)
    past = cache_kv[layer, page_table, :, 0:2].reshape(Bd, PAST_LEN, 2, N_KV, HEAD_DIM)
    full = jnp.concatenate([past, jnp.stack([kc, vc], axis=2)], axis=1)
    n_cmp = (L - CMP_BLOCK) // CMP_STRIDE + 1
    kcmp = compress(full[:, :, 0], pe[0], w1[0], b1[0], w2[0], n_cmp)
    vcmp = compress(full[:, :, 1], pe[1], w1[1], b1[1], w2[1], n_cmp)
    o_cmp, p = cmp_attend(q, kcmp, vcmp, tq)
    n_slc = -(-L // SLC_BLOCK)
    sel = select_blocks(p, tq, n_cmp, n_slc)
    pos = sel[..., None] * SLC_BLOCK + jnp.arange(SLC_BLOCK)
    bi = jnp.arange(Bd)[:, None, None, None, None]
    hi = jnp.arange(N_KV)[None, :, None, None, None]
    phys = page_table[bi, jnp.minimum(pos // PAGE_SIZE, n_pages - 1)]
    off = pos % PAGE_SIZE
    from_past = (pos < PAST_LEN)[..., None]
    new_i = jnp.clip(pos - PAST_LEN, 0, T - 1)
    kb = jnp.where(from_past, cache_kv[layer, phys, off, 2, hi], ks[bi, new_i, hi])
    vb = jnp.where(from_past, cache_kv[layer, phys, off, 3, hi], vs[bi, new_i, hi])
    o_slc = sel_attend(qr, kb, vb, pos, tq)
    win_buf = win_prev.shape[1]
    buf = jnp.concatenate([win_prev, jnp.stack([kw, vw], axis=2)], axis=1)
    kpos = PAST_LEN - win_buf + jnp.arange(win_buf + T)
    o_win = win_attend(qr, buf[:, :, 0], buf[:, :, 1], tq, kpos)
    rows = jnp.stack([kc, vc, ks, vs], axis=2)
    return combine(g, o_cmp, o_slc, o_win), rows, buf[:, -win_buf:]


def setup_inputs(seed: int = 0) -> dict:
    key = jax.random.key(seed)
    ks = jax.random.split(key, 17)
    n_pages = PAST_LEN // PAGE_SIZE
    n_used = DEC_BATCH * n_pages
    n_phys = n_used + max(1, n_used // 4)
    win_buf = min(WINDOW, PAST_LEN)

    def nrm(k, shape, scale):
        return jax.random.normal(k, shape, jnp.float32) * scale

    perm = jax.random.permutation(ks[4], n_phys)
    return {
        'x_prompt': nrm(ks[0], (BATCH, SEQ, D_MODEL), 1.0),
        'x_sample': nrm(ks[1], (DEC_BATCH, DEC_SEQ, D_MODEL), 1.0),
        'cache_kv': nrm(ks[2], (DEPTH, n_phys, PAGE_SIZE, 4, N_KV, HEAD_DIM), 1.0),
        'state_win': nrm(ks[3], (DEPTH, DEC_BATCH, win_buf, 2, N_KV, HEAD_DIM), 1.0),
        'page_table': perm[:n_used].reshape(DEC_BATCH, n_pages).astype(jnp.int32),
        'norm_g': 1.0 + nrm(ks[5], (DEPTH, D_MODEL), 0.02),
        'w_in': nrm(ks[6], (DEPTH, D_MODEL, D_IN), D_MODEL ** -0.5),
        'ln_g': 1.0 + nrm(ks[7], (DEPTH, D_A), 0.02),
        'ln_b': nrm(ks[8], (DEPTH, D_A), 0.02),
        'w_s': nrm(ks[9], (DEPTH, A_GROUPS, CHUNK, CHUNK), CHUNK ** -0.5),
        'b_s': 1.0 + nrm(ks[10], (DEPTH, A_GROUPS, CHUNK), 0.1),
        'cmp_pos': nrm(ks[11], (DEPTH, 2, CMP_BLOCK, HEAD_DIM), 0.1),
        'w_cmp1': nrm(ks[12], (DEPTH, 2, CMP_BLOCK * HEAD_DIM, CMP_HIDDEN), (CMP_BLOCK * HEAD_DIM) ** -0.5),
        'b_cmp1': nrm(ks[13], (DEPTH, 2, CMP_HIDDEN), 0.02),
        'w_cmp2': nrm(ks[14], (DEPTH, 2, CMP_HIDDEN, HEAD_DIM), CMP_HIDDEN ** -0.5),
        'w_out': nrm(ks[15], (DEPTH, D_MIX, D_MODEL), D_MIX ** -0.5),
        'final_g': 1.0 + nrm(ks[16], (D_MODEL,), 0.02),
    }


def reference(x_prompt, x_sample, cache_kv, state_win, page_table, norm_g, w_in, ln_g, ln_b,
              w_s, b_s, cmp_pos, w_cmp1, b_cmp1, w_cmp2, w_out, final_g):
    win_buf = min(WINDOW, PAST_LEN)
    x_p, x_s = x_prompt, x_sample
    kv_p, win_p, kv_s, win_s, v_s = [], [], [], [], []
    for l in range(DEPTH):
        u, v, z_a, q, kv, g, z_b = project(x_p, norm_g[l], w_in[l])
        a_out, _ = chunk_gmlp(u, v, ln_g[l], ln_b[l], w_s[l], b_s[l])
        b_out, rows, win = nsa_prompt(q, kv, g, cmp_pos[l], w_cmp1[l], b_cmp1[l], w_cmp2[l], win_buf)
        x_p = x_p + merge(a_out, z_a, b_out, z_b, w_out[l])
        kv_p.append(rows)
        win_p.append(win)
        u, v, z_a, q, kv, g, z_b = project(x_s, norm_g[l], w_in[l])
        a_out, v_rows = chunk_gmlp(u, v, ln_g[l], ln_b[l], w_s[l], b_s[l])
        b_out, rows, win = nsa_sample(q, kv, g, cache_kv, l, page_table, state_win[l],
                                      cmp_pos[l], w_cmp1[l], b_cmp1[l], w_cmp2[l])
        x_s = x_s + merge(a_out, z_a, b_out, z_b, w_out[l])
        kv_s.append(rows)
        win_s.append(win)
        v_s.append(v_rows)
    y_prompt = rmsnorm(x_p, final_g)
    y_sample = rmsnorm(x_s, final_g)
    return (y_prompt, y_sample, jnp.stack(kv_p), jnp.stack(win_p), jnp.stack(kv_s), jnp.stack(win_s), jnp.stack(v_s))
```

```python
from contextlib import ExitStack
import os
import numpy as np
import concourse.bass as bass
import concourse.mybir as mybir
from concourse.bass_utils import run_bass_kernel_spmd

F32 = mybir.dt.float32
BF16 = mybir.dt.bfloat16
I32 = mybir.dt.int32
AF = mybir.ActivationFunctionType
ALU = mybir.AluOpType
AX = mybir.AxisListType
NDMASEM = 40
NSWSEM = 4

D_MODEL = 1024
SEQ = 2048
NT = 16
D_IN = 3352
PAST = 16384
EPS = 1e-6
SCALE = 0.125
NEGM = -30000.0
FORCE = 1e9
NEG = -1e30
N_PHYS = 5120
CHUNKS = [(0, 512), (512, 1024), (1024, 1536), (1536, 2048), (2048, 2560), (2560, 2840), (2840, 3352)]


class Sched:
    def __init__(self, nc):
        self.nc = nc
        self.eng = {'pe': nc.tensor, 'act': nc.scalar, 'dve': nc.vector, 'pool': nc.gpsimd, 'sp': nc.sync}
        self.sem = {k: nc.alloc_semaphore('sem_' + k) for k in self.eng}
        self.cnt = {k: 0 for k in self.eng}
        self.waited = {k: {} for k in self.eng}
        self.lastw = {}
        self.readers = {}
        self.dma_sems = [nc.alloc_semaphore('dq%d' % i) for i in range(NDMASEM)]
        self.dma_val = [0] * NDMASEM
        self.dma_next = 0
        self.sw_next = 0

    def _wait(self, e, ev):
        sem, val, name = ev
        w = self.waited[e]
        if w.get(name, 0) >= val:
            return
        self.eng[e].wait_ge(sem, val)
        w[name] = val

    def _deps(self, e, reads, writes):
        best = {}

        def add(ev):
            if ev[2] not in best or best[ev[2]][1] < ev[1]:
                best[ev[2]] = ev
        for k in reads:
            if k in self.lastw:
                add(self.lastw[k])
        for k in writes:
            if k in self.lastw:
                add(self.lastw[k])
            for ev in self.readers.get(k, {}).values():
                add(ev)
        for name, ev in best.items():
            if name == 'pe' and e == 'pe':
                continue
            self._wait(e, ev)

    def _record(self, ev, reads, writes):
        for k in reads:
            d = self.readers.setdefault(k, {})
            d[ev[2]] = ev
        for k in writes:
            self.lastw[k] = ev
            self.readers[k] = {}

    def op(self, e, fn, reads=(), writes=()):
        self._deps(e, reads, writes)
        inst = fn(self.eng[e])
        self.cnt[e] += 1
        inst.then_inc(self.sem[e], 1)
        self._record((self.sem[e], self.cnt[e], e), reads, writes)

    def dma(self, e, out, in_, reads=(), writes=(), fn=None):
        self._deps(e, reads, writes)
        if e == 'pool':
            i = NDMASEM - NSWSEM + self.sw_next
            self.sw_next = (self.sw_next + 1) % NSWSEM
        else:
            i = self.dma_next
            self.dma_next = (i + 1) % (NDMASEM - NSWSEM)
        sem = self.dma_sems[i]
        name = 'dq%d' % i
        if self.dma_val[i] > 0:
            self._wait(e, (sem, self.dma_val[i], name))
        if fn is None:
            inst = self.eng[e].dma_start(out=out, in_=in_)
        else:
            inst = fn(self.eng[e])
        self.dma_val[i] += 16
        inst.then_inc(sem, 16)
        self._record((sem, self.dma_val[i], name), reads, writes)

    def barrier(self):
        evs = [(self.sem[k], self.cnt[k], k) for k in self.eng if self.cnt[k] > 0]
        evs += [(self.dma_sems[i], self.dma_val[i], 'dq%d' % i) for i in range(NDMASEM) if self.dma_val[i] > 0]
        for e in self.eng:
            for ev in evs:
                if ev[2] != e:
                    self._wait(e, ev)

    def finish(self):
        for i in range(NDMASEM):
            if self.dma_val[i] > 0:
                self._wait('sp', (self.dma_sems[i], self.dma_val[i], 'dq%d' % i))
        for k in self.eng:
            if k != 'sp' and self.cnt[k] > 0:
                self._wait('sp', (self.sem[k], self.cnt[k], k))


def _rope_tab(pos):
    half = 32
    inv = (10000.0 ** (-np.arange(half, dtype=np.float64) / half)).astype(np.float32)
    ang = pos.astype(np.float32)[:, None] * inv[None, :]
    cos = np.cos(ang.astype(np.float64)).astype(np.float32)
    sin = np.sin(ang.astype(np.float64)).astype(np.float32)
    return np.stack([np.concatenate([cos, cos], -1), np.concatenate([-sin, sin], -1)], 1).astype(np.float32)


def _host_consts():
    c = {}
    c['ropeP'] = _rope_tab(np.arange(SEQ))
    c['ropeS'] = _rope_tab(PAST + np.tile(np.arange(8), 4))
    k = np.arange(128)[:, None]
    t = np.arange(128)[None, :]
    cm = np.zeros((128, 2, 128), np.float32)
    cm[:, 0] = (k <= t)
    cm[:, 1] = (k > t)
    c['cmask'] = cm
    n = np.arange(128)[:, None, None]
    i = np.arange(NT)[None, :, None]
    tq = np.arange(128)[None, None, :]
    c['m01c'] = ((16 * n + 31 <= 128 * i + tq) & (n < 127)).astype(np.float32)
    blk = np.arange(32)[:, None]
    key = np.arange(SEQ)[None, :]
    c['Ep'] = (key // 64 == blk).astype(np.float32)
    nn = np.arange(128)[:, None]
    j = np.arange(32)[None, :]
    c['ovP'] = ((16 * nn <= 64 * j + 63) & (16 * nn + 31 >= 64 * j) & (nn < 127)).astype(np.float32)
    tt = (128 * np.arange(8, 16)[None, :, None] + np.arange(128)[:, None, None])
    jj = np.arange(32)[None, None, :]
    tb = tt // 64
    forced = (jj == 0) | (jj == tb) | (jj == tb - 1)
    valid = jj * 64 <= tt
    c['vmP'] = (valid & ~forced).astype(np.float32)
    c['acP'] = np.where(forced, FORCE, np.where(valid, 0.0, NEG)).astype(np.float32)
    p = np.arange(128)[:, None]
    x = np.arange(4096)[None, :]
    c['Amat'] = (x // 32 == p).astype(np.float32)
    npr = (np.arange(8)[None, :, None] * 128 + np.arange(128)[:, None, None])
    ns = npr - 1
    js = np.arange(257)[None, None, :]
    c['ovS'] = ((16 * ns <= 64 * js + 63) & (16 * ns + 31 >= 64 * js) & (ns >= 0) & (ns <= 1022)).astype(np.float32)
    q = np.arange(64)
    hk, tqq = q // 32, q % 8
    c['Gm'] = ((hk[:, None] == hk[None, :]) & (tqq[:, None] == tqq[None, :])).astype(np.float32)
    fj = np.zeros(257, bool)
    fj[[0, 255, 256]] = True
    c['vmS'] = np.broadcast_to((~fj).astype(np.float32)[None, :], (64, 257)).copy()
    c['acS'] = np.broadcast_to(np.where(fj, FORCE, 0.0).astype(np.float32)[None, :], (64, 257)).copy()
    kp = np.arange(32)
    bp, tp = kp // 8, kp % 8
    mn = np.zeros((32, 4, 64), np.float32)
    for b in range(4):
        mn[:, b, :] = ((bp[:, None] == b) & (tp[:, None] <= tqq[None, :]))
    c['mnew'] = mn
    c['mwin0'] = (np.arange(128)[:, None] > tqq[None, :]).astype(np.float32)
    nf = np.ones((128, 1), np.float32)
    nf[0, 0] = 0.0
    c['notfirst'] = nf
    c['pmod'] = ((np.arange(128) % 64) * 2).astype(np.float32)[:, None]
    return c


def build(do_prompt_attn=True, do_sample_attn=True):
    nc = bass.Bass("TRN2", target_bir_lowering=False)
    S = Sched(nc)
    consts = _host_consts()

    def din(name, shape, dt=F32):
        return nc.dram_tensor(name, list(shape), dt, kind="ExternalInput").ap()

    def dout(name, shape, dt=F32):
        return nc.dram_tensor(name, list(shape), dt, kind="ExternalOutput").ap()

    def sb(name, shape, dt=F32):
        return nc.alloc_sbuf_tensor(name, list(shape), dt)

    xp = din('xp', [SEQ, D_MODEL])
    xs = din('xs', [32, D_MODEL])
    cacheH = [din('cache%d' % h_, [N_PHYS * 128, 256]) for h_ in range(2)]
    swin = din('swin', [4, 512, 256])
    ptl = din('ptl', [128, 256], I32)
    gT = din('gT', [128, 8])
    w_in = din('w_in', [D_MODEL, D_IN])
    w_out = din('w_out', [D_MODEL, D_MODEL])
    lngb = din('lngb', [128, 2, 512])
    fgb = din('fgb', [128, D_MODEL])
    w_s = din('w_s', [8, 128, 128])
    bsT = din('bsT', [128, 8])
    bsS = din('bsS', [32, 8])
    peT = din('peT', [64, 2, 32])
    w1 = din('w1', [2, 2048, 128])
    b1T = din('b1T', [128, 2])
    w2 = din('w2', [2, 128, 64])
    cd = {k: din('c_' + k, v.shape) for k, v in consts.items()}

    yp = dout('yp', [SEQ, D_MODEL])
    ys = dout('ys', [32, D_MODEL])
    kvp = dout('kvp', [SEQ, 512])
    winp = dout('winp', [512, 256])
    kvs = dout('kvs', [32, 512])
    wins = dout('wins', [4, 512, 256])
    vso = dout('vso', [32, 512])

    PW = [nc.alloc_psum_tensor('PW%d' % i, [128, 512], F32) for i in range(2)]
    TB = [nc.alloc_psum_tensor('TB%d' % i, [128, 1024], BF16) for i in range(2)]
    POc = nc.alloc_psum_tensor('POc', [128, 512], F32)
    POs = nc.alloc_psum_tensor('POs', [128, 512], F32)
    POw = nc.alloc_psum_tensor('POw', [128, 512], F32)
    PM = nc.alloc_psum_tensor('PM', [128, 512], F32)
    pwi = [0]
    tbi = [0]

    def next_pw():
        pwi[0] ^= 1
        return PW[pwi[0]], 'PW%d' % pwi[0]

    def next_tb():
        tbi[0] ^= 1
        return TB[tbi[0]], 'TB%d' % tbi[0]

    def mm(out, lhsT, rhs, start, stop, reads, writes):
        S.op('pe', lambda e: e.matmul(out, lhsT=lhsT, rhs=rhs, start=start, stop=stop, skip_group_check=True),
             reads=reads, writes=writes)

    def tr(out, in_, ident, reads, writes):
        S.op('pe', lambda e: e.transpose(out=out, in_=in_, identity=ident), reads=reads, writes=writes)

    def act(out, in_, func, reads, writes, **kw):
        S.op('act', lambda e: e.activation(out=out, in_=in_, func=func, **kw), reads=reads, writes=writes)

    def tt(eng, out, in0, in1, op, reads, writes):
        S.op(eng, lambda e: e.tensor_tensor(out=out, in0=in0, in1=in1, op=op), reads=reads, writes=writes)

    def ts(eng, out, in0, s1, s2, op0, op1, reads, writes):
        if op1 is None:
            S.op(eng, lambda e: e.tensor_scalar(out=out, in0=in0, scalar1=s1, scalar2=None, op0=op0), reads=reads, writes=writes)
        else:
            S.op(eng, lambda e: e.tensor_scalar(out=out, in0=in0, scalar1=s1, scalar2=s2, op0=op0, op1=op1),
                 reads=reads, writes=writes)

    def cp(eng, out, in_, reads, writes):
        if eng == 'act':
            act(out, in_, AF.Copy, reads, writes)
        elif out.dtype == in_.dtype:
            S.op(eng, lambda e: e.tensor_scalar(out=out, in0=in_, scalar1=1.0, scalar2=None, op0=ALU.mult),
                 reads=reads, writes=writes)
        else:
            S.op(eng, lambda e: e.tensor_copy(out=out, in_=in_), reads=reads, writes=writes)

    idf = sb('idf', [128, 128])
    idb = sb('idb', [128, 128], BF16)
    w_in_bf = sb('w_in_bf', [128, 8, D_IN], BF16)
    w_out_bf = sb('w_out_bf', [128, 8, D_MODEL], BF16)
    gT_sb = sb('gT_sb', [128, 8])
    lngb_sb = sb('lngb_sb', [128, 2, 512])
    fg_sb = sb('fg_sb', [128, D_MODEL])
    wsT_bf = sb('wsT_bf', [128, 8, 128], BF16)
    wsST_bf = sb('wsST_bf', [32, 8, 32], BF16)
    bsT_sb = sb('bsT_sb', [128, 8])
    bsS_sb = sb('bsS_sb', [32, 8])
    w2_bf = sb('w2_bf', [128, 2, 64], BF16)
    cb = sb('cb', [128, 2])
    b1T_sb = sb('b1T_sb', [128, 2])
    peT_bf = sb('peT_bf', [64, 2, 32], BF16)
    xt = [sb('xt%d' % i, [128, D_MODEL]) for i in range(2)]
    ropts = [sb('ropt%d' % i, [128, 2, 64]) for i in range(2)]
    ss = sb('ss', [128, 1])
    hn = sb('hn', [128, D_MODEL], BF16)
    junk = hn
    hT = sb('hT', [128, 8, 128], BF16)
    u_g = sb('u_g', [128, 512])
    v_g = sb('v_g', [128, 512])
    v_ln = sb('v_ln', [128, 512], BF16)
    sza = sb('sza', [128, 512])
    q_f = sb('q_f', [128, 512])
    kv_f = sb('kv_f', [128, 768])
    gates = sb('gates', [128, 24])
    szb = sb('szb', [128, 512])
    st6 = sb('st6', [128, 6])
    mv = sb('mv', [128, 2])
    rs = sb('rs', [128, 1])
    a_f = sb('a_f', [128, 512])
    merged = sb('merged', [128, D_MODEL], BF16)
    mT = sb('mT', [128, 8, 128], BF16)
    rt1 = sb('rt1', [128, 512])
    rt2 = sb('rt2', [128, 512])
    qr_bf = sb('qr_bf', [128, 8, 64], BF16)
    q_bf = sb('q_bf', [128, 8, 64], BF16)
    kvb = sb('kvb', [128, 768], BF16)
    bo_f = sb('bo_f', [128, 512])
    tmpo = sb('tmpo', [128, 4, 64])

    S.op('pool', lambda e: e.memset(idf[:], 1.0), writes=['idf'])
    S.op('pool', lambda e: e.affine_select(out=idf[:], in_=idf[:], pattern=[[-1, 128]], base=0, channel_multiplier=1,
                                           compare_op=ALU.is_equal, fill=0.0), reads=['idf'], writes=['idf'])
    cp('dve', idb[:], idf[:], ['idf'], ['idb'])
    S.dma('sp', gT_sb[:], gT[:, :], writes=['gT'])
    S.dma('sp', lngb_sb[:], lngb[:, :, :], writes=['lngb'])
    S.dma('sp', fg_sb[:], fgb[:, :], writes=['fg'])
    S.dma('sp', bsT_sb[:], bsT[:, :], writes=['bsT'])
    S.dma('sp', bsS_sb[:], bsS[:, :], writes=['bsS'])
    S.dma('sp', b1T_sb[:], b1T[:, :], writes=['b1T'])

    setup_stack = ExitStack()
    stage = [setup_stack.enter_context(nc.sbuf_tensor('stage%d' % i, [128, 1024], F32)) for i in range(2)]
    sti = [0]

    def next_stage():
        i = sti[0]
        sti[0] ^= 1
        return stage[i], 'stage%d' % i

    def load_cast(dst_ap, src_ap, P, W, dkey, scale_ap=None, eng='dve'):
        st, sk = next_stage()
        S.dma('sp', st[0:P, 0:W], src_ap, writes=[sk])
        if scale_ap is not None:
            ts(eng, dst_ap, st[0:P, 0:W], scale_ap, None, ALU.mult, None, [sk, 'gT'], [dkey])
        else:
            cp(eng, dst_ap, st[0:P, 0:W], [sk], [dkey])

    for kc in range(8):
        for hf in range(4):
            c0 = hf * 838
            load_cast(w_in_bf[:, kc, c0:c0 + 838], w_in[kc * 128:(kc + 1) * 128, c0:c0 + 838], 128, 838, 'w_in_bf',
                      scale_ap=gT_sb[:, kc:kc + 1], eng=('dve' if hf % 2 == 0 else 'pool'))
    for kc in range(8):
        load_cast(w_out_bf[:, kc, :], w_out[kc * 128:(kc + 1) * 128, :], 128, 1024, 'w_out_bf',
                  eng=('dve' if kc % 2 == 0 else 'pool'))
    st, sk = next_stage()
    S.dma('sp', st[:, 0:128].rearrange("p (c d) -> p c d", c=2), w2.rearrange("c k d -> k c d"), writes=[sk])
    cp('dve', w2_bf[:].rearrange("p c d -> p (c d)"), st[:, 0:128], [sk], ['w2_bf'])
    st, sk = next_stage()
    S.dma('sp', st[0:64, 0:64].rearrange("p (c r) -> p c r", c=2), peT[:, :, :], writes=[sk])
    cp('dve', peT_bf[:].rearrange("p c r -> p (c r)"), st[0:64, 0:64], [sk], ['peT'])
    st, sk = next_stage()
    wsv = st[:, 0:1024].rearrange("p (g s) -> p g s", g=8)
    S.dma('sp', wsv, w_s.rearrange("g t s -> t g s"), writes=[sk])
    S.op('pool', lambda e: e.affine_select(out=wsv, in_=wsv, pattern=[[0, 8], [-1, 128]], base=0, channel_multiplier=1,
                                           compare_op=ALU.is_ge, fill=0.0), reads=[sk], writes=[sk])
    cp('dve', junk[:, 0:1024], st[:, 0:1024], [sk], ['hn'])
    tb, tk = next_tb()
    for g in range(8):
        tr(tb[:, g * 128:(g + 1) * 128], junk[:, g * 128:(g + 1) * 128], idb[:], ['hn', 'idb'], [tk])
    cp('dve', wsT_bf[:].rearrange("p g t -> p (g t)"), tb[:, :], [tk], ['wsT'])
    st, sk = next_stage()
    wss = st[0:32, 0:256].rearrange("p (g s) -> p g s", g=8)
    S.op('pool', lambda e: e.memset(st[0:32, 0:256], 0.0), writes=[sk])
    with nc.allow_non_contiguous_dma(reason="tiny 8x8 blocks of w_s"):
        for b in range(4):
            S.dma('sp', wss[8 * b:8 * b + 8, :, 8 * b:8 * b + 8], w_s[:, 0:8, 0:8].rearrange("g t s -> t g s"),
                  reads=[sk], writes=[sk + 'b%d' % b])
    S.op('pool', lambda e: e.affine_select(out=wss, in_=wss, pattern=[[0, 8], [-1, 32]], base=0, channel_multiplier=1,
                                           compare_op=ALU.is_ge, fill=0.0), reads=[sk] + [sk + 'b%d' % b for b in range(4)],
         writes=[sk])
    cp('dve', junk[0:32, 0:256], st[0:32, 0:256], [sk], ['hn'])
    tb, tk = next_tb()
    for g in range(8):
        tr(tb[0:32, g * 32:(g + 1) * 32], junk[0:32, g * 32:(g + 1) * 32], idb[0:32, 0:32], ['hn', 'idb'], [tk])
    cp('dve', wsST_bf[:].rearrange("p g t -> p (g t)"), tb[0:32, 0:256], [tk], ['wsST'])

    def const_bf(stack, name):
        shape = list(consts[name].shape)
        t = stack.enter_context(nc.sbuf_tensor('k_' + name, shape, BF16))
        P, W = shape[0], int(np.prod(shape[1:]))
        if len(shape) == 2:
            flat, src = t[:], cd[name]
        else:
            flat, src = t[:].rearrange("p a b -> p (a b)"), cd[name].rearrange("p a b -> p (a b)")
        for c0 in range(0, W, 1024):
            wd = min(1024, W - c0)
            load_cast(flat[:, c0:c0 + wd], src[:, c0:c0 + wd], P, wd, 'k_' + name)
        return t

    def const_f32(stack, name):
        shape = list(consts[name].shape)
        t = stack.enter_context(nc.sbuf_tensor('k_' + name, shape, F32))
        S.dma('sp', t[:], cd[name], writes=['k_' + name])
        return t

    def token_front(xb, kx, P, ws_bf, bs_sb, ropt, ropekey):
        act(junk[0:P, :], xb[0:P, :], AF.Square, [kx], ['hn', 'ss'], accum_out=ss[0:P, :])
        ts('dve', ss[0:P], ss[0:P], 1.0 / D_MODEL, EPS, ALU.mult, ALU.add, ['ss'], ['ss'])
        act(ss[0:P], ss[0:P], AF.Sqrt, ['ss'], ['ss'])
        S.op('dve', lambda e: e.reciprocal(out=ss[0:P], in_=ss[0:P]), reads=['ss'], writes=['ss'])
        ts('dve', hn[0:P], xb[0:P], ss[0:P, 0:1], None, ALU.mult, None, [kx, 'ss'], ['hn'])
        tb, tk = next_tb()
        for kc in range(8):
            tr(tb[:, kc * 128:kc * 128 + P], hn[0:P, kc * 128:(kc + 1) * 128], idb[0:P, 0:P], ['hn', 'idb'], [tk])
        cp('dve', hT[:, :, 0:P], tb[:, :].rearrange("p (k t) -> p k t", k=8)[:, :, 0:P], [tk], ['hT'])
        for ci, (c0, c1) in enumerate(CHUNKS):
            w = c1 - c0
            pw, pk = next_pw()
            for kc in range(8):
                mm(pw[0:P, 0:w], hT[:, kc, 0:P], w_in_bf[:, kc, c0:c1], kc == 0, kc == 7, ['hT', 'w_in_bf'], [pk])
            if ci == 0:
                act(u_g[0:P], pw[0:P, :], AF.Gelu_apprx_tanh, [pk], ['u_g'])
            elif ci == 1:
                act(v_g[0:P], pw[0:P, :], AF.Gelu_apprx_tanh, [pk], ['v_g'])
                S.op('dve', lambda e: e.bn_stats(out=st6[0:P], in_=v_g[0:P]), reads=['v_g'], writes=['st6'])
                S.op('dve', lambda e: e.bn_aggr(out=mv[0:P], in_=st6[0:P]), reads=['st6'], writes=['mv'])
                ts('dve', rs[0:P], mv[0:P, 1:2], EPS, None, ALU.add, None, ['mv'], ['rs'])
                act(rs[0:P], rs[0:P], AF.Sqrt, ['rs'], ['rs'])
                S.op('dve', lambda e: e.reciprocal(out=rs[0:P], in_=rs[0:P]), reads=['rs'], writes=['rs'])
                ts('dve', v_g[0:P], v_g[0:P], mv[0:P, 0:1], rs[0:P, 0:1], ALU.subtract, ALU.mult, ['v_g', 'mv', 'rs'], ['v_g'])
                tt('dve', v_g[0:P], v_g[0:P], lngb_sb[0:P, 0, :], ALU.mult, ['v_g', 'lngb'], ['v_g'])
                tt('dve', v_g[0:P], v_g[0:P], lngb_sb[0:P, 1, :], ALU.add, ['v_g', 'lngb'], ['v_g'])
                cp('act', v_ln[0:P], v_g[0:P], ['v_g'], ['v_ln'])
            elif ci == 2:
                act(sza[0:P], pw[0:P, :], AF.Silu, [pk], ['sza'])
            elif ci == 3:
                cp('act', q_f[0:P], pw[0:P, :], [pk], ['q_f'])
            elif ci == 4:
                cp('act', kv_f[0:P, 0:512], pw[0:P, :], [pk], ['kv_f'])
            elif ci == 5:
                cp('act', kv_f[0:P, 512:768], pw[0:P, 0:256], [pk], ['kv_f'])
                act(gates[0:P], pw[0:P, 256:280], AF.Sigmoid, [pk], ['gates'])
            else:
                act(szb[0:P], pw[0:P, :], AF.Silu, [pk], ['szb'])
        pw, pk = next_pw()
        for g in range(8):
            mm(pw[0:P, g * 64:(g + 1) * 64], ws_bf[0:P, g, 0:P], v_ln[0:P, g * 64:(g + 1) * 64], g == 0, g == 7,
               ['v_ln', 'wsT', 'wsST'], [pk])
        tt('dve', a_f[0:P].rearrange("p (g d) -> p g d", g=8), pw[0:P, :].rearrange("p (g d) -> p g d", g=8),
           bs_sb[0:P, :].unsqueeze(2).to_broadcast([P, 8, 64]), ALU.add, [pk, 'bsT', 'bsS'], ['a_f'])
        tt('dve', a_f[0:P], a_f[0:P], u_g[0:P], ALU.mult, ['a_f', 'u_g'], ['a_f'])
        tt('dve', merged[0:P, 0:512], a_f[0:P], sza[0:P], ALU.mult, ['a_f', 'sza'], ['merged_a'])
        cos_b = lambda H: ropt[0:P, 0, :].unsqueeze(1).to_broadcast([P, H, 64])
        sin_lo = lambda H: ropt[0:P, 1, 0:32].unsqueeze(1).to_broadcast([P, H, 32])
        sin_hi = lambda H: ropt[0:P, 1, 32:64].unsqueeze(1).to_broadcast([P, H, 32])

        def rope(src, dst, H, skey, dkey):
            t1 = rt1[0:P, 0:H * 64].rearrange("p (h d) -> p h d", h=H)
            t2 = rt2[0:P, 0:H * 64].rearrange("p (h d) -> p h d", h=H)
            tt('pool', t1, src, cos_b(H), ALU.mult, [skey, ropekey], ['rt1'])
            tt('pool', t2[:, :, 0:32], src[:, :, 32:64], sin_lo(H), ALU.mult, [skey, ropekey], ['rt2a'])
            tt('pool', t2[:, :, 32:64], src[:, :, 0:32], sin_hi(H), ALU.mult, [skey, ropekey], ['rt2b'])
            tt('pool', dst, t1, t2, ALU.add, ['rt1', 'rt2a', 'rt2b'], [dkey])

        rope(q_f[0:P].rearrange("p (h d) -> p h d", h=8), qr_bf[0:P], 8, 'q_f', 'qr_bf')
        cp('act', q_bf[0:P].rearrange("p h d -> p (h d)"), q_f[0:P], ['q_f'], ['q_bf'])
        ksv = kv_f[0:P, 256:384].rearrange("p (h d) -> p h d", h=2)
        kwv = kv_f[0:P, 512:640].rearrange("p (h d) -> p h d", h=2)
        rope(ksv, ksv, 2, 'kv_f', 'kv_f')
        rope(kwv, kwv, 2, 'kv_f', 'kv_f')
        cp('act', kvb[0:P], kv_f[0:P], ['kv_f'], ['kvb'])

    def token_back(xb, kx, P, ydst):
        tb, tk = next_tb()
        for kc in range(8):
            tr(tb[:, kc * 128:kc * 128 + P], merged[0:P, kc * 128:(kc + 1) * 128], idb[0:P, 0:P],
               ['merged_a', 'merged_b', 'idb'], [tk])
        cp('dve', mT[:, :, 0:P], tb[:, :].rearrange("p (k t) -> p k t", k=8)[:, :, 0:P], [tk], ['mT'])
        for half in range(2):
            pw, pk = next_pw()
            for kc in range(8):
                mm(pw[0:P, :], mT[:, kc, 0:P], w_out_bf[:, kc, half * 512:(half + 1) * 512], kc == 0, kc == 7,
                   ['mT', 'w_out_bf'], [pk])
            tt('dve', xb[0:P, half * 512:(half + 1) * 512], xb[0:P, half * 512:(half + 1) * 512], pw[0:P, :], ALU.add,
               [kx, pk], [kx])
        act(junk[0:P, :], xb[0:P, :], AF.Square, [kx], ['hn', 'ss'], accum_out=ss[0:P, :])
        ts('dve', ss[0:P], ss[0:P], 1.0 / D_MODEL, EPS, ALU.mult, ALU.add, ['ss'], ['ss'])
        act(ss[0:P], ss[0:P], AF.Sqrt, ['ss'], ['ss'])
        S.op('dve', lambda e: e.reciprocal(out=ss[0:P], in_=ss[0:P]), reads=['ss'], writes=['ss'])
        S.op('dve', lambda e: e.scalar_tensor_tensor(out=xb[0:P], in0=xb[0:P], scalar=ss[0:P, 0:1], in1=fg_sb[0:P],
                                                     op0=ALU.mult, op1=ALU.mult), reads=[kx, 'ss', 'fg'], writes=[kx])
        S.dma('sp', ydst, xb[0:P, :], reads=[kx])

    pst = ExitStack()

    def psb(name, shape, dt=F32):
        return pst.enter_context(nc.sbuf_tensor(name, list(shape), dt))

    w1p = psb('w1p', [64, 2, 32, 128], BF16)
    for c in range(2):
        w1v = w1[c].rearrange("(rs d) k -> d rs k", d=64)
        for r8 in range(4):
            load_cast(w1p[:, c, r8 * 8:(r8 + 1) * 8, :].rearrange("p a k -> p (a k)"),
                      w1v[:, r8 * 8:(r8 + 1) * 8, :], 64, 1024, 'w1p')
    for c in range(2):
        for rs_ in range(32):
            mm(PM[:, c:c + 1], w1p[:, c, rs_, :], peT_bf[:, c, rs_:rs_ + 1], (c == 0 and rs_ == 0), (c == 1 and rs_ == 31),
               ['w1p', 'peT'], ['PM'])
    tt('dve', cb[:], PM[:, 0:2], b1T_sb[:], ALU.add, ['PM', 'b1T'], ['cb'])

    KsT = [psb('KsT%d' % h, [64, SEQ], BF16) for h in range(2)]
    KwT = [psb('KwT%d' % h, [64, SEQ], BF16) for h in range(2)]
    Vsaug = psb('Vsaug', [128, NT, 2, 66], BF16)
    Vwaug = psb('Vwaug', [128, NT, 2, 66], BF16)
    kcT = psb('kcT', [64, 2, 2, 144], BF16)
    hidnew = psb('hidnew', [128, 2, 2, 8], BF16)
    hidV = psb('hidV', [128, 2, 128], BF16)
    KcmpT = psb('KcmpT', [64, 2, 128], BF16)
    vcaug = psb('vcaug', [128, 2, 98], BF16)
    qrT = psb('qrT', [64, 8, 128], BF16)
    qT = psb('qT', [64, 8, 128], BF16)
    pT = [psb('pT%d' % i, [128, 512], BF16) for i in range(3)]
    negsel = psb('negsel', [128, 2, 32])
    negselT = psb('negselT', [32, 2, 4, 128], BF16)
    sc = psb('sc', [128, 2, 32])
    sc2 = psb('sc2', [128, 32])
    m16 = psb('m16', [128, 16])
    rden = psb('rden', [128, 4])
    wgt = psb('wgt', [128, 4])
    k_cmask = const_bf(pst, 'cmask')
    k_m01c = const_bf(pst, 'm01c')
    k_Ep = const_bf(pst, 'Ep')
    k_vmP = const_f32(pst, 'vmP')
    k_acP = const_f32(pst, 'acP')
    st, sk = next_stage()
    S.dma('sp', st[:, 0:32], cd['ovP'], writes=[sk])
    S.op('pool', lambda e: e.memset(vcaug[:], 0.0), writes=['vcaug'])
    S.op('pool', lambda e: e.memset(vcaug[:, :, 64:65], 1.0), reads=['vcaug'], writes=['vcaug1'])
    for h in range(2):
        cp('dve', vcaug[:, h, 65:97], st[:, 0:32], [sk, 'vcaug'], ['vcaug_ov%d' % h])
    S.op('pool', lambda e: e.memset(Vsaug[:], 1.0), writes=['Vsaug'])
    S.op('pool', lambda e: e.memset(Vwaug[:], 1.0), writes=['Vwaug'])
    S.op('pool', lambda e: e.memset(kcT[:], 0.0), writes=['kcT'])
    S.op('pool', lambda e: e.memset(hidV[:], 0.0), writes=['hidV'])
    S.op('pool', lambda e: e.memset(hidnew[:], 0.0), writes=['hidnew'])
    S.op('pool', lambda e: e.memset(KcmpT[:], 0.0), writes=['KcmpT'])
    pti = [0]

    def next_pT():
        pti[0] = (pti[0] + 1) % 3
        return pT[pti[0]], 'pT%d' % pti[0]

    def attend(i, hk, KT, Vaug, vkey, kkey, chunks, po, pok, masks, selmask):
        rhs_q = qrT[:, hk * 4:(hk + 1) * 4, :].rearrange("p g t -> p (g t)")
        for ci, kc in enumerate(chunks):
            pw, pk = next_pw()
            mm(pw[:, :], KT[hk][:, kc * 128:(kc + 1) * 128], rhs_q, True, not selmask, [kkey, 'qrT'], [pk])
            if selmask:
                mm(pw[:, :], k_Ep[:, kc * 128:(kc + 1) * 128], negselT[:, hk].rearrange("p g t -> p (g t)"), False, True,
                   ['k_Ep', 'negselT'], [pk])
            p_t, ptk = next_pT()
            act(p_t[:, :], pw[:, :], AF.Exp, [pk], [ptk], scale=SCALE)
            if kc in masks:
                mk = masks[kc]
                tt('dve', p_t[:, :].rearrange("p (g t) -> p g t", g=4), p_t[:, :].rearrange("p (g t) -> p g t", g=4),
                   k_cmask[:, mk, :].unsqueeze(1).to_broadcast([128, 4, 128]), ALU.mult, [ptk, 'k_cmask'], [ptk])
            for g in range(4):
                mm(po[:, g * 65:(g + 1) * 65], p_t[:, g * 128:(g + 1) * 128], Vaug[:, kc, hk, 0:65],
                   (ci == 0 and g == 0), (ci == len(chunks) - 1 and g == 3), [ptk, vkey], [pok])

    def branch_out(po, pok, hk, br, width, first):
        pv = po[:, 0:4 * width].rearrange("p (g w) -> p g w", g=4)
        ts('dve', rden[:], pv[:, :, 64], 1e-30, None, ALU.max, None, [pok], ['rden'])
        S.op('dve', lambda e: e.reciprocal(out=rden[:], in_=rden[:]), reads=['rden'], writes=['rden'])
        gv = gates[:, hk * 12:(hk + 1) * 12].rearrange("p (g c) -> p g c", c=3)[:, :, br]
        tt('dve', wgt[:], rden[:], gv, ALU.mult, ['rden', 'gates'], ['wgt'])
        dst = bo_f[:, hk * 256:(hk + 1) * 256].rearrange("p (g d) -> p g d", g=4)
        wb = wgt[:, :].unsqueeze(2).to_broadcast([128, 4, 64])
        if first:
            tt('dve', dst, pv[:, :, 0:64], wb, ALU.mult, [pok, 'wgt'], ['bo_f'])
        else:
            tt('dve', tmpo[:], pv[:, :, 0:64], wb, ALU.mult, [pok, 'wgt'], ['tmpo'])
            tt('dve', dst, dst, tmpo[:], ALU.add, ['bo_f', 'tmpo'], ['bo_f'])

    def prefetch(i):
        S.dma('sp', xt[i % 2][:], xp[128 * i:128 * (i + 1), :], writes=['xt%d' % (i % 2)])
        S.dma('sp', ropts[i % 2][:], cd['ropeP'][128 * i:128 * (i + 1), :, :], writes=['ropt%d' % (i % 2)])

    prefetch(0)
    for i in range(NT):
        xb, kx = xt[i % 2], 'xt%d' % (i % 2)
        if i + 1 < NT:
            prefetch(i + 1)
        token_front(xb, kx, 128, wsT_bf, bsT_sb, ropts[i % 2], 'ropt%d' % (i % 2))
        S.dma('sp', kvp[128 * i:128 * (i + 1), :], kv_f[:, 0:512], reads=['kv_f'])
        if i >= 12:
            S.dma('sp', winp[128 * (i - 12):128 * (i - 11), :], kv_f[:, 512:768], reads=['kv_f'])
        if do_prompt_attn:
            SUB = os.environ.get('KSUB', 'ABCD123')
            if 'A' in SUB:
                tb, tk = next_tb()
                for h in range(8):
                    tr(tb[0:64, h * 128:(h + 1) * 128], qr_bf[:, h, :], idb[:], ['qr_bf', 'idb'], [tk])
                cp('dve', qrT[:].rearrange("p h t -> p (h t)"), tb[0:64, :], [tk], ['qrT'])
                tb, tk = next_tb()
                for h in range(8):
                    tr(tb[0:64, h * 128:(h + 1) * 128], q_bf[:, h, :], idb[:], ['q_bf', 'idb'], [tk])
                cp('act', qT[:].rearrange("p h t -> p (h t)"), tb[0:64, :], [tk], ['qT'])
            if 'B' in SUB:
                tb, tk = next_tb()
                srcs = [256, 320, 512, 576, 0, 64, 128, 192]
                for j, c0 in enumerate(srcs):
                    tr(tb[0:64, j * 128:(j + 1) * 128], kvb[:, c0:c0 + 64], idb[:], ['kvb', 'idb'], [tk])
                for h in range(2 if '1' in SUB else 0):
                    cp('dve', KsT[h][:, 128 * i:128 * (i + 1)], tb[0:64, h * 128:(h + 1) * 128], [tk], ['KsT'])
                    cp('dve', KwT[h][:, 128 * i:128 * (i + 1)], tb[0:64, (2 + h) * 128:(3 + h) * 128], [tk], ['KwT'])
                if '2' in SUB:
                    cp('dve', kcT[:, :, :, 16:144], tb[0:64, 512:1024].rearrange("p (c h t) -> p c h t", c=2, h=2), [tk], ['kcT'])
                if '3' in SUB:
                    cp('act', Vsaug[:, i, :, 0:64], kvb[:, 384:512].rearrange("p (h d) -> p h d", h=2), ['kvb', 'Vsaug'], ['Vsaug'])
                    cp('act', Vwaug[:, i, :, 0:64], kvb[:, 640:768].rearrange("p (h d) -> p h d", h=2), ['kvb', 'Vwaug'], ['Vwaug'])
            m0 = 1 if i == 0 else 0
            nb = 8 - m0
            if 'C' in SUB:
                for c in range(2):
                    for hk in range(2):
                        grp = c * 2 + hk
                        for r in range(2):
                            for s_ in range(16):
                                st0 = 16 * m0 + 16 * r + s_
                                mm(PM[:, grp * 8 + m0:grp * 8 + 8], w1p[:, c, r * 16 + s_, :],
                                   kcT[:, c, hk, st0:st0 + 16 * (nb - 1) + 1:16], (r == 0 and s_ == 0), (r == 1 and s_ == 15),
                                   ['w1p', 'kcT'], ['PM'])
                for c in range(2):
                    act(hidnew[:, c, :, m0:8], PM[:, c * 16:(c + 1) * 16].rearrange("p (h m) -> p h m", h=2)[:, :, m0:8],
                        AF.Gelu_apprx_tanh, ['PM', 'cb'], ['hidnew'], bias=cb[:, c:c + 1])
            n0 = 8 * i - 1 + m0
            if 'D' in SUB:
                cp('dve', hidV[:, :, n0:n0 + nb], hidnew[:, 1, :, m0:8], ['hidnew', 'hidV'], ['hidV'])
                cp('dve', kcT[:, :, :, 0:16], kcT[:, :, :, 128:144], ['kcT'], ['kcT'])
                mm(PM[0:64, 32:48], w2_bf[:, 0, :], hidnew[:, 0, :, :].rearrange("p h m -> p (h m)"), True, True,
                   ['w2_bf', 'hidnew'], ['PM'])
                cp('dve', KcmpT[:, :, n0:n0 + nb], PM[0:64, 32:48].rearrange("p (h m) -> p h m", h=2)[:, :, m0:8], ['PM'], ['KcmpT'])
                for hk in range(2):
                    mm(PM[:, 64 + hk * 64:128 + hk * 64], hidV[:, hk, :], w2_bf[:, 1, :], hk == 0, hk == 1, ['hidV', 'w2_bf'], ['PM'])
                cp('dve', vcaug[:, :, 0:64], PM[:, 64:192].rearrange("p (h d) -> p h d", h=2), ['PM', 'vcaug'], ['vcaug'])
            lvl = int(do_prompt_attn)
            if lvl == 1:
                S.op('pool', lambda e: e.memset(bo_f[:], 0.0), writes=['bo_f'])
            for hk in range(2 if lvl >= 2 else 0):
                pw, pk = next_pw()
                mm(pw[:, :], KcmpT[:, hk, :], qT[:, hk * 4:(hk + 1) * 4, :].rearrange("p g t -> p (g t)"), True, True,
                   ['KcmpT', 'qT'], [pk])
                p_t, ptk = next_pT()
                act(p_t[:, :], pw[:, :], AF.Exp, [pk], [ptk], scale=SCALE)
                tt('dve', p_t[:, :].rearrange("p (g t) -> p g t", g=4), p_t[:, :].rearrange("p (g t) -> p g t", g=4),
                   k_m01c[:, i, :].unsqueeze(1).to_broadcast([128, 4, 128]), ALU.mult, [ptk, 'k_m01c'], [ptk])
                for g in range(4):
                    mm(POc[:, g * 97:(g + 1) * 97], p_t[:, g * 128:(g + 1) * 128], vcaug[:, hk, 0:97], g == 0, g == 3,
                       [ptk, 'vcaug', 'vcaug1', 'vcaug_ov%d' % hk], ['POc'])
                branch_out(POc, 'POc', hk, 0, 97, True)
                sel = i >= 8
                if sel:
                    pv = POc[:, 0:388].rearrange("p (g w) -> p g w", g=4)
                    tt('dve', tmpo[:, :, 0:32], pv[:, :, 65:97], rden[:, :].unsqueeze(2).to_broadcast([128, 4, 32]), ALU.mult,
                       ['POc', 'rden'], ['tmpo'])
                    S.op('dve', lambda e: e.tensor_reduce(out=sc2[:], in_=tmpo[:, :, 0:32].rearrange("p g j -> p j g"),
                                                          axis=AX.X, op=ALU.add), reads=['tmpo'], writes=['sc2'])
                    tt('dve', sc2[:], sc2[:], k_vmP[:, i - 8, :], ALU.mult, ['sc2', 'k_vmP'], ['sc2'])
                    tt('dve', sc2[:], sc2[:], k_acP[:, i - 8, :], ALU.add, ['sc2', 'k_acP'], ['sc2'])
                    S.op('dve', lambda e: e.max(out=m16[:, 0:8], in_=sc2[:]), reads=['sc2'], writes=['m16'])
                    S.op('dve', lambda e: e.match_replace(out=sc[:, 0, :], in_to_replace=m16[:, 0:8], in_values=sc2[:],
                                                          imm_value=-3.0e38), reads=['sc2', 'm16'], writes=['sc'])
                    S.op('dve', lambda e: e.max(out=m16[:, 8:16], in_=sc[:, 0, :]), reads=['sc'], writes=['m16'])
                    ts('dve', negsel[:, hk, :], sc2[:], m16[:, 15:16], NEGM, ALU.is_lt, ALU.mult, ['sc2', 'm16'], ['negsel'])
                    tr(PM[0:32, 256:384], negsel[:, hk, :], idf[:], ['negsel', 'idf'], ['PM'])
                    cp('dve', negselT[:, hk], PM[0:32, 256:384].unsqueeze(1).to_broadcast([32, 4, 128]), ['PM'], ['negselT'])
                if lvl >= 3:
                    attend(i, hk, KsT, Vsaug, 'Vsaug', 'KsT', list(range(i + 1)), POs, 'POs', {i: 0}, sel)
                    branch_out(POs, 'POs', hk, 1, 65, False)
                if lvl < 4:
                    continue
                wch = list(range(max(0, i - 4), i + 1))
                wm = {i: 0}
                if i >= 4:
                    wm[i - 4] = 1
                attend(i, hk, KwT, Vwaug, 'Vwaug', 'KwT', wch, POw, 'POw', wm, False)
                branch_out(POw, 'POw', hk, 2, 65, False)
            tt('dve', merged[:, 512:1024], bo_f[:], szb[:], ALU.mult, ['bo_f', 'szb'], ['merged_b'])
        else:
            S.op('pool', lambda e: e.memset(merged[:, 512:1024], 0.0), writes=['merged_b'])
        token_back(xb, kx, 128, yp[128 * i:128 * (i + 1), :])

    S.barrier()
    pst.close()

    sst = ExitStack()

    def ssb(name, shape, dt=F32):
        return sst.enter_context(nc.sbuf_tensor(name, list(shape), dt))

    xb, kx = xt[0], 'xt0'
    S.dma('sp', xb[0:32, :], xs[:, :], writes=[kx])
    S.dma('sp', ropts[0][0:32], cd['ropeS'], writes=['ropt0'])
    token_front(xb, kx, 32, wsST_bf, bsS_sb, ropts[0], 'ropt0')
    S.dma('sp', kvs[:, :], kv_f[0:32, 0:512], reads=['kv_f'])
    S.dma('sp', vso[:, :], v_g[0:32, :], reads=['v_g'])
    for b in range(4):
        S.dma('sp', wins[b, 0:504, :], swin[b, 8:512, :])
        S.dma('sp', wins[b, 504:512, :], kv_f[8 * b:8 * b + 8, 512:768], reads=['kv_f'])

    if do_sample_attn:
        k_Amat = const_bf(sst, 'Amat')
        k_mnew = const_bf(sst, 'mnew')
        k_mwin0 = const_bf(sst, 'mwin0')
        k_Gm = const_f32(sst, 'Gm')
        k_vmS = const_f32(sst, 'vmS')
        k_acS = const_f32(sst, 'acS')
        k_nf = const_f32(sst, 'notfirst')
        k_pmod = const_f32(sst, 'pmod')
        w1s = ssb('w1s', [128, 2, 16, 128], BF16)
        for c in range(2):
            w1v = w1[c].rearrange("(rsp sd) k -> sd rsp k", sd=128)
            for a8 in range(2):
                load_cast(w1s[:, c, a8 * 8:(a8 + 1) * 8, :].rearrange("p a k -> p (a k)"), w1v[:, a8 * 8:(a8 + 1) * 8, :],
                          128, 1024, 'w1s')
        vcaugS = ssb('vcaugS', [128, 8, 386], BF16)
        S.op('pool', lambda e: e.memset(vcaugS[:, :, 128:129], 1.0), writes=['vcS1'])
        st, sk = next_stage()
        for ch in range(8):
            st, sk = next_stage()
            S.dma('sp', st[:, 0:257], cd['ovS'][:, ch, :], writes=[sk])
            cp('dve', vcaugS[:, ch, 129:386], st[:, 0:257], [sk], ['vcS_ov'])
        g1 = [ssb('g1_%d' % i, [128, 2, 256]) for i in range(4)]
        gint = [ssb('gint%d' % i, [128, 4, 2, 64], BF16) for i in range(2)]
        idx_i = [ssb('idx_i%d' % s2_, [128, 256], I32) for s2_ in range(2)]
        ptl_sb = ssb('ptl_sb', [128, 256], I32)
        XTg = [ssb('XTg%d' % i, [128, 4, 8, 65], BF16) for i in range(2)]
        hidS = ssb('hidS', [128, 2, 2, 64], BF16)
        KcmpS = ssb('KcmpS', [128, 1024], BF16)
        qblk_r = ssb('qblk_r', [128, 4, 64], BF16)
        qblk_u = ssb('qblk_u', [128, 4, 64], BF16)
        KnT = ssb('KnT', [128, 2, 32], BF16)
        Vn = ssb('Vn', [32, 2, 130], BF16)
        wbuf = ssb('wbuf', [128, 4, 256])
        KwS = ssb('KwS', [128, 512], BF16)
        VwS = ssb('VwS', [128, 4, 130], BF16)
        pTw = ssb('pTw', [128, 5, 64], BF16)
        PcS = ssb('PcS', [128, 8, 64], BF16)
        impn = ssb('impn', [64, 257])
        scS = ssb('scS', [64, 257])
        scS2 = ssb('scS2', [64, 257])
        m16S = ssb('m16S', [64, 16])
        negS = ssb('negS', [64, 256])
        negST = ssb('negST', [128, 2, 64], BF16)
        KsTt = [ssb('KsTt%d' % i, [128, 2, 128], BF16) for i in range(2)]
        Vst = [ssb('Vst%d' % i, [128, 2, 130], BF16) for i in range(8)]
        pTs = [ssb('pTs%d' % i, [128, 8, 64], BF16) for i in range(2)]
        pTn = ssb('pTn', [32, 64], BF16)
        Ocomb = ssb('Ocomb', [64, 64])
        Otmp = ssb('Otmp', [64, 64])
        gq = ssb('gq', [64, 3])
        rdS = ssb('rdS', [64, 1])
        wgS = ssb('wgS', [64, 1])

        S.dma('sp', ptl_sb[:], ptl[:, :], writes=['ptl'])
        idxf = ssb('idxf', [128, 256])
        ts('dve', idxf[:], ptl_sb[:], 128.0, k_pmod[:, 0:1], ALU.mult, ALU.add, ['ptl', 'k_pmod'], ['idxf'])
        for s2_ in range(2):
            ts('dve', idx_i[s2_][:], idxf[:], float(s2_), None, ALU.add, None, ['idxf'], ['idx'])
        S.op('pool', lambda e: e.memset(qblk_r[:], 0.0), writes=['qblk_r'])
        S.op('pool', lambda e: e.memset(qblk_u[:], 0.0), writes=['qblk_u'])
        S.op('pool', lambda e: e.memset(Vn[:], 1.0), writes=['Vn'])
        S.op('pool', lambda e: e.memset(VwS[:], 1.0), writes=['VwS'])
        for i in range(8):
            S.op('pool', lambda e: e.memset(Vst[i][:], 1.0), writes=['Vst%d' % i])
        for i in range(2):
            S.op('pool', lambda e: e.memset(XTg[i][:], 0.0), writes=['XTg%d' % i])
        for (src, dst, skey, dkey) in ((qr_bf, qblk_r, 'qr_bf', 'qblk_r'), (q_bf, qblk_u, 'q_bf', 'qblk_u')):
            tb, tk = next_tb()
            for h in range(8):
                hk_, g = h // 4, h % 4
                tr(tb[hk_ * 64:(hk_ + 1) * 64, g * 32:(g + 1) * 32], src[0:32, h, :], idb[0:32, 0:32], [skey, 'idb'], [tk])
            tv = tb[:, 0:128].rearrange("p (g b t) -> p b g t", g=4, b=4)
            dv = dst[:].rearrange("p b (h g t) -> p b h g t", h=2, g=4)
            for b in range(4):
                cp('dve', dv[0:64, b, 0], tv[0:64, b], [tk, dkey], [dkey])
                cp('dve', dv[64:128, b, 1], tv[64:128, b], [tk, dkey], [dkey])
        tb, tk = next_tb()
        tr(tb[:, 0:32], kvb[0:32, 256:384], idb[0:32, 0:32], ['kvb', 'idb'], [tk])
        tr(tb[:, 32:64], kvb[0:32, 512:640], idb[0:32, 0:32], ['kvb', 'idb'], [tk])
        cp('dve', KnT[:].rearrange("p a t -> p (a t)"), tb[:, 0:64], [tk], ['KnT'])
        cp('dve', Vn[:, 0, 0:128], kvb[0:32, 384:512], ['kvb', 'Vn'], ['Vn'])
        cp('dve', Vn[:, 1, 0:128], kvb[0:32, 640:768], ['kvb', 'Vn'], ['Vn'])

        def s_branch(po, pok, br, first):
            ts('dve', rdS[:], po[0:64, 128:129], 1e-30, None, ALU.max, None, [pok], ['rdS'])
            S.op('dve', lambda e: e.reciprocal(out=rdS[:], in_=rdS[:]), reads=['rdS'], writes=['rdS'])
            tt('dve', wgS[:], rdS[:], gq[:, br:br + 1], ALU.mult, ['rdS'] + ['gq%d' % h for h in range(8)], ['wgS'])
            for hk in range(2):
                rows = slice(hk * 32, (hk + 1) * 32)
                src = po[rows, hk * 64:(hk + 1) * 64]
                if first:
                    ts('dve', Ocomb[rows, :], src, wgS[rows, 0:1], None, ALU.mult, None, [pok, 'wgS'], ['Ocomb'])
                else:
                    ts('dve', Otmp[rows, :], src, wgS[rows, 0:1], None, ALU.mult, None, [pok, 'wgS'], ['Otmp'])
                    tt('dve', Ocomb[rows, :], Ocomb[rows, :], Otmp[rows, :], ALU.add, ['Ocomb', 'Otmp'], ['Ocomb'])

        gi = [0]
        SIMG = os.environ.get('KSIMGATHER', '0') == '1'

        def gather(b, j, half):
            buf = g1[gi[0] % 4]
            key = 'g1_%d' % (gi[0] % 4)
            gi[0] += 1
            col = b * 64 + j
            for s2_ in range(2):
                dst = buf[:, s2_, :]
                kk = key + 'ab'[s2_]
                S.dma('pool', None, None, reads=['idx'], writes=[kk + '0'],
                      fn=lambda e: e.indirect_dma_start(out=dst, out_offset=None, in_=cacheH[half][:, :],
                                                        in_offset=bass.IndirectOffsetOnAxis(
                                                            ap=idx_i[s2_][:, col:col + 1], axis=0)))
                S.lastw[kk + '1'] = S.lastw[kk + '0']
                S.readers[kk + '1'] = {}
                S._wait('pool', S.lastw[kk + '0'])
            return buf, key

        for b in range(4):
            for h in range(8):
                S.dma('sp', gq[h * 8:(h + 1) * 8, :], gates[8 * b:8 * b + 8, h * 3:(h + 1) * 3], reads=['gates'], writes=['gq%d' % h])
            S.dma('sp', wbuf[:], swin[b].rearrange("(c p) f -> p c f", p=128), writes=['wbuf'])
            pw, pk = next_pw()
            for ch in range(4):
                tr(pw[:, ch * 128:(ch + 1) * 128], wbuf[:, ch, 0:128], idf[:], ['wbuf', 'idf'], [pk])
            cp('dve', KwS[:], pw[:, :], [pk], ['KwS'])
            cp('act', VwS[:, :, 0:128], wbuf[:, :, 128:256], ['wbuf', 'VwS'], ['VwS'])
            pw, pk = next_pw()
            for ch in range(4):
                mm(pw[:, ch * 64:(ch + 1) * 64], KwS[:, ch * 128:(ch + 1) * 128], qblk_r[:, b, :], ch == 0, False,
                   ['KwS', 'qblk_r'], [pk])
            mm(pw[0:32, 256:320], KnT[:, 1, :], qblk_r[:, b, :], False, True, ['KnT', 'qblk_r'], [pk])
            act(pTw[:, 0:4, :].rearrange("p c q -> p (c q)"), pw[:, 0:256], AF.Exp, [pk], ['pTw'], scale=SCALE)
            act(pTw[0:32, 4, :], pw[0:32, 256:320], AF.Exp, [pk], ['pTw4'], scale=SCALE)
            tt('dve', pTw[:, 0, :], pTw[:, 0, :], k_mwin0[:], ALU.mult, ['pTw', 'k_mwin0'], ['pTw'])
            tt('dve', pTw[0:32, 4, :], pTw[0:32, 4, :], k_mnew[:, b, :], ALU.mult, ['pTw4', 'k_mnew'], ['pTw4'])
            for ch in range(4):
                mm(POw[0:64, 0:129], pTw[:, ch, :], VwS[:, ch, 0:129], ch == 0, False, ['pTw', 'VwS'], ['POw'])
            mm(POw[0:64, 0:129], pTw[0:32, 4, :], Vn[:, 1, 0:129], False, True, ['pTw4', 'Vn'], ['POw'])
            slv = int(do_sample_attn)
            for G in range(16 if slv >= 2 else 0):
                xg, xk = XTg[G % 2], 'XTg%d' % (G % 2)
                for jj in range(4):
                    j = G * 4 + jj
                    buf, key = gather(b, j, 0)
                    gi_t, gik = gint[j % 2], 'gint%d' % (j % 2)
                    for s2 in range(2):
                        cp('act' if s2 == 0 else 'dve', gi_t[:, :, s2, :], buf[:, s2, :].rearrange("p (q d) -> p q d", d=64),
                           [key + 'a0', key + 'a1', key + 'b0', key + 'b1', gik + 'x%d' % (1 - s2)], [gik + 'x%d' % s2])
                    tb, tk = next_tb()
                    for q4 in range(4):
                        tr(tb[:, q4 * 128:(q4 + 1) * 128], gi_t[:, q4, :, :].rearrange("p s d -> p (s d)"), idb[:],
                           [gik + 'x0', gik + 'x1', 'idb'], [tk])
                    for q4 in range(4):
                        cp('act' if j % 2 == 0 else 'dve',
                           xg[:, q4, :, 1 + 16 * jj:17 + 16 * jj].rearrange("p sp sg -> p sg sp"),
                           tb[:, q4 * 128:(q4 + 1) * 128].rearrange("p (sg sp) -> p sg sp", sp=8), [tk, xk], [xk])
                for q4 in range(4):
                    c = q4 // 2
                    for r in range(2):
                        for sp in range(8):
                            mm(PM[:, q4 * 64:(q4 + 1) * 64], w1s[:, c, r * 8 + sp, :], xg[:, q4, sp, r:r + 64],
                               (r == 0 and sp == 0), (r == 1 and sp == 7), ['w1s', xk], ['PM'])
                xn_, xnk = XTg[(G + 1) % 2], 'XTg%d' % ((G + 1) % 2)
                cp('dve', xn_[:, :, :, 0:1], xg[:, :, :, 64:65], [xk, xnk], [xnk])
                for c in range(2):
                    act(hidS[:, c].rearrange("p h n -> p (h n)"), PM[:, c * 128:(c + 1) * 128], AF.Gelu_apprx_tanh,
                        ['PM', 'cb'], ['hidS'], bias=cb[:, c:c + 1])
                for hk in range(2):
                    mm(PM[hk * 64:(hk + 1) * 64, 256:320], w2_bf[:, 0, :], hidS[:, 0, hk, :], True, True,
                       ['w2_bf', 'hidS'], ['PM'])
                cp('dve', KcmpS[:, 64 * G:64 * (G + 1)], PM[:, 256:320], ['PM'], ['KcmpS'])
                half = (G % 2) * 64
                for hk in range(2):
                    mm(PM[half:half + 64, 320 + hk * 64:384 + hk * 64], hidS[:, 1, hk, :], w2_bf[:, 1, :], True, True,
                       ['hidS', 'w2_bf'], ['PM'])
                cp('act', vcaugS[half:half + 64, G // 2, 0:128], PM[half:half + 64, 320:448], ['PM', 'vcS'], ['vcS'])
            if slv >= 3:
                pw, pk = next_pw()
                for ch in range(8):
                    mm(pw[:, ch * 64:(ch + 1) * 64], KcmpS[:, ch * 128:(ch + 1) * 128], qblk_u[:, b, :], ch == 0, ch == 7,
                       ['KcmpS', 'qblk_u'], [pk])
                act(PcS[:].rearrange("p c q -> p (c q)"), pw[:, :], AF.Exp, [pk], ['PcS'], scale=SCALE)
                ts('dve', PcS[:, 0, :], PcS[:, 0, :], k_nf[:, 0:1], None, ALU.mult, None, ['PcS', 'k_notfirst'], ['PcS'])
                for ch in range(8):
                    mm(POc[0:64, 0:386], PcS[:, ch, :], vcaugS[:, ch, :], ch == 0, ch == 7, ['PcS', 'vcS', 'vcS1', 'vcS_ov'], ['POc'])
                ts('dve', rdS[:], POc[0:64, 128:129], 1e-30, None, ALU.max, None, ['POc'], ['rdS'])
                S.op('dve', lambda e: e.reciprocal(out=rdS[:], in_=rdS[:]), reads=['rdS'], writes=['rdS'])
                ts('dve', impn[:], POc[0:64, 129:386], rdS[:, 0:1], None, ALU.mult, None, ['POc', 'rdS'], ['impn'])
                s_branch(POc, 'POc', 0, True)
                pw, pk = next_pw()
                mm(pw[0:64, 0:257], k_Gm[:], impn[:], True, True, ['k_Gm', 'impn'], [pk])
                tt('dve', scS[:], pw[0:64, 0:257], k_vmS[:], ALU.mult, [pk, 'k_vmS'], ['scS'])
                tt('dve', scS[:], scS[:], k_acS[:], ALU.add, ['scS', 'k_acS'], ['scS'])
                S.op('dve', lambda e: e.max(out=m16S[:, 0:8], in_=scS[:]), reads=['scS'], writes=['m16S'])
                S.op('dve', lambda e: e.match_replace(out=scS2[:], in_to_replace=m16S[:, 0:8], in_values=scS[:], imm_value=-3.0e38),
                     reads=['scS', 'm16S'], writes=['scS2'])
                S.op('dve', lambda e: e.max(out=m16S[:, 8:16], in_=scS2[:]), reads=['scS2'], writes=['m16S'])
                ts('dve', negS[:], scS[:, 0:256], m16S[:, 15:16], NEGM, ALU.is_lt, ALU.mult, ['scS', 'm16S'], ['negS'])
                pw, pk = next_pw()
                for hf in range(2):
                    tr(pw[:, hf * 64:(hf + 1) * 64], negS[:, hf * 128:(hf + 1) * 128], idf[0:64, 0:64], ['negS', 'idf'], [pk])
                cp('dve', negST[:].rearrange("p a q -> p (a q)"), pw[:, 0:128], [pk], ['negST'])
            if slv >= 4:
                for j in range(64):
                    buf, key = gather(b, j, 1)
                    kt, ktk = KsTt[j % 2], 'KsTt%d' % (j % 2)
                    vt, vtk = Vst[j % 8], 'Vst%d' % (j % 8)
                    pw, pk = next_pw()
                    for s2 in range(2):
                        tr(pw[:, s2 * 128:(s2 + 1) * 128], buf[:, s2, 0:128], idf[:], [key + 'a0', key + 'a1', key + 'b0', key + 'b1', 'idf'], [pk])
                    cp('act', kt[:].rearrange("p a k -> p (a k)"), pw[:, 0:256], [pk], [ktk])
                    cp('dve', vt[:, :, 0:128], buf[:, :, 128:256], [key + 'a0', key + 'a1', key + 'b0', key + 'b1', vtk], [vtk])
                    grp = j // 4
                    pt_, ptk = pTs[grp % 2], 'pTs%d' % (grp % 2)
                    for s2 in range(2):
                        slot = (j % 4) * 2 + s2
                        mm(POs[:, slot * 64:(slot + 1) * 64], kt[:, s2, :], qblk_r[:, b, :], slot == 0, False,
                           [ktk, 'qblk_r'], ['POs'])
                        mm(POs[:, slot * 64:(slot + 1) * 64], k_Amat[:, 128 * (j % 32):128 * (j % 32 + 1)], negST[:, j // 32, :],
                           False, slot == 7, ['k_Amat', 'negST'], ['POs'])
                    if j % 4 == 3:
                        act(pt_[:].rearrange("p c q -> p (c q)"), POs[:, :], AF.Exp, ['POs'], [ptk], scale=SCALE)
                        for jj in range(4):
                            j2 = j - 3 + jj
                            for s2 in range(2):
                                slot = jj * 2 + s2
                                mm(POc[0:64, 0:129], pt_[:, slot, :], Vst[j2 % 8][:, s2, 0:129], (j2 == 0 and s2 == 0), False,
                                   [ptk, 'Vst%d' % (j2 % 8)], ['POc'])
                pw, pk = next_pw()
                mm(pw[0:32, 0:64], KnT[:, 0, :], qblk_r[:, b, :], True, True, ['KnT', 'qblk_r'], [pk])
                act(pTn[:], pw[0:32, 0:64], AF.Exp, [pk], ['pTn'], scale=SCALE)
                tt('dve', pTn[:], pTn[:], k_mnew[:, b, :], ALU.mult, ['pTn', 'k_mnew'], ['pTn'])
                mm(POc[0:64, 0:129], pTn[:], Vn[:, 0, 0:129], False, True, ['pTn', 'Vn'], ['POc'])
                s_branch(POc, 'POc', 1, False)
            s_branch(POw, 'POw', 2, slv < 3)
            for h in range(8):
                S.dma('sp', bo_f[8 * b:8 * b + 8, h * 64:(h + 1) * 64], Ocomb[h * 8:(h + 1) * 8, :], reads=['Ocomb'],
                      writes=['bo_f%d' % (b * 8 + h)])
        tt('dve', merged[0:32, 512:1024], bo_f[0:32], szb[0:32], ALU.mult, ['szb'] + ['bo_f%d' % k for k in range(32)],
           ['merged_b'])
    else:
        S.op('pool', lambda e: e.memset(merged[0:32, 512:1024], 0.0), writes=['merged_b'])
    token_back(xb, kx, 32, ys[:, :])
    S.finish()
    sst.close()
    setup_stack.close()
    return nc, consts


_PROG = {}


def kernel(x_prompt, x_sample, cache_kv, state_win, page_table, norm_g, w_in, ln_g, ln_b, w_s, b_s, cmp_pos,
           w_cmp1, b_cmp1, w_cmp2, w_out, final_g, _flags=(4, 4)):
    f32 = np.float32
    if _flags not in _PROG:
        _PROG[_flags] = build(*_flags)
    nc, consts = _PROG[_flags]
    x_prompt = np.asarray(x_prompt, f32)
    x_sample = np.asarray(x_sample, f32)
    cache = np.asarray(cache_kv, f32).reshape(N_PHYS * 128, 2, 256)
    state_win = np.asarray(state_win, f32)
    page_table = np.asarray(page_table, np.int32)
    shared = {
        'cache0': np.ascontiguousarray(cache[:, 0, :]),
        'cache1': np.ascontiguousarray(cache[:, 1, :]),
        'gT': np.ascontiguousarray(np.asarray(norm_g, f32).reshape(8, 128).T),
        'w_in': np.asarray(w_in, f32).reshape(D_MODEL, D_IN),
        'w_out': np.asarray(w_out, f32).reshape(D_MODEL, D_MODEL),
        'lngb': np.ascontiguousarray(np.broadcast_to(
            np.stack([np.asarray(ln_g, f32).reshape(512), np.asarray(ln_b, f32).reshape(512)])[None], (128, 2, 512))),
        'fgb': np.ascontiguousarray(np.broadcast_to(np.asarray(final_g, f32).reshape(1, D_MODEL), (128, D_MODEL))),
        'w_s': np.asarray(w_s, f32).reshape(8, 128, 128),
        'bsT': np.ascontiguousarray(np.asarray(b_s, f32).reshape(8, 128).T),
        'bsS': np.ascontiguousarray(np.tile(np.asarray(b_s, f32).reshape(8, 128)[:, 0:8].T, (4, 1))),
        'peT': np.ascontiguousarray(np.asarray(cmp_pos, f32).reshape(2, 32, 64).transpose(2, 0, 1)),
        'w1': np.asarray(w_cmp1, f32).reshape(2, 2048, 128),
        'b1T': np.ascontiguousarray(np.asarray(b_cmp1, f32).reshape(2, 128).T),
        'w2': np.asarray(w_cmp2, f32).reshape(2, 128, 64),
    }
    for k, v in consts.items():
        shared['c_' + k] = v
    in_maps = []
    for c in range(8):
        m = dict(shared)
        m['xp'] = x_prompt[c]
        m['xs'] = np.ascontiguousarray(x_sample[4 * c:4 * c + 4].reshape(32, D_MODEL))
        m['swin'] = np.ascontiguousarray(state_win[0, 4 * c:4 * c + 4].reshape(4, 512, 256))
        pt = page_table[4 * c:4 * c + 4]
        ptl = pt.reshape(4, 64, 2).transpose(2, 0, 1).reshape(2, 256)
        m['ptl'] = np.ascontiguousarray(np.repeat(ptl, 64, axis=0).astype(np.int32))
        in_maps.append(m)
    res = run_bass_kernel_spmd(nc, in_maps, core_ids=list(range(8)))
    r = res.results
    y_prompt = np.stack([r[c]['yp'] for c in range(8)]).reshape(8, SEQ, D_MODEL)
    y_sample = np.concatenate([r[c]['ys'].reshape(4, 8, D_MODEL) for c in range(8)], 0)
    kv_p = np.stack([r[c]['kvp'] for c in range(8)]).reshape(1, 8, SEQ, 4, 2, 64)
    win_p = np.stack([r[c]['winp'] for c in range(8)]).reshape(1, 8, 512, 2, 2, 64)
    kv_s = np.concatenate([r[c]['kvs'].reshape(4, 8, 4, 2, 64) for c in range(8)], 0).reshape(1, 32, 8, 4, 2, 64)
    win_s = np.concatenate([r[c]['wins'] for c in range(8)], 0).reshape(1, 32, 512, 2, 2, 64)
    v_s = np.concatenate([r[c]['vso'].reshape(4, 8, 512) for c in range(8)], 0).reshape(1, 32, 8, 512)
    return (y_prompt.astype(f32), y_sample.astype(f32), kv_p.astype(f32), win_p.astype(f32), kv_s.astype(f32),
            win_s.astype(f32), v_s.astype(f32))
```

```python
from contextlib import ExitStack
import os
import numpy as np
import concourse.bass as bass
import concourse.mybir as mybir
from concourse.bass_utils import run_bass_kernel_spmd

F32 = mybir.dt.float32
BF16 = mybir.dt.bfloat16
I32 = mybir.dt.int32
AF = mybir.ActivationFunctionType
ALU = mybir.AluOpType
AX = mybir.AxisListType
NDMASEM = 40
NSWSEM = 4

D_MODEL = 1024
SEQ = 2048
NT = 16
D_IN = 3352
PAST = 16384
EPS = 1e-6
SCALE = 0.125
NEGM = -30000.0
FORCE = 1e9
NEG = -1e30
N_PHYS = 5120
CHUNKS = [(0, 512), (512, 1024), (1024, 1536), (1536, 2048), (2048, 2560), (2560, 2840), (2840, 3352)]


class Sched:
    def __init__(self, nc):
        self.nc = nc
        self.eng = {'pe': nc.tensor, 'act': nc.scalar, 'dve': nc.vector, 'pool': nc.gpsimd, 'sp': nc.sync}
        self.sem = {k: nc.alloc_semaphore('sem_' + k) for k in self.eng}
        self.cnt = {k: 0 for k in self.eng}
        self.waited = {k: {} for k in self.eng}
        self.lastw = {}
        self.readers = {}
        self.dma_sems = [nc.alloc_semaphore('dq%d' % i) for i in range(NDMASEM)]
        self.dma_val = [0] * NDMASEM
        self.dma_next = 0
        self.sw_next = 0

    def _wait(self, e, ev):
        sem, val, name = ev
        w = self.waited[e]
        if w.get(name, 0) >= val:
            return
        self.eng[e].wait_ge(sem, val)
        w[name] = val

    def _deps(self, e, reads, writes):
        best = {}

        def add(ev):
            if ev[2] not in best or best[ev[2]][1] < ev[1]:
                best[ev[2]] = ev
        for k in reads:
            if k in self.lastw:
                add(self.lastw[k])
        for k in writes:
            if k in self.lastw:
                add(self.lastw[k])
            for ev in self.readers.get(k, {}).values():
                add(ev)
        for name, ev in best.items():
            if name == 'pe' and e == 'pe':
                continue
            self._wait(e, ev)

    def _record(self, ev, reads, writes):
        for k in reads:
            d = self.readers.setdefault(k, {})
            d[ev[2]] = ev
        for k in writes:
            self.lastw[k] = ev
            self.readers[k] = {}

    def op(self, e, fn, reads=(), writes=()):
        self._deps(e, reads, writes)
        inst = fn(self.eng[e])
        self.cnt[e] += 1
        inst.then_inc(self.sem[e], 1)
        self._record((self.sem[e], self.cnt[e], e), reads, writes)

    def dma(self, e, out, in_, reads=(), writes=(), fn=None):
        self._deps(e, reads, writes)
        if e == 'pool':
            i = NDMASEM - NSWSEM + self.sw_next
            self.sw_next = (self.sw_next + 1) % NSWSEM
        else:
            i = self.dma_next
            self.dma_next = (i + 1) % (NDMASEM - NSWSEM)
        sem = self.dma_sems[i]
        name = 'dq%d' % i
        if self.dma_val[i] > 0:
            self._wait(e, (sem, self.dma_val[i], name))
        if fn is None:
            inst = self.eng[e].dma_start(out=out, in_=in_)
        else:
            inst = fn(self.eng[e])
        self.dma_val[i] += 16
        inst.then_inc(sem, 16)
        self._record((sem, self.dma_val[i], name), reads, writes)

    def barrier(self):
        evs = [(self.sem[k], self.cnt[k], k) for k in self.eng if self.cnt[k] > 0]
        evs += [(self.dma_sems[i], self.dma_val[i], 'dq%d' % i) for i in range(NDMASEM) if self.dma_val[i] > 0]
        for e in self.eng:
            for ev in evs:
                if ev[2] != e:
                    self._wait(e, ev)

    def finish(self):
        for i in range(NDMASEM):
            if self.dma_val[i] > 0:
                self._wait('sp', (self.dma_sems[i], self.dma_val[i], 'dq%d' % i))
        for k in self.eng:
            if k != 'sp' and self.cnt[k] > 0:
                self._wait('sp', (self.sem[k], self.cnt[k], k))


def _rope_tab(pos):
    half = 32
    inv = (10000.0 ** (-np.arange(half, dtype=np.float64) / half)).astype(np.float32)
    ang = pos.astype(np.float32)[:, None] * inv[None, :]
    cos = np.cos(ang.astype(np.float64)).astype(np.float32)
    sin = np.sin(ang.astype(np.float64)).astype(np.float32)
    return np.stack([np.concatenate([cos, cos], -1), np.concatenate([-sin, sin], -1)], 1).astype(np.float32)


def _host_consts():
    c = {}
    c['ropeP'] = _rope_tab(np.arange(SEQ))
    c['ropeS'] = _rope_tab(PAST + np.tile(np.arange(8), 4))
    k = np.arange(128)[:, None]
    t = np.arange(128)[None, :]
    cm = np.zeros((128, 2, 128), np.float32)
    cm[:, 0] = (k <= t)
    cm[:, 1] = (k > t)
    c['cmask'] = cm
    n = np.arange(128)[:, None, None]
    i = np.arange(NT)[None, :, None]
    tq = np.arange(128)[None, None, :]
    c['m01c'] = ((16 * n + 31 <= 128 * i + tq) & (n < 127)).astype(np.float32)
    blk = np.arange(32)[:, None]
    key = np.arange(SEQ)[None, :]
    c['Ep'] = (key // 64 == blk).astype(np.float32)
    nn = np.arange(128)[:, None]
    j = np.arange(32)[None, :]
    c['ovP'] = ((16 * nn <= 64 * j + 63) & (16 * nn + 31 >= 64 * j) & (nn < 127)).astype(np.float32)
    tt = (128 * np.arange(8, 16)[None, :, None] + np.arange(128)[:, None, None])
    jj = np.arange(32)[None, None, :]
    tb = tt // 64
    forced = (jj == 0) | (jj == tb) | (jj == tb - 1)
    valid = jj * 64 <= tt
    c['vmP'] = (valid & ~forced).astype(np.float32)
    c['acP'] = np.where(forced, FORCE, np.where(valid, 0.0, NEG)).astype(np.float32)
    p = np.arange(128)[:, None]
    x = np.arange(4096)[None, :]
    c['Amat'] = (x // 32 == p).astype(np.float32)
    npr = (np.arange(8)[None, :, None] * 128 + np.arange(128)[:, None, None])
    ns = npr - 1
    js = np.arange(257)[None, None, :]
    c['ovS'] = ((16 * ns <= 64 * js + 63) & (16 * ns + 31 >= 64 * js) & (ns >= 0) & (ns <= 1022)).astype(np.float32)
    q = np.arange(64)
    hk, tqq = q // 32, q % 8
    c['Gm'] = ((hk[:, None] == hk[None, :]) & (tqq[:, None] == tqq[None, :])).astype(np.float32)
    fj = np.zeros(257, bool)
    fj[[0, 255, 256]] = True
    c['vmS'] = np.broadcast_to((~fj).astype(np.float32)[None, :], (64, 257)).copy()
    c['acS'] = np.broadcast_to(np.where(fj, FORCE, 0.0).astype(np.float32)[None, :], (64, 257)).copy()
    kp = np.arange(32)
    bp, tp = kp // 8, kp % 8
    mn = np.zeros((32, 4, 64), np.float32)
    for b in range(4):
        mn[:, b, :] = ((bp[:, None] == b) & (tp[:, None] <= tqq[None, :]))
    c['mnew'] = mn
    c['mwin0'] = (np.arange(128)[:, None] > tqq[None, :]).astype(np.float32)
    nf = np.ones((128, 1), np.float32)
    nf[0, 0] = 0.0
    c['notfirst'] = nf
    c['pmod'] = (np.arange(128) % 64).astype(np.float32)[:, None]
    return c


def build(do_prompt_attn=True, do_sample_attn=True):
    nc = bass.Bass("TRN2", target_bir_lowering=False)
    S = Sched(nc)
    consts = _host_consts()

    def din(name, shape, dt=F32):
        return nc.dram_tensor(name, list(shape), dt, kind="ExternalInput").ap()

    def dout(name, shape, dt=F32):
        return nc.dram_tensor(name, list(shape), dt, kind="ExternalOutput").ap()

    def sb(name, shape, dt=F32):
        return nc.alloc_sbuf_tensor(name, list(shape), dt)

    xp = din('xp', [SEQ, D_MODEL])
    xs = din('xs', [32, D_MODEL])
    cacheH = [din('cache%d' % h_, [N_PHYS * 64, 512]) for h_ in range(2)]
    swin = din('swin', [4, 512, 256])
    ptl = din('ptl', [128, 256], I32)
    gT = din('gT', [128, 8])
    w_in = din('w_in', [D_MODEL, D_IN])
    w_out = din('w_out', [D_MODEL, D_MODEL])
    lngb = din('lngb', [128, 2, 512])
    fgb = din('fgb', [128, D_MODEL])
    w_s = din('w_s', [8, 128, 128])
    bsT = din('bsT', [128, 8])
    bsS = din('bsS', [32, 8])
    peT = din('peT', [64, 2, 32])
    w1 = din('w1', [2, 2048, 128])
    b1T = din('b1T', [128, 2])
    w2 = din('w2', [2, 128, 64])
    cd = {k: din('c_' + k, v.shape) for k, v in consts.items()}

    yp = dout('yp', [SEQ, D_MODEL])
    ys = dout('ys', [32, D_MODEL])
    kvp = dout('kvp', [SEQ, 512])
    winp = dout('winp', [512, 256])
    kvs = dout('kvs', [32, 512])
    wins = dout('wins', [4, 512, 256])
    vso = dout('vso', [32, 512])

    PW = [nc.alloc_psum_tensor('PW%d' % i, [128, 512], F32) for i in range(2)]
    TB = [nc.alloc_psum_tensor('TB%d' % i, [128, 1024], BF16) for i in range(2)]
    POc = nc.alloc_psum_tensor('POc', [128, 512], F32)
    POs = nc.alloc_psum_tensor('POs', [128, 512], F32)
    POw = nc.alloc_psum_tensor('POw', [128, 512], F32)
    PM = nc.alloc_psum_tensor('PM', [128, 512], F32)
    pwi = [0]
    tbi = [0]

    def next_pw():
        pwi[0] ^= 1
        return PW[pwi[0]], 'PW%d' % pwi[0]

    def next_tb():
        tbi[0] ^= 1
        return TB[tbi[0]], 'TB%d' % tbi[0]

    def mm(out, lhsT, rhs, start, stop, reads, writes):
        S.op('pe', lambda e: e.matmul(out, lhsT=lhsT, rhs=rhs, start=start, stop=stop, skip_group_check=True),
             reads=reads, writes=writes)

    def tr(out, in_, ident, reads, writes):
        S.op('pe', lambda e: e.transpose(out=out, in_=in_, identity=ident), reads=reads, writes=writes)

    def act(out, in_, func, reads, writes, **kw):
        S.op('act', lambda e: e.activation(out=out, in_=in_, func=func, **kw), reads=reads, writes=writes)

    def tt(eng, out, in0, in1, op, reads, writes):
        S.op(eng, lambda e: e.tensor_tensor(out=out, in0=in0, in1=in1, op=op), reads=reads, writes=writes)

    def ts(eng, out, in0, s1, s2, op0, op1, reads, writes):
        if op1 is None:
            S.op(eng, lambda e: e.tensor_scalar(out=out, in0=in0, scalar1=s1, scalar2=None, op0=op0), reads=reads, writes=writes)
        else:
            S.op(eng, lambda e: e.tensor_scalar(out=out, in0=in0, scalar1=s1, scalar2=s2, op0=op0, op1=op1),
                 reads=reads, writes=writes)

    def cp(eng, out, in_, reads, writes):
        if eng == 'act':
            act(out, in_, AF.Copy, reads, writes)
        elif out.dtype == in_.dtype:
            S.op(eng, lambda e: e.tensor_scalar(out=out, in0=in_, scalar1=1.0, scalar2=None, op0=ALU.mult),
                 reads=reads, writes=writes)
        else:
            S.op(eng, lambda e: e.tensor_copy(out=out, in_=in_), reads=reads, writes=writes)

    idf = sb('idf', [128, 128])
    idb = sb('idb', [128, 128], BF16)
    w_in_bf = sb('w_in_bf', [128, 8, D_IN], BF16)
    w_out_bf = sb('w_out_bf', [128, 8, D_MODEL], BF16)
    gT_sb = sb('gT_sb', [128, 8])
    lngb_sb = sb('lngb_sb', [128, 2, 512])
    fg_sb = sb('fg_sb', [128, D_MODEL])
    wsT_bf = sb('wsT_bf', [128, 8, 128], BF16)
    wsST_bf = sb('wsST_bf', [32, 8, 32], BF16)
    bsT_sb = sb('bsT_sb', [128, 8])
    bsS_sb = sb('bsS_sb', [32, 8])
    w2_bf = sb('w2_bf', [128, 2, 64], BF16)
    cb = sb('cb', [128, 2])
    b1T_sb = sb('b1T_sb', [128, 2])
    peT_bf = sb('peT_bf', [64, 2, 32], BF16)
    xt = [sb('xt%d' % i, [128, D_MODEL]) for i in range(2)]
    ropts = [sb('ropt%d' % i, [128, 2, 64]) for i in range(2)]
    ss = sb('ss', [128, 1])
    hn = sb('hn', [128, D_MODEL], BF16)
    junk = hn
    hT = sb('hT', [128, 8, 128], BF16)
    u_g = sb('u_g', [128, 512])
    v_g = sb('v_g', [128, 512])
    v_ln = sb('v_ln', [128, 512], BF16)
    sza = sb('sza', [128, 512])
    q_f = sb('q_f', [128, 512])
    kv_f = sb('kv_f', [128, 768])
    gates = sb('gates', [128, 24])
    szb = sb('szb', [128, 512])
    st6 = sb('st6', [128, 6])
    mv = sb('mv', [128, 2])
    rs = sb('rs', [128, 1])
    a_f = sb('a_f', [128, 512])
    merged = sb('merged', [128, D_MODEL], BF16)
    mT = sb('mT', [128, 8, 128], BF16)
    rt1 = sb('rt1', [128, 512])
    rt2 = sb('rt2', [128, 512])
    qr_bf = sb('qr_bf', [128, 8, 64], BF16)
    q_bf = sb('q_bf', [128, 8, 64], BF16)
    kvb = sb('kvb', [128, 768], BF16)
    bo_f = sb('bo_f', [128, 512])
    tmpo = sb('tmpo', [128, 4, 64])

    S.op('pool', lambda e: e.memset(idf[:], 1.0), writes=['idf'])
    S.op('pool', lambda e: e.affine_select(out=idf[:], in_=idf[:], pattern=[[-1, 128]], base=0, channel_multiplier=1,
                                           compare_op=ALU.is_equal, fill=0.0), reads=['idf'], writes=['idf'])
    cp('dve', idb[:], idf[:], ['idf'], ['idb'])
    S.dma('sp', gT_sb[:], gT[:, :], writes=['gT'])
    S.dma('sp', lngb_sb[:], lngb[:, :, :], writes=['lngb'])
    S.dma('sp', fg_sb[:], fgb[:, :], writes=['fg'])
    S.dma('sp', bsT_sb[:], bsT[:, :], writes=['bsT'])
    S.dma('sp', bsS_sb[:], bsS[:, :], writes=['bsS'])
    S.dma('sp', b1T_sb[:], b1T[:, :], writes=['b1T'])

    setup_stack = ExitStack()
    stage = [setup_stack.enter_context(nc.sbuf_tensor('stage%d' % i, [128, 1024], F32)) for i in range(2)]
    sti = [0]

    def next_stage():
        i = sti[0]
        sti[0] ^= 1
        return stage[i], 'stage%d' % i

    def load_cast(dst_ap, src_ap, P, W, dkey, scale_ap=None, eng='dve'):
        st, sk = next_stage()
        S.dma('sp', st[0:P, 0:W], src_ap, writes=[sk])
        if scale_ap is not None:
            ts(eng, dst_ap, st[0:P, 0:W], scale_ap, None, ALU.mult, None, [sk, 'gT'], [dkey])
        else:
            cp(eng, dst_ap, st[0:P, 0:W], [sk], [dkey])

    for kc in range(8):
        for hf in range(4):
            c0 = hf * 838
            load_cast(w_in_bf[:, kc, c0:c0 + 838], w_in[kc * 128:(kc + 1) * 128, c0:c0 + 838], 128, 838, 'w_in_bf',
                      scale_ap=gT_sb[:, kc:kc + 1], eng=('dve' if hf % 2 == 0 else 'pool'))
    for kc in range(8):
        load_cast(w_out_bf[:, kc, :], w_out[kc * 128:(kc + 1) * 128, :], 128, 1024, 'w_out_bf',
                  eng=('dve' if kc % 2 == 0 else 'pool'))
    st, sk = next_stage()
    S.dma('sp', st[:, 0:128].rearrange("p (c d) -> p c d", c=2), w2.rearrange("c k d -> k c d"), writes=[sk])
    cp('dve', w2_bf[:].rearrange("p c d -> p (c d)"), st[:, 0:128], [sk], ['w2_bf'])
    st, sk = next_stage()
    S.dma('sp', st[0:64, 0:64].rearrange("p (c r) -> p c r", c=2), peT[:, :, :], writes=[sk])
    cp('dve', peT_bf[:].rearrange("p c r -> p (c r)"), st[0:64, 0:64], [sk], ['peT'])
    st, sk = next_stage()
    wsv = st[:, 0:1024].rearrange("p (g s) -> p g s", g=8)
    S.dma('sp', wsv, w_s.rearrange("g t s -> t g s"), writes=[sk])
    S.op('pool', lambda e: e.affine_select(out=wsv, in_=wsv, pattern=[[0, 8], [-1, 128]], base=0, channel_multiplier=1,
                                           compare_op=ALU.is_ge, fill=0.0), reads=[sk], writes=[sk])
    cp('dve', junk[:, 0:1024], st[:, 0:1024], [sk], ['hn'])
    tb, tk = next_tb()
    for g in range(8):
        tr(tb[:, g * 128:(g + 1) * 128], junk[:, g * 128:(g + 1) * 128], idb[:], ['hn', 'idb'], [tk])
    cp('dve', wsT_bf[:].rearrange("p g t -> p (g t)"), tb[:, :], [tk], ['wsT'])
    st, sk = next_stage()
    wss = st[0:32, 0:256].rearrange("p (g s) -> p g s", g=8)
    S.op('pool', lambda e: e.memset(st[0:32, 0:256], 0.0), writes=[sk])
    with nc.allow_non_contiguous_dma(reason="tiny 8x8 blocks of w_s"):
        for b in range(4):
            S.dma('sp', wss[8 * b:8 * b + 8, :, 8 * b:8 * b + 8], w_s[:, 0:8, 0:8].rearrange("g t s -> t g s"),
                  reads=[sk], writes=[sk + 'b%d' % b])
    S.op('pool', lambda e: e.affine_select(out=wss, in_=wss, pattern=[[0, 8], [-1, 32]], base=0, channel_multiplier=1,
                                           compare_op=ALU.is_ge, fill=0.0), reads=[sk] + [sk + 'b%d' % b for b in range(4)],
         writes=[sk])
    cp('dve', junk[0:32, 0:256], st[0:32, 0:256], [sk], ['hn'])
    tb, tk = next_tb()
    for g in range(8):
        tr(tb[0:32, g * 32:(g + 1) * 32], junk[0:32, g * 32:(g + 1) * 32], idb[0:32, 0:32], ['hn', 'idb'], [tk])
    cp('dve', wsST_bf[:].rearrange("p g t -> p (g t)"), tb[0:32, 0:256], [tk], ['wsST'])

    def const_bf(stack, name):
        shape = list(consts[name].shape)
        t = stack.enter_context(nc.sbuf_tensor('k_' + name, shape, BF16))
        P, W = shape[0], int(np.prod(shape[1:]))
        if len(shape) == 2:
            flat, src = t[:], cd[name]
        else:
            flat, src = t[:].rearrange("p a b -> p (a b)"), cd[name].rearrange("p a b -> p (a b)")
        for c0 in range(0, W, 1024):
            wd = min(1024, W - c0)
            load_cast(flat[:, c0:c0 + wd], src[:, c0:c0 + wd], P, wd, 'k_' + name)
        return t

    def const_f32(stack, name):
        shape = list(consts[name].shape)
        t = stack.enter_context(nc.sbuf_tensor('k_' + name, shape, F32))
        S.dma('sp', t[:], cd[name], writes=['k_' + name])
        return t

    def token_front(xb, kx, P, ws_bf, bs_sb, ropt, ropekey):
        act(junk[0:P, :], xb[0:P, :], AF.Square, [kx], ['hn', 'ss'], accum_out=ss[0:P, :])
        ts('dve', ss[0:P], ss[0:P], 1.0 / D_MODEL, EPS, ALU.mult, ALU.add, ['ss'], ['ss'])
        act(ss[0:P], ss[0:P], AF.Sqrt, ['ss'], ['ss'])
        S.op('dve', lambda e: e.reciprocal(out=ss[0:P], in_=ss[0:P]), reads=['ss'], writes=['ss'])
        ts('dve', hn[0:P], xb[0:P], ss[0:P, 0:1], None, ALU.mult, None, [kx, 'ss'], ['hn'])
        tb, tk = next_tb()
        for kc in range(8):
            tr(tb[:, kc * 128:kc * 128 + P], hn[0:P, kc * 128:(kc + 1) * 128], idb[0:P, 0:P], ['hn', 'idb'], [tk])
        cp('dve', hT[:, :, 0:P], tb[:, :].rearrange("p (k t) -> p k t", k=8)[:, :, 0:P], [tk], ['hT'])
        for ci, (c0, c1) in enumerate(CHUNKS):
            w = c1 - c0
            pw, pk = next_pw()
            for kc in range(8):
                mm(pw[0:P, 0:w], hT[:, kc, 0:P], w_in_bf[:, kc, c0:c1], kc == 0, kc == 7, ['hT', 'w_in_bf'], [pk])
            if ci == 0:
                act(u_g[0:P], pw[0:P, :], AF.Gelu_apprx_tanh, [pk], ['u_g'])
            elif ci == 1:
                act(v_g[0:P], pw[0:P, :], AF.Gelu_apprx_tanh, [pk], ['v_g'])
                S.op('dve', lambda e: e.bn_stats(out=st6[0:P], in_=v_g[0:P]), reads=['v_g'], writes=['st6'])
                S.op('dve', lambda e: e.bn_aggr(out=mv[0:P], in_=st6[0:P]), reads=['st6'], writes=['mv'])
                ts('dve', rs[0:P], mv[0:P, 1:2], EPS, None, ALU.add, None, ['mv'], ['rs'])
                act(rs[0:P], rs[0:P], AF.Sqrt, ['rs'], ['rs'])
                S.op('dve', lambda e: e.reciprocal(out=rs[0:P], in_=rs[0:P]), reads=['rs'], writes=['rs'])
                ts('dve', v_g[0:P], v_g[0:P], mv[0:P, 0:1], rs[0:P, 0:1], ALU.subtract, ALU.mult, ['v_g', 'mv', 'rs'], ['v_g'])
                tt('dve', v_g[0:P], v_g[0:P], lngb_sb[0:P, 0, :], ALU.mult, ['v_g', 'lngb'], ['v_g'])
                tt('dve', v_g[0:P], v_g[0:P], lngb_sb[0:P, 1, :], ALU.add, ['v_g', 'lngb'], ['v_g'])
                cp('act', v_ln[0:P], v_g[0:P], ['v_g'], ['v_ln'])
            elif ci == 2:
                act(sza[0:P], pw[0:P, :], AF.Silu, [pk], ['sza'])
            elif ci == 3:
                cp('act', q_f[0:P], pw[0:P, :], [pk], ['q_f'])
            elif ci == 4:
                cp('act', kv_f[0:P, 0:512], pw[0:P, :], [pk], ['kv_f'])
            elif ci == 5:
                cp('act', kv_f[0:P, 512:768], pw[0:P, 0:256], [pk], ['kv_f'])
                act(gates[0:P], pw[0:P, 256:280], AF.Sigmoid, [pk], ['gates'])
            else:
                act(szb[0:P], pw[0:P, :], AF.Silu, [pk], ['szb'])
        pw, pk = next_pw()
        for g in range(8):
            mm(pw[0:P, g * 64:(g + 1) * 64], ws_bf[0:P, g, 0:P], v_ln[0:P, g * 64:(g + 1) * 64], g == 0, g == 7,
               ['v_ln', 'wsT', 'wsST'], [pk])
        tt('dve', a_f[0:P].rearrange("p (g d) -> p g d", g=8), pw[0:P, :].rearrange("p (g d) -> p g d", g=8),
           bs_sb[0:P, :].unsqueeze(2).to_broadcast([P, 8, 64]), ALU.add, [pk, 'bsT', 'bsS'], ['a_f'])
        tt('dve', a_f[0:P], a_f[0:P], u_g[0:P], ALU.mult, ['a_f', 'u_g'], ['a_f'])
        tt('dve', merged[0:P, 0:512], a_f[0:P], sza[0:P], ALU.mult, ['a_f', 'sza'], ['merged_a'])
        cos_b = lambda H: ropt[0:P, 0, :].unsqueeze(1).to_broadcast([P, H, 64])
        sin_lo = lambda H: ropt[0:P, 1, 0:32].unsqueeze(1).to_broadcast([P, H, 32])
        sin_hi = lambda H: ropt[0:P, 1, 32:64].unsqueeze(1).to_broadcast([P, H, 32])

        def rope(src, dst, H, skey, dkey):
            t1 = rt1[0:P, 0:H * 64].rearrange("p (h d) -> p h d", h=H)
            t2 = rt2[0:P, 0:H * 64].rearrange("p (h d) -> p h d", h=H)
            tt('pool', t1, src, cos_b(H), ALU.mult, [skey, ropekey], ['rt1'])
            tt('pool', t2[:, :, 0:32], src[:, :, 32:64], sin_lo(H), ALU.mult, [skey, ropekey], ['rt2a'])
            tt('pool', t2[:, :, 32:64], src[:, :, 0:32], sin_hi(H), ALU.mult, [skey, ropekey], ['rt2b'])
            tt('pool', dst, t1, t2, ALU.add, ['rt1', 'rt2a', 'rt2b'], [dkey])

        rope(q_f[0:P].rearrange("p (h d) -> p h d", h=8), qr_bf[0:P], 8, 'q_f', 'qr_bf')
        cp('act', q_bf[0:P].rearrange("p h d -> p (h d)"), q_f[0:P], ['q_f'], ['q_bf'])
        ksv = kv_f[0:P, 256:384].rearrange("p (h d) -> p h d", h=2)
        kwv = kv_f[0:P, 512:640].rearrange("p (h d) -> p h d", h=2)
        rope(ksv, ksv, 2, 'kv_f', 'kv_f')
        rope(kwv, kwv, 2, 'kv_f', 'kv_f')
        cp('act', kvb[0:P], kv_f[0:P], ['kv_f'], ['kvb'])

    def token_back(xb, kx, P, ydst):
        tb, tk = next_tb()
        for kc in range(8):
            tr(tb[:, kc * 128:kc * 128 + P], merged[0:P, kc * 128:(kc + 1) * 128], idb[0:P, 0:P],
               ['merged_a', 'merged_b', 'idb'], [tk])
        cp('dve', mT[:, :, 0:P], tb[:, :].rearrange("p (k t) -> p k t", k=8)[:, :, 0:P], [tk], ['mT'])
        for half in range(2):
            pw, pk = next_pw()
            for kc in range(8):
                mm(pw[0:P, :], mT[:, kc, 0:P], w_out_bf[:, kc, half * 512:(half + 1) * 512], kc == 0, kc == 7,
                   ['mT', 'w_out_bf'], [pk])
            tt('dve', xb[0:P, half * 512:(half + 1) * 512], xb[0:P, half * 512:(half + 1) * 512], pw[0:P, :], ALU.add,
               [kx, pk], [kx])
        act(junk[0:P, :], xb[0:P, :], AF.Square, [kx], ['hn', 'ss'], accum_out=ss[0:P, :])
        ts('dve', ss[0:P], ss[0:P], 1.0 / D_MODEL, EPS, ALU.mult, ALU.add, ['ss'], ['ss'])
        act(ss[0:P], ss[0:P], AF.Sqrt, ['ss'], ['ss'])
        S.op('dve', lambda e: e.reciprocal(out=ss[0:P], in_=ss[0:P]), reads=['ss'], writes=['ss'])
        S.op('dve', lambda e: e.scalar_tensor_tensor(out=xb[0:P], in0=xb[0:P], scalar=ss[0:P, 0:1], in1=fg_sb[0:P],
                                                     op0=ALU.mult, op1=ALU.mult), reads=[kx, 'ss', 'fg'], writes=[kx])
        S.dma('sp', ydst, xb[0:P, :], reads=[kx])

    pst = ExitStack()

    def psb(name, shape, dt=F32):
        return pst.enter_context(nc.sbuf_tensor(name, list(shape), dt))

    w1p = psb('w1p', [64, 2, 32, 128], BF16)
    for c in range(2):
        w1v = w1[c].rearrange("(rs d) k -> d rs k", d=64)
        for r8 in range(4):
            load_cast(w1p[:, c, r8 * 8:(r8 + 1) * 8, :].rearrange("p a k -> p (a k)"),
                      w1v[:, r8 * 8:(r8 + 1) * 8, :], 64, 1024, 'w1p')
    for c in range(2):
        for rs_ in range(32):
            mm(PM[:, c:c + 1], w1p[:, c, rs_, :], peT_bf[:, c, rs_:rs_ + 1], (c == 0 and rs_ == 0), (c == 1 and rs_ == 31),
               ['w1p', 'peT'], ['PM'])
    tt('dve', cb[:], PM[:, 0:2], b1T_sb[:], ALU.add, ['PM', 'b1T'], ['cb'])

    KsT = [psb('KsT%d' % h, [64, SEQ], BF16) for h in range(2)]
    KwT = [psb('KwT%d' % h, [64, SEQ], BF16) for h in range(2)]
    Vsaug = psb('Vsaug', [128, NT, 2, 66], BF16)
    Vwaug = psb('Vwaug', [128, NT, 2, 66], BF16)
    kcT = psb('kcT', [64, 2, 2, 144], BF16)
    hidnew = psb('hidnew', [128, 2, 2, 8], BF16)
    hidV = psb('hidV', [128, 2, 128], BF16)
    KcmpT = psb('KcmpT', [64, 2, 128], BF16)
    vcaug = psb('vcaug', [128, 2, 98], BF16)
    qrT = psb('qrT', [64, 8, 128], BF16)
    qT = psb('qT', [64, 8, 128], BF16)
    pT = [psb('pT%d' % i, [128, 512], BF16) for i in range(3)]
    negsel = psb('negsel', [128, 2, 32])
    negselT = psb('negselT', [32, 2, 4, 128], BF16)
    sc = psb('sc', [128, 2, 32])
    sc2 = psb('sc2', [128, 32])
    m16 = psb('m16', [128, 16])
    rden = psb('rden', [128, 4])
    wgt = psb('wgt', [128, 4])
    k_cmask = const_bf(pst, 'cmask')
    k_m01c = const_bf(pst, 'm01c')
    k_Ep = const_bf(pst, 'Ep')
    k_vmP = const_f32(pst, 'vmP')
    k_acP = const_f32(pst, 'acP')
    st, sk = next_stage()
    S.dma('sp', st[:, 0:32], cd['ovP'], writes=[sk])
    S.op('pool', lambda e: e.memset(vcaug[:], 0.0), writes=['vcaug'])
    S.op('pool', lambda e: e.memset(vcaug[:, :, 64:65], 1.0), reads=['vcaug'], writes=['vcaug1'])
    for h in range(2):
        cp('dve', vcaug[:, h, 65:97], st[:, 0:32], [sk, 'vcaug'], ['vcaug_ov%d' % h])
    S.op('pool', lambda e: e.memset(Vsaug[:], 1.0), writes=['Vsaug'])
    S.op('pool', lambda e: e.memset(Vwaug[:], 1.0), writes=['Vwaug'])
    S.op('pool', lambda e: e.memset(kcT[:], 0.0), writes=['kcT'])
    S.op('pool', lambda e: e.memset(hidV[:], 0.0), writes=['hidV'])
    S.op('pool', lambda e: e.memset(hidnew[:], 0.0), writes=['hidnew'])
    S.op('pool', lambda e: e.memset(KcmpT[:], 0.0), writes=['KcmpT'])
    pti = [0]

    def next_pT():
        pti[0] = (pti[0] + 1) % 3
        return pT[pti[0]], 'pT%d' % pti[0]

    def attend(i, hk, KT, Vaug, vkey, kkey, chunks, po, pok, masks, selmask):
        rhs_q = qrT[:, hk * 4:(hk + 1) * 4, :].rearrange("p g t -> p (g t)")
        for ci, kc in enumerate(chunks):
            pw, pk = next_pw()
            mm(pw[:, :], KT[hk][:, kc * 128:(kc + 1) * 128], rhs_q, True, not selmask, [kkey, 'qrT'], [pk])
            if selmask:
                mm(pw[:, :], k_Ep[:, kc * 128:(kc + 1) * 128], negselT[:, hk].rearrange("p g t -> p (g t)"), False, True,
                   ['k_Ep', 'negselT'], [pk])
            p_t, ptk = next_pT()
            act(p_t[:, :], pw[:, :], AF.Exp, [pk], [ptk], scale=SCALE)
            if kc in masks:
                mk = masks[kc]
                tt('dve', p_t[:, :].rearrange("p (g t) -> p g t", g=4), p_t[:, :].rearrange("p (g t) -> p g t", g=4),
                   k_cmask[:, mk, :].unsqueeze(1).to_broadcast([128, 4, 128]), ALU.mult, [ptk, 'k_cmask'], [ptk])
            for g in range(4):
                mm(po[:, g * 65:(g + 1) * 65], p_t[:, g * 128:(g + 1) * 128], Vaug[:, kc, hk, 0:65],
                   (ci == 0 and g == 0), (ci == len(chunks) - 1 and g == 3), [ptk, vkey], [pok])

    def branch_out(po, pok, hk, br, width, first):
        pv = po[:, 0:4 * width].rearrange("p (g w) -> p g w", g=4)
        ts('dve', rden[:], pv[:, :, 64], 1e-30, None, ALU.max, None, [pok], ['rden'])
        S.op('dve', lambda e: e.reciprocal(out=rden[:], in_=rden[:]), reads=['rden'], writes=['rden'])
        gv = gates[:, hk * 12:(hk + 1) * 12].rearrange("p (g c) -> p g c", c=3)[:, :, br]
        tt('dve', wgt[:], rden[:], gv, ALU.mult, ['rden', 'gates'], ['wgt'])
        dst = bo_f[:, hk * 256:(hk + 1) * 256].rearrange("p (g d) -> p g d", g=4)
        wb = wgt[:, :].unsqueeze(2).to_broadcast([128, 4, 64])
        if first:
            tt('dve', dst, pv[:, :, 0:64], wb, ALU.mult, [pok, 'wgt'], ['bo_f'])
        else:
            tt('dve', tmpo[:], pv[:, :, 0:64], wb, ALU.mult, [pok, 'wgt'], ['tmpo'])
            tt('dve', dst, dst, tmpo[:], ALU.add, ['bo_f', 'tmpo'], ['bo_f'])

    def prefetch(i):
        S.dma('sp', xt[i % 2][:], xp[128 * i:128 * (i + 1), :], writes=['xt%d' % (i % 2)])
        S.dma('sp', ropts[i % 2][:], cd['ropeP'][128 * i:128 * (i + 1), :, :], writes=['ropt%d' % (i % 2)])

    prefetch(0)
    for i in range(NT):
        xb, kx = xt[i % 2], 'xt%d' % (i % 2)
        if i + 1 < NT:
            prefetch(i + 1)
        token_front(xb, kx, 128, wsT_bf, bsT_sb, ropts[i % 2], 'ropt%d' % (i % 2))
        S.dma('sp', kvp[128 * i:128 * (i + 1), :], kv_f[:, 0:512], reads=['kv_f'])
        if i >= 12:
            S.dma('sp', winp[128 * (i - 12):128 * (i - 11), :], kv_f[:, 512:768], reads=['kv_f'])
        if do_prompt_attn:
            SUB = os.environ.get('KSUB', 'ABCD123')
            if 'A' in SUB:
                tb, tk = next_tb()
                for h in range(8):
                    tr(tb[0:64, h * 128:(h + 1) * 128], qr_bf[:, h, :], idb[:], ['qr_bf', 'idb'], [tk])
                cp('dve', qrT[:].rearrange("p h t -> p (h t)"), tb[0:64, :], [tk], ['qrT'])
                tb, tk = next_tb()
                for h in range(8):
                    tr(tb[0:64, h * 128:(h + 1) * 128], q_bf[:, h, :], idb[:], ['q_bf', 'idb'], [tk])
                cp('act', qT[:].rearrange("p h t -> p (h t)"), tb[0:64, :], [tk], ['qT'])
            if 'B' in SUB:
                tb, tk = next_tb()
                srcs = [256, 320, 512, 576, 0, 64, 128, 192]
                for j, c0 in enumerate(srcs):
                    tr(tb[0:64, j * 128:(j + 1) * 128], kvb[:, c0:c0 + 64], idb[:], ['kvb', 'idb'], [tk])
                for h in range(2 if '1' in SUB else 0):
                    cp('dve', KsT[h][:, 128 * i:128 * (i + 1)], tb[0:64, h * 128:(h + 1) * 128], [tk], ['KsT'])
                    cp('dve', KwT[h][:, 128 * i:128 * (i + 1)], tb[0:64, (2 + h) * 128:(3 + h) * 128], [tk], ['KwT'])
                if '2' in SUB:
                    cp('dve', kcT[:, :, :, 16:144], tb[0:64, 512:1024].rearrange("p (c h t) -> p c h t", c=2, h=2), [tk], ['kcT'])
                if '3' in SUB:
                    cp('act', Vsaug[:, i, :, 0:64], kvb[:, 384:512].rearrange("p (h d) -> p h d", h=2), ['kvb', 'Vsaug'], ['Vsaug'])
                    cp('act', Vwaug[:, i, :, 0:64], kvb[:, 640:768].rearrange("p (h d) -> p h d", h=2), ['kvb', 'Vwaug'], ['Vwaug'])
            m0 = 1 if i == 0 else 0
            nb = 8 - m0
            if 'C' in SUB:
                for c in range(2):
                    for hk in range(2):
                        grp = c * 2 + hk
                        for r in range(2):
                            for s_ in range(16):
                                st0 = 16 * m0 + 16 * r + s_
                                mm(PM[:, grp * 8 + m0:grp * 8 + 8], w1p[:, c, r * 16 + s_, :],
                                   kcT[:, c, hk, st0:st0 + 16 * (nb - 1) + 1:16], (r == 0 and s_ == 0), (r == 1 and s_ == 15),
                                   ['w1p', 'kcT'], ['PM'])
                for c in range(2):
                    act(hidnew[:, c, :, m0:8], PM[:, c * 16:(c + 1) * 16].rearrange("p (h m) -> p h m", h=2)[:, :, m0:8],
                        AF.Gelu_apprx_tanh, ['PM', 'cb'], ['hidnew'], bias=cb[:, c:c + 1])
            n0 = 8 * i - 1 + m0
            if 'D' in SUB:
                cp('dve', hidV[:, :, n0:n0 + nb], hidnew[:, 1, :, m0:8], ['hidnew', 'hidV'], ['hidV'])
                cp('dve', kcT[:, :, :, 0:16], kcT[:, :, :, 128:144], ['kcT'], ['kcT'])
                mm(PM[0:64, 32:48], w2_bf[:, 0, :], hidnew[:, 0, :, :].rearrange("p h m -> p (h m)"), True, True,
                   ['w2_bf', 'hidnew'], ['PM'])
                cp('dve', KcmpT[:, :, n0:n0 + nb], PM[0:64, 32:48].rearrange("p (h m) -> p h m", h=2)[:, :, m0:8], ['PM'], ['KcmpT'])
                for hk in range(2):
                    mm(PM[:, 64 + hk * 64:128 + hk * 64], hidV[:, hk, :], w2_bf[:, 1, :], hk == 0, hk == 1, ['hidV', 'w2_bf'], ['PM'])
                cp('dve', vcaug[:, :, 0:64], PM[:, 64:192].rearrange("p (h d) -> p h d", h=2), ['PM', 'vcaug'], ['vcaug'])
            lvl = int(do_prompt_attn)
            if lvl == 1:
                S.op('pool', lambda e: e.memset(bo_f[:], 0.0), writes=['bo_f'])
            for hk in range(2 if lvl >= 2 else 0):
                pw, pk = next_pw()
                mm(pw[:, :], KcmpT[:, hk, :], qT[:, hk * 4:(hk + 1) * 4, :].rearrange("p g t -> p (g t)"), True, True,
                   ['KcmpT', 'qT'], [pk])
                p_t, ptk = next_pT()
                act(p_t[:, :], pw[:, :], AF.Exp, [pk], [ptk], scale=SCALE)
                tt('dve', p_t[:, :].rearrange("p (g t) -> p g t", g=4), p_t[:, :].rearrange("p (g t) -> p g t", g=4),
                   k_m01c[:, i, :].unsqueeze(1).to_broadcast([128, 4, 128]), ALU.mult, [ptk, 'k_m01c'], [ptk])
                for g in range(4):
                    mm(POc[:, g * 97:(g + 1) * 97], p_t[:, g * 128:(g + 1) * 128], vcaug[:, hk, 0:97], g == 0, g == 3,
                       [ptk, 'vcaug', 'vcaug1', 'vcaug_ov%d' % hk], ['POc'])
                branch_out(POc, 'POc', hk, 0, 97, True)
                sel = i >= 8
                if sel:
                    pv = POc[:, 0:388].rearrange("p (g w) -> p g w", g=4)
                    tt('dve', tmpo[:, :, 0:32], pv[:, :, 65:97], rden[:, :].unsqueeze(2).to_broadcast([128, 4, 32]), ALU.mult,
                       ['POc', 'rden'], ['tmpo'])
                    S.op('dve', lambda e: e.tensor_reduce(out=sc2[:], in_=tmpo[:, :, 0:32].rearrange("p g j -> p j g"),
                                                          axis=AX.X, op=ALU.add), reads=['tmpo'], writes=['sc2'])
                    tt('dve', sc2[:], sc2[:], k_vmP[:, i - 8, :], ALU.mult, ['sc2', 'k_vmP'], ['sc2'])
                    tt('dve', sc2[:], sc2[:], k_acP[:, i - 8, :], ALU.add, ['sc2', 'k_acP'], ['sc2'])
                    S.op('dve', lambda e: e.max(out=m16[:, 0:8], in_=sc2[:]), reads=['sc2'], writes=['m16'])
                    S.op('dve', lambda e: e.match_replace(out=sc[:, 0, :], in_to_replace=m16[:, 0:8], in_values=sc2[:],
                                                          imm_value=-3.0e38), reads=['sc2', 'm16'], writes=['sc'])
                    S.op('dve', lambda e: e.max(out=m16[:, 8:16], in_=sc[:, 0, :]), reads=['sc'], writes=['m16'])
                    ts('dve', negsel[:, hk, :], sc2[:], m16[:, 15:16], NEGM, ALU.is_lt, ALU.mult, ['sc2', 'm16'], ['negsel'])
                    tr(PM[0:32, 256:384], negsel[:, hk, :], idf[:], ['negsel', 'idf'], ['PM'])
                    cp('dve', negselT[:, hk], PM[0:32, 256:384].unsqueeze(1).to_broadcast([32, 4, 128]), ['PM'], ['negselT'])
                if lvl >= 3:
                    attend(i, hk, KsT, Vsaug, 'Vsaug', 'KsT', list(range(i + 1)), POs, 'POs', {i: 0}, sel)
                    branch_out(POs, 'POs', hk, 1, 65, False)
                if lvl < 4:
                    continue
                wch = list(range(max(0, i - 4), i + 1))
                wm = {i: 0}
                if i >= 4:
                    wm[i - 4] = 1
                attend(i, hk, KwT, Vwaug, 'Vwaug', 'KwT', wch, POw, 'POw', wm, False)
                branch_out(POw, 'POw', hk, 2, 65, False)
            tt('dve', merged[:, 512:1024], bo_f[:], szb[:], ALU.mult, ['bo_f', 'szb'], ['merged_b'])
        else:
            S.op('pool', lambda e: e.memset(merged[:, 512:1024], 0.0), writes=['merged_b'])
        token_back(xb, kx, 128, yp[128 * i:128 * (i + 1), :])

    S.barrier()
    pst.close()

    sst = ExitStack()

    def ssb(name, shape, dt=F32):
        return sst.enter_context(nc.sbuf_tensor(name, list(shape), dt))

    xb, kx = xt[0], 'xt0'
    S.dma('sp', xb[0:32, :], xs[:, :], writes=[kx])
    S.dma('sp', ropts[0][0:32], cd['ropeS'], writes=['ropt0'])
    token_front(xb, kx, 32, wsST_bf, bsS_sb, ropts[0], 'ropt0')
    S.dma('sp', kvs[:, :], kv_f[0:32, 0:512], reads=['kv_f'])
    S.dma('sp', vso[:, :], v_g[0:32, :], reads=['v_g'])
    for b in range(4):
        S.dma('sp', wins[b, 0:504, :], swin[b, 8:512, :])
        S.dma('sp', wins[b, 504:512, :], kv_f[8 * b:8 * b + 8, 512:768], reads=['kv_f'])

    if do_sample_attn:
        k_Amat = const_bf(sst, 'Amat')
        k_mnew = const_bf(sst, 'mnew')
        k_mwin0 = const_bf(sst, 'mwin0')
        k_Gm = const_f32(sst, 'Gm')
        k_vmS = const_f32(sst, 'vmS')
        k_acS = const_f32(sst, 'acS')
        k_nf = const_f32(sst, 'notfirst')
        k_pmod = const_f32(sst, 'pmod')
        w1s = ssb('w1s', [128, 2, 16, 128], BF16)
        for c in range(2):
            w1v = w1[c].rearrange("(rsp sd) k -> sd rsp k", sd=128)
            for a8 in range(2):
                load_cast(w1s[:, c, a8 * 8:(a8 + 1) * 8, :].rearrange("p a k -> p (a k)"), w1v[:, a8 * 8:(a8 + 1) * 8, :],
                          128, 1024, 'w1s')
        vcaugS = ssb('vcaugS', [128, 8, 386], BF16)
        S.op('pool', lambda e: e.memset(vcaugS[:, :, 128:129], 1.0), writes=['vcS1'])
        st, sk = next_stage()
        for ch in range(8):
            st, sk = next_stage()
            S.dma('sp', st[:, 0:257], cd['ovS'][:, ch, :], writes=[sk])
            cp('dve', vcaugS[:, ch, 129:386], st[:, 0:257], [sk], ['vcS_ov'])
        g1 = [ssb('g1_%d' % i, [128, 2, 256]) for i in range(4)]
        gint = [ssb('gint%d' % i, [128, 4, 2, 64], BF16) for i in range(2)]
        idx_i = [ssb('idx_i%d' % s2_, [128, 256], I32) for s2_ in range(2)]
        ptl_sb = ssb('ptl_sb', [128, 256], I32)
        XTg = [ssb('XTg%d' % i, [128, 4, 8, 65], BF16) for i in range(2)]
        hidS = ssb('hidS', [128, 2, 2, 64], BF16)
        KcmpS = ssb('KcmpS', [128, 1024], BF16)
        qblk_r = ssb('qblk_r', [128, 4, 64], BF16)
        qblk_u = ssb('qblk_u', [128, 4, 64], BF16)
        KnT = ssb('KnT', [128, 2, 32], BF16)
        Vn = ssb('Vn', [32, 2, 130], BF16)
        wbuf = ssb('wbuf', [128, 4, 256])
        KwS = ssb('KwS', [128, 512], BF16)
        VwS = ssb('VwS', [128, 4, 130], BF16)
        pTw = ssb('pTw', [128, 5, 64], BF16)
        PcS = ssb('PcS', [128, 8, 64], BF16)
        impn = ssb('impn', [64, 257])
        scS = ssb('scS', [64, 257])
        scS2 = ssb('scS2', [64, 257])
        m16S = ssb('m16S', [64, 16])
        negS = ssb('negS', [64, 256])
        negST = ssb('negST', [128, 2, 64], BF16)
        KsTt = [ssb('KsTt%d' % i, [128, 2, 128], BF16) for i in range(2)]
        Vst = [ssb('Vst%d' % i, [128, 2, 130], BF16) for i in range(8)]
        pTs = [ssb('pTs%d' % i, [128, 8, 64], BF16) for i in range(2)]
        pTn = ssb('pTn', [32, 64], BF16)
        Ocomb = ssb('Ocomb', [64, 64])
        Otmp = ssb('Otmp', [64, 64])
        gq = ssb('gq', [64, 3])
        rdS = ssb('rdS', [64, 1])
        wgS = ssb('wgS', [64, 1])

        S.dma('sp', ptl_sb[:], ptl[:, :], writes=['ptl'])
        idxf = ssb('idxf', [128, 256])
        ts('dve', idxf[:], ptl_sb[:], 64.0, k_pmod[:, 0:1], ALU.mult, ALU.add, ['ptl', 'k_pmod'], ['idxf'])
        for s2_ in range(2):
            ts('dve', idx_i[s2_][:], idxf[:], float(s2_), None, ALU.add, None, ['idxf'], ['idx'])
        S.op('pool', lambda e: e.memset(qblk_r[:], 0.0), writes=['qblk_r'])
        S.op('pool', lambda e: e.memset(qblk_u[:], 0.0), writes=['qblk_u'])
        S.op('pool', lambda e: e.memset(Vn[:], 1.0), writes=['Vn'])
        S.op('pool', lambda e: e.memset(VwS[:], 1.0), writes=['VwS'])
        for i in range(8):
            S.op('pool', lambda e: e.memset(Vst[i][:], 1.0), writes=['Vst%d' % i])
        for i in range(2):
            S.op('pool', lambda e: e.memset(XTg[i][:], 0.0), writes=['XTg%d' % i])
        for (src, dst, skey, dkey) in ((qr_bf, qblk_r, 'qr_bf', 'qblk_r'), (q_bf, qblk_u, 'q_bf', 'qblk_u')):
            tb, tk = next_tb()
            for h in range(8):
                hk_, g = h // 4, h % 4
                tr(tb[hk_ * 64:(hk_ + 1) * 64, g * 32:(g + 1) * 32], src[0:32, h, :], idb[0:32, 0:32], [skey, 'idb'], [tk])
            tv = tb[:, 0:128].rearrange("p (g b t) -> p b g t", g=4, b=4)
            dv = dst[:].rearrange("p b (h g t) -> p b h g t", h=2, g=4)
            for b in range(4):
                cp('dve', dv[0:64, b, 0], tv[0:64, b], [tk, dkey], [dkey])
                cp('dve', dv[64:128, b, 1], tv[64:128, b], [tk, dkey], [dkey])
        tb, tk = next_tb()
        tr(tb[:, 0:32], kvb[0:32, 256:384], idb[0:32, 0:32], ['kvb', 'idb'], [tk])
        tr(tb[:, 32:64], kvb[0:32, 512:640], idb[0:32, 0:32], ['kvb', 'idb'], [tk])
        cp('dve', KnT[:].rearrange("p a t -> p (a t)"), tb[:, 0:64], [tk], ['KnT'])
        cp('dve', Vn[:, 0, 0:128], kvb[0:32, 384:512], ['kvb', 'Vn'], ['Vn'])
        cp('dve', Vn[:, 1, 0:128], kvb[0:32, 640:768], ['kvb', 'Vn'], ['Vn'])

        def s_branch(po, pok, br, first):
            ts('dve', rdS[:], po[0:64, 128:129], 1e-30, None, ALU.max, None, [pok], ['rdS'])
            S.op('dve', lambda e: e.reciprocal(out=rdS[:], in_=rdS[:]), reads=['rdS'], writes=['rdS'])
            tt('dve', wgS[:], rdS[:], gq[:, br:br + 1], ALU.mult, ['rdS'] + ['gq%d' % h for h in range(8)], ['wgS'])
            for hk in range(2):
                rows = slice(hk * 32, (hk + 1) * 32)
                src = po[rows, hk * 64:(hk + 1) * 64]
                if first:
                    ts('dve', Ocomb[rows, :], src, wgS[rows, 0:1], None, ALU.mult, None, [pok, 'wgS'], ['Ocomb'])
                else:
                    ts('dve', Otmp[rows, :], src, wgS[rows, 0:1], None, ALU.mult, None, [pok, 'wgS'], ['Otmp'])
                    tt('dve', Ocomb[rows, :], Ocomb[rows, :], Otmp[rows, :], ALU.add, ['Ocomb', 'Otmp'], ['Ocomb'])

        gi = [0]
        SIMG = os.environ.get('KSIMGATHER', '0') == '1'

        def gather(b, j, half):
            buf = g1[gi[0] % 4]
            key = 'g1_%d' % (gi[0] % 4)
            gi[0] += 1
            col = b * 64 + j
            S.dma('pool', None, None, reads=['idx'], writes=[key + 'a0'],
                  fn=lambda e: e.indirect_dma_start(out=buf[:].rearrange("p a b -> p (a b)"), out_offset=None,
                                                    in_=cacheH[half][:, :],
                                                    in_offset=bass.IndirectOffsetOnAxis(ap=idx_i[0][:, col:col + 1], axis=0)))
            for kk in ('a1', 'b0', 'b1'):
                S.lastw[key + kk] = S.lastw[key + 'a0']
                S.readers[key + kk] = {}
            S._wait('pool', S.lastw[key + 'a0'])
            return buf, key

        for b in range(4):
            for h in range(8):
                S.dma('sp', gq[h * 8:(h + 1) * 8, :], gates[8 * b:8 * b + 8, h * 3:(h + 1) * 3], reads=['gates'], writes=['gq%d' % h])
            S.dma('sp', wbuf[:], swin[b].rearrange("(c p) f -> p c f", p=128), writes=['wbuf'])
            pw, pk = next_pw()
            for ch in range(4):
                tr(pw[:, ch * 128:(ch + 1) * 128], wbuf[:, ch, 0:128], idf[:], ['wbuf', 'idf'], [pk])
            cp('dve', KwS[:], pw[:, :], [pk], ['KwS'])
            cp('act', VwS[:, :, 0:128], wbuf[:, :, 128:256], ['wbuf', 'VwS'], ['VwS'])
            pw, pk = next_pw()
            for ch in range(4):
                mm(pw[:, ch * 64:(ch + 1) * 64], KwS[:, ch * 128:(ch + 1) * 128], qblk_r[:, b, :], ch == 0, False,
                   ['KwS', 'qblk_r'], [pk])
            mm(pw[0:32, 256:320], KnT[:, 1, :], qblk_r[:, b, :], False, True, ['KnT', 'qblk_r'], [pk])
            act(pTw[:, 0:4, :].rearrange("p c q -> p (c q)"), pw[:, 0:256], AF.Exp, [pk], ['pTw'], scale=SCALE)
            act(pTw[0:32, 4, :], pw[0:32, 256:320], AF.Exp, [pk], ['pTw4'], scale=SCALE)
            tt('dve', pTw[:, 0, :], pTw[:, 0, :], k_mwin0[:], ALU.mult, ['pTw', 'k_mwin0'], ['pTw'])
            tt('dve', pTw[0:32, 4, :], pTw[0:32, 4, :], k_mnew[:, b, :], ALU.mult, ['pTw4', 'k_mnew'], ['pTw4'])
            for ch in range(4):
                mm(POw[0:64, 0:129], pTw[:, ch, :], VwS[:, ch, 0:129], ch == 0, False, ['pTw', 'VwS'], ['POw'])
            mm(POw[0:64, 0:129], pTw[0:32, 4, :], Vn[:, 1, 0:129], False, True, ['pTw4', 'Vn'], ['POw'])
            slv = int(do_sample_attn)
            for G in range(16 if slv >= 2 else 0):
                xg, xk = XTg[G % 2], 'XTg%d' % (G % 2)
                for jj in range(4):
                    j = G * 4 + jj
                    buf, key = gather(b, j, 0)
                    gi_t, gik = gint[j % 2], 'gint%d' % (j % 2)
                    for s2 in range(2):
                        cp('act' if s2 == 0 else 'dve', gi_t[:, :, s2, :], buf[:, s2, :].rearrange("p (q d) -> p q d", d=64),
                           [key + 'a0', key + 'a1', key + 'b0', key + 'b1', gik + 'x%d' % (1 - s2)], [gik + 'x%d' % s2])
                    tb, tk = next_tb()
                    for q4 in range(4):
                        tr(tb[:, q4 * 128:(q4 + 1) * 128], gi_t[:, q4, :, :].rearrange("p s d -> p (s d)"), idb[:],
                           [gik + 'x0', gik + 'x1', 'idb'], [tk])
                    for q4 in range(4):
                        cp('act' if j % 2 == 0 else 'dve',
                           xg[:, q4, :, 1 + 16 * jj:17 + 16 * jj].rearrange("p sp sg -> p sg sp"),
                           tb[:, q4 * 128:(q4 + 1) * 128].rearrange("p (sg sp) -> p sg sp", sp=8), [tk, xk], [xk])
                for q4 in range(4):
                    c = q4 // 2
                    for r in range(2):
                        for sp in range(8):
                            mm(PM[:, q4 * 64:(q4 + 1) * 64], w1s[:, c, r * 8 + sp, :], xg[:, q4, sp, r:r + 64],
                               (r == 0 and sp == 0), (r == 1 and sp == 7), ['w1s', xk], ['PM'])
                xn_, xnk = XTg[(G + 1) % 2], 'XTg%d' % ((G + 1) % 2)
                cp('dve', xn_[:, :, :, 0:1], xg[:, :, :, 64:65], [xk, xnk], [xnk])
                for c in range(2):
                    act(hidS[:, c].rearrange("p h n -> p (h n)"), PM[:, c * 128:(c + 1) * 128], AF.Gelu_apprx_tanh,
                        ['PM', 'cb'], ['hidS'], bias=cb[:, c:c + 1])
                for hk in range(2):
                    mm(PM[hk * 64:(hk + 1) * 64, 256:320], w2_bf[:, 0, :], hidS[:, 0, hk, :], True, True,
                       ['w2_bf', 'hidS'], ['PM'])
                cp('dve', KcmpS[:, 64 * G:64 * (G + 1)], PM[:, 256:320], ['PM'], ['KcmpS'])
                half = (G % 2) * 64
                for hk in range(2):
                    mm(PM[half:half + 64, 320 + hk * 64:384 + hk * 64], hidS[:, 1, hk, :], w2_bf[:, 1, :], True, True,
                       ['hidS', 'w2_bf'], ['PM'])
                cp('act', vcaugS[half:half + 64, G // 2, 0:128], PM[half:half + 64, 320:448], ['PM', 'vcS'], ['vcS'])
            if slv >= 3:
                pw, pk = next_pw()
                for ch in range(8):
                    mm(pw[:, ch * 64:(ch + 1) * 64], KcmpS[:, ch * 128:(ch + 1) * 128], qblk_u[:, b, :], ch == 0, ch == 7,
                       ['KcmpS', 'qblk_u'], [pk])
                act(PcS[:].rearrange("p c q -> p (c q)"), pw[:, :], AF.Exp, [pk], ['PcS'], scale=SCALE)
                ts('dve', PcS[:, 0, :], PcS[:, 0, :], k_nf[:, 0:1], None, ALU.mult, None, ['PcS', 'k_notfirst'], ['PcS'])
                for ch in range(8):
                    mm(POc[0:64, 0:386], PcS[:, ch, :], vcaugS[:, ch, :], ch == 0, ch == 7, ['PcS', 'vcS', 'vcS1', 'vcS_ov'], ['POc'])
                ts('dve', rdS[:], POc[0:64, 128:129], 1e-30, None, ALU.max, None, ['POc'], ['rdS'])
                S.op('dve', lambda e: e.reciprocal(out=rdS[:], in_=rdS[:]), reads=['rdS'], writes=['rdS'])
                ts('dve', impn[:], POc[0:64, 129:386], rdS[:, 0:1], None, ALU.mult, None, ['POc', 'rdS'], ['impn'])
                s_branch(POc, 'POc', 0, True)
                pw, pk = next_pw()
                mm(pw[0:64, 0:257], k_Gm[:], impn[:], True, True, ['k_Gm', 'impn'], [pk])
                tt('dve', scS[:], pw[0:64, 0:257], k_vmS[:], ALU.mult, [pk, 'k_vmS'], ['scS'])
                tt('dve', scS[:], scS[:], k_acS[:], ALU.add, ['scS', 'k_acS'], ['scS'])
                S.op('dve', lambda e: e.max(out=m16S[:, 0:8], in_=scS[:]), reads=['scS'], writes=['m16S'])
                S.op('dve', lambda e: e.match_replace(out=scS2[:], in_to_replace=m16S[:, 0:8], in_values=scS[:], imm_value=-3.0e38),
                     reads=['scS', 'm16S'], writes=['scS2'])
                S.op('dve', lambda e: e.max(out=m16S[:, 8:16], in_=scS2[:]), reads=['scS2'], writes=['m16S'])
                ts('dve', negS[:], scS[:, 0:256], m16S[:, 15:16], NEGM, ALU.is_lt, ALU.mult, ['scS', 'm16S'], ['negS'])
                pw, pk = next_pw()
                for hf in range(2):
                    tr(pw[:, hf * 64:(hf + 1) * 64], negS[:, hf * 128:(hf + 1) * 128], idf[0:64, 0:64], ['negS', 'idf'], [pk])
                cp('dve', negST[:].rearrange("p a q -> p (a q)"), pw[:, 0:128], [pk], ['negST'])
            if slv >= 4:
                for j in range(64):
                    buf, key = gather(b, j, 1)
                    kt, ktk = KsTt[j % 2], 'KsTt%d' % (j % 2)
                    vt, vtk = Vst[j % 8], 'Vst%d' % (j % 8)
                    pw, pk = next_pw()
                    for s2 in range(2):
                        tr(pw[:, s2 * 128:(s2 + 1) * 128], buf[:, s2, 0:128], idf[:], [key + 'a0', key + 'a1', key + 'b0', key + 'b1', 'idf'], [pk])
                    cp('act', kt[:].rearrange("p a k -> p (a k)"), pw[:, 0:256], [pk], [ktk])
                    cp('dve', vt[:, :, 0:128], buf[:, :, 128:256], [key + 'a0', key + 'a1', key + 'b0', key + 'b1', vtk], [vtk])
                    grp = j // 4
                    pt_, ptk = pTs[grp % 2], 'pTs%d' % (grp % 2)
                    for s2 in range(2):
                        slot = (j % 4) * 2 + s2
                        mm(POs[:, slot * 64:(slot + 1) * 64], kt[:, s2, :], qblk_r[:, b, :], slot == 0, False,
                           [ktk, 'qblk_r'], ['POs'])
                        mm(POs[:, slot * 64:(slot + 1) * 64], k_Amat[:, 128 * (j % 32):128 * (j % 32 + 1)], negST[:, j // 32, :],
                           False, slot == 7, ['k_Amat', 'negST'], ['POs'])
                    if j % 4 == 3:
                        act(pt_[:].rearrange("p c q -> p (c q)"), POs[:, :], AF.Exp, ['POs'], [ptk], scale=SCALE)
                        for jj in range(4):
                            j2 = j - 3 + jj
                            for s2 in range(2):
                                slot = jj * 2 + s2
                                mm(POc[0:64, 0:129], pt_[:, slot, :], Vst[j2 % 8][:, s2, 0:129], (j2 == 0 and s2 == 0), False,
                                   [ptk, 'Vst%d' % (j2 % 8)], ['POc'])
                pw, pk = next_pw()
                mm(pw[0:32, 0:64], KnT[:, 0, :], qblk_r[:, b, :], True, True, ['KnT', 'qblk_r'], [pk])
                act(pTn[:], pw[0:32, 0:64], AF.Exp, [pk], ['pTn'], scale=SCALE)
                tt('dve', pTn[:], pTn[:], k_mnew[:, b, :], ALU.mult, ['pTn', 'k_mnew'], ['pTn'])
                mm(POc[0:64, 0:129], pTn[:], Vn[:, 0, 0:129], False, True, ['pTn', 'Vn'], ['POc'])
                s_branch(POc, 'POc', 1, False)
            s_branch(POw, 'POw', 2, slv < 3)
            for h in range(8):
                S.dma('sp', bo_f[8 * b:8 * b + 8, h * 64:(h + 1) * 64], Ocomb[h * 8:(h + 1) * 8, :], reads=['Ocomb'],
                      writes=['bo_f%d' % (b * 8 + h)])
        tt('dve', merged[0:32, 512:1024], bo_f[0:32], szb[0:32], ALU.mult, ['szb'] + ['bo_f%d' % k for k in range(32)],
           ['merged_b'])
    else:
        S.op('pool', lambda e: e.memset(merged[0:32, 512:1024], 0.0), writes=['merged_b'])
    token_back(xb, kx, 32, ys[:, :])
    S.finish()
    sst.close()
    setup_stack.close()
    return nc, consts


_PROG = {}


def kernel(x_prompt, x_sample, cache_kv, state_win, page_table, norm_g, w_in, ln_g, ln_b, w_s, b_s, cmp_pos,
           w_cmp1, b_cmp1, w_cmp2, w_out, final_g, _flags=(4, 4)):
    f32 = np.float32
    if _flags not in _PROG:
        _PROG[_flags] = build(*_flags)
    nc, consts = _PROG[_flags]
    x_prompt = np.asarray(x_prompt, f32)
    x_sample = np.asarray(x_sample, f32)
    cache = np.asarray(cache_kv, f32).reshape(N_PHYS * 128, 2, 256)
    state_win = np.asarray(state_win, f32)
    page_table = np.asarray(page_table, np.int32)
    shared = {
        'cache0': np.ascontiguousarray(cache[:, 0, :]).reshape(N_PHYS * 64, 512),
        'cache1': np.ascontiguousarray(cache[:, 1, :]).reshape(N_PHYS * 64, 512),
        'gT': np.ascontiguousarray(np.asarray(norm_g, f32).reshape(8, 128).T),
        'w_in': np.asarray(w_in, f32).reshape(D_MODEL, D_IN),
        'w_out': np.asarray(w_out, f32).reshape(D_MODEL, D_MODEL),
        'lngb': np.ascontiguousarray(np.broadcast_to(
            np.stack([np.asarray(ln_g, f32).reshape(512), np.asarray(ln_b, f32).reshape(512)])[None], (128, 2, 512))),
        'fgb': np.ascontiguousarray(np.broadcast_to(np.asarray(final_g, f32).reshape(1, D_MODEL), (128, D_MODEL))),
        'w_s': np.asarray(w_s, f32).reshape(8, 128, 128),
        'bsT': np.ascontiguousarray(np.asarray(b_s, f32).reshape(8, 128).T),
        'bsS': np.ascontiguousarray(np.tile(np.asarray(b_s, f32).reshape(8, 128)[:, 0:8].T, (4, 1))),
        'peT': np.ascontiguousarray(np.asarray(cmp_pos, f32).reshape(2, 32, 64).transpose(2, 0, 1)),
        'w1': np.asarray(w_cmp1, f32).reshape(2, 2048, 128),
        'b1T': np.ascontiguousarray(np.asarray(b_cmp1, f32).reshape(2, 128).T),
        'w2': np.asarray(w_cmp2, f32).reshape(2, 128, 64),
    }
    for k, v in consts.items():
        shared['c_' + k] = v
    in_maps = []
    for c in range(8):
        m = dict(shared)
        m['xp'] = x_prompt[c]
        m['xs'] = np.ascontiguousarray(x_sample[4 * c:4 * c + 4].reshape(32, D_MODEL))
        m['swin'] = np.ascontiguousarray(state_win[0, 4 * c:4 * c + 4].reshape(4, 512, 256))
        pt = page_table[4 * c:4 * c + 4]
        ptl = pt.reshape(4, 64, 2).transpose(2, 0, 1).reshape(2, 256)
        m['ptl'] = np.ascontiguousarray(np.repeat(ptl, 64, axis=0).astype(np.int32))
        in_maps.append(m)
    res = run_bass_kernel_spmd(nc, in_maps, core_ids=list(range(8)))
    r = res.results
    y_prompt = np.stack([r[c]['yp'] for c in range(8)]).reshape(8, SEQ, D_MODEL)
    y_sample = np.concatenate([r[c]['ys'].reshape(4, 8, D_MODEL) for c in range(8)], 0)
    kv_p = np.stack([r[c]['kvp'] for c in range(8)]).reshape(1, 8, SEQ, 4, 2, 64)
    win_p = np.stack([r[c]['winp'] for c in range(8)]).reshape(1, 8, 512, 2, 2, 64)
    kv_s = np.concatenate([r[c]['kvs'].reshape(4, 8, 4, 2, 64) for c in range(8)], 0).reshape(1, 32, 8, 4, 2, 64)
    win_s = np.concatenate([r[c]['wins'] for c in range(8)], 0).reshape(1, 32, 512, 2, 2, 64)
    v_s = np.concatenate([r[c]['vso'].reshape(4, 8, 512) for c in range(8)], 0).reshape(1, 32, 8, 512)
    return (y_prompt.astype(f32), y_sample.astype(f32), kv_p.astype(f32), win_p.astype(f32), kv_s.astype(f32),
            win_s.astype(f32), v_s.astype(f32))
```

```python
from contextlib import ExitStack
import os
import numpy as np
import concourse.bass as bass
import concourse.mybir as mybir
from concourse.bass_utils import run_bass_kernel_spmd

F32 = mybir.dt.float32
BF16 = mybir.dt.bfloat16
I32 = mybir.dt.int32
AF = mybir.ActivationFunctionType
ALU = mybir.AluOpType
AX = mybir.AxisListType
NDMASEM = 40
NSWSEM = 4

D_MODEL = 1024
SEQ = 2048
NT = 16
D_IN = 3352
PAST = 16384
EPS = 1e-6
SCALE = 0.125
NEGM = -30000.0
FORCE = 1e9
NEG = -1e30
N_PHYS = 5120
CHUNKS = [(0, 512), (512, 1024), (1024, 1536), (1536, 2048), (2048, 2560), (2560, 2840), (2840, 3352)]


class Sched:
    def __init__(self, nc):
        self.nc = nc
        self.eng = {'pe': nc.tensor, 'act': nc.scalar, 'dve': nc.vector, 'pool': nc.gpsimd, 'sp': nc.sync}
        self.sem = {k: nc.alloc_semaphore('sem_' + k) for k in self.eng}
        self.cnt = {k: 0 for k in self.eng}
        self.waited = {k: {} for k in self.eng}
        self.lastw = {}
        self.readers = {}
        self.dma_sems = [nc.alloc_semaphore('dq%d' % i) for i in range(NDMASEM)]
        self.dma_val = [0] * NDMASEM
        self.dma_next = 0
        self.sw_next = 0

    def _wait(self, e, ev):
        sem, val, name = ev
        w = self.waited[e]
        if w.get(name, 0) >= val:
            return
        self.eng[e].wait_ge(sem, val)
        w[name] = val

    def _deps(self, e, reads, writes):
        best = {}

        def add(ev):
            if ev[2] not in best or best[ev[2]][1] < ev[1]:
                best[ev[2]] = ev
        for k in reads:
            if k in self.lastw:
                add(self.lastw[k])
        for k in writes:
            if k in self.lastw:
                add(self.lastw[k])
            for ev in self.readers.get(k, {}).values():
                add(ev)
        for name, ev in best.items():
            if name == 'pe' and e == 'pe':
                continue
            self._wait(e, ev)

    def _record(self, ev, reads, writes):
        for k in reads:
            d = self.readers.setdefault(k, {})
            d[ev[2]] = ev
        for k in writes:
            self.lastw[k] = ev
            self.readers[k] = {}

    def op(self, e, fn, reads=(), writes=()):
        self._deps(e, reads, writes)
        inst = fn(self.eng[e])
        self.cnt[e] += 1
        inst.then_inc(self.sem[e], 1)
        self._record((self.sem[e], self.cnt[e], e), reads, writes)

    def dma(self, e, out, in_, reads=(), writes=(), fn=None):
        self._deps(e, reads, writes)
        if e == 'pool':
            i = NDMASEM - NSWSEM + self.sw_next
            self.sw_next = (self.sw_next + 1) % NSWSEM
        else:
            i = self.dma_next
            self.dma_next = (i + 1) % (NDMASEM - NSWSEM)
        sem = self.dma_sems[i]
        name = 'dq%d' % i
        if self.dma_val[i] > 0:
            self._wait(e, (sem, self.dma_val[i], name))
        if fn is None:
            inst = self.eng[e].dma_start(out=out, in_=in_)
        else:
            inst = fn(self.eng[e])
        self.dma_val[i] += 16
        inst.then_inc(sem, 16)
        self._record((sem, self.dma_val[i], name), reads, writes)

    def barrier(self):
        evs = [(self.sem[k], self.cnt[k], k) for k in self.eng if self.cnt[k] > 0]
        evs += [(self.dma_sems[i], self.dma_val[i], 'dq%d' % i) for i in range(NDMASEM) if self.dma_val[i] > 0]
        for e in self.eng:
            for ev in evs:
                if ev[2] != e:
                    self._wait(e, ev)

    def finish(self):
        for i in range(NDMASEM):
            if self.dma_val[i] > 0:
                self._wait('sp', (self.dma_sems[i], self.dma_val[i], 'dq%d' % i))
        for k in self.eng:
            if k != 'sp' and self.cnt[k] > 0:
                self._wait('sp', (self.sem[k], self.cnt[k], k))


def _rope_tab(pos):
    half = 32
    inv = (10000.0 ** (-np.arange(half, dtype=np.float64) / half)).astype(np.float32)
    ang = pos.astype(np.float32)[:, None] * inv[None, :]
    cos = np.cos(ang.astype(np.float64)).astype(np.float32)
    sin = np.sin(ang.astype(np.float64)).astype(np.float32)
    return np.stack([np.concatenate([cos, cos], -1), np.concatenate([-sin, sin], -1)], 1).astype(np.float32)


def _host_consts():
    c = {}
    c['ropeP'] = _rope_tab(np.arange(SEQ))
    c['ropeS'] = _rope_tab(PAST + np.tile(np.arange(8), 4))
    k = np.arange(128)[:, None]
    t = np.arange(128)[None, :]
    cm = np.zeros((128, 2, 128), np.float32)
    cm[:, 0] = (k <= t)
    cm[:, 1] = (k > t)
    c['cmask'] = cm
    n = np.arange(128)[:, None, None]
    i = np.arange(NT)[None, :, None]
    tq = np.arange(128)[None, None, :]
    c['m01c'] = ((16 * n + 31 <= 128 * i + tq) & (n < 127)).astype(np.float32)
    blk = np.arange(32)[:, None]
    key = np.arange(SEQ)[None, :]
    c['Ep'] = (key // 64 == blk).astype(np.float32)
    nn = np.arange(128)[:, None]
    j = np.arange(32)[None, :]
    c['ovP'] = ((16 * nn <= 64 * j + 63) & (16 * nn + 31 >= 64 * j) & (nn < 127)).astype(np.float32)
    tt = (128 * np.arange(8, 16)[None, :, None] + np.arange(128)[:, None, None])
    jj = np.arange(32)[None, None, :]
    tb = tt // 64
    forced = (jj == 0) | (jj == tb) | (jj == tb - 1)
    valid = jj * 64 <= tt
    c['vmP'] = (valid & ~forced).astype(np.float32)
    c['acP'] = np.where(forced, FORCE, np.where(valid, 0.0, NEG)).astype(np.float32)
    p = np.arange(128)[:, None]
    x = np.arange(2048)[None, :]
    c['Amat'] = (x // 16 == p).astype(np.float32)
    npr = (np.arange(8)[None, :, None] * 128 + np.arange(128)[:, None, None])
    ns = npr - 1
    js = np.arange(257)[None, None, :]
    c['ovS'] = ((16 * ns <= 64 * js + 63) & (16 * ns + 31 >= 64 * js) & (ns >= 0) & (ns <= 1022)).astype(np.float32)
    q = np.arange(64)
    hk, tqq = q // 32, q % 8
    c['Gm'] = ((hk[:, None] == hk[None, :]) & (tqq[:, None] == tqq[None, :])).astype(np.float32)
    fj = np.zeros(257, bool)
    fj[[0, 255, 256]] = True
    c['vmS'] = np.broadcast_to((~fj).astype(np.float32)[None, :], (64, 257)).copy()
    c['acS'] = np.broadcast_to(np.where(fj, FORCE, 0.0).astype(np.float32)[None, :], (64, 257)).copy()
    kp = np.arange(32)
    bp, tp = kp // 8, kp % 8
    mn = np.zeros((32, 4, 64), np.float32)
    for b in range(4):
        mn[:, b, :] = ((bp[:, None] == b) & (tp[:, None] <= tqq[None, :]))
    c['mnew'] = mn
    c['mwin0'] = (np.arange(128)[:, None] > tqq[None, :]).astype(np.float32)
    nf = np.ones((128, 1), np.float32)
    nf[0, 0] = 0.0
    c['notfirst'] = nf
    c['pmod'] = (np.arange(128) % 32).astype(np.float32)[:, None]
    return c


def build(do_prompt_attn=True, do_sample_attn=True):
    nc = bass.Bass("TRN2", target_bir_lowering=False)
    S = Sched(nc)
    consts = _host_consts()

    def din(name, shape, dt=F32):
        return nc.dram_tensor(name, list(shape), dt, kind="ExternalInput").ap()

    def dout(name, shape, dt=F32):
        return nc.dram_tensor(name, list(shape), dt, kind="ExternalOutput").ap()

    def sb(name, shape, dt=F32):
        return nc.alloc_sbuf_tensor(name, list(shape), dt)

    xp = din('xp', [SEQ, D_MODEL])
    xs = din('xs', [32, D_MODEL])
    cacheH = [din('cache%d' % h_, [N_PHYS * 32, 1024]) for h_ in range(2)]
    swin = din('swin', [4, 512, 256])
    ptl = din('ptl', [128, 128], I32)
    gT = din('gT', [128, 8])
    w_in = din('w_in', [D_MODEL, D_IN])
    w_out = din('w_out', [D_MODEL, D_MODEL])
    lngb = din('lngb', [128, 2, 512])
    fgb = din('fgb', [128, D_MODEL])
    w_s = din('w_s', [8, 128, 128])
    bsT = din('bsT', [128, 8])
    bsS = din('bsS', [32, 8])
    peT = din('peT', [64, 2, 32])
    w1 = din('w1', [2, 2048, 128])
    b1T = din('b1T', [128, 2])
    w2 = din('w2', [2, 128, 64])
    cd = {k: din('c_' + k, v.shape) for k, v in consts.items()}

    yp = dout('yp', [SEQ, D_MODEL])
    ys = dout('ys', [32, D_MODEL])
    kvp = dout('kvp', [SEQ, 512])
    winp = dout('winp', [512, 256])
    kvs = dout('kvs', [32, 512])
    wins = dout('wins', [4, 512, 256])
    vso = dout('vso', [32, 512])

    PW = [nc.alloc_psum_tensor('PW%d' % i, [128, 512], F32) for i in range(2)]
    TB = [nc.alloc_psum_tensor('TB%d' % i, [128, 1024], BF16) for i in range(2)]
    POc = nc.alloc_psum_tensor('POc', [128, 512], F32)
    POs = nc.alloc_psum_tensor('POs', [128, 512], F32)
    POw = nc.alloc_psum_tensor('POw', [128, 512], F32)
    PM = nc.alloc_psum_tensor('PM', [128, 512], F32)
    pwi = [0]
    tbi = [0]

    def next_pw():
        pwi[0] ^= 1
        return PW[pwi[0]], 'PW%d' % pwi[0]

    def next_tb():
        tbi[0] ^= 1
        return TB[tbi[0]], 'TB%d' % tbi[0]

    def mm(out, lhsT, rhs, start, stop, reads, writes):
        S.op('pe', lambda e: e.matmul(out, lhsT=lhsT, rhs=rhs, start=start, stop=stop, skip_group_check=True),
             reads=reads, writes=writes)

    def tr(out, in_, ident, reads, writes):
        S.op('pe', lambda e: e.transpose(out=out, in_=in_, identity=ident), reads=reads, writes=writes)

    def act(out, in_, func, reads, writes, **kw):
        S.op('act', lambda e: e.activation(out=out, in_=in_, func=func, **kw), reads=reads, writes=writes)

    def tt(eng, out, in0, in1, op, reads, writes):
        S.op(eng, lambda e: e.tensor_tensor(out=out, in0=in0, in1=in1, op=op), reads=reads, writes=writes)

    def ts(eng, out, in0, s1, s2, op0, op1, reads, writes):
        if op1 is None:
            S.op(eng, lambda e: e.tensor_scalar(out=out, in0=in0, scalar1=s1, scalar2=None, op0=op0), reads=reads, writes=writes)
        else:
            S.op(eng, lambda e: e.tensor_scalar(out=out, in0=in0, scalar1=s1, scalar2=s2, op0=op0, op1=op1),
                 reads=reads, writes=writes)

    def cp(eng, out, in_, reads, writes):
        if eng == 'act':
            act(out, in_, AF.Copy, reads, writes)
        elif out.dtype == in_.dtype:
            S.op(eng, lambda e: e.tensor_scalar(out=out, in0=in_, scalar1=1.0, scalar2=None, op0=ALU.mult),
                 reads=reads, writes=writes)
        else:
            S.op(eng, lambda e: e.tensor_copy(out=out, in_=in_), reads=reads, writes=writes)

    idf = sb('idf', [128, 128])
    idb = sb('idb', [128, 128], BF16)
    w_in_bf = sb('w_in_bf', [128, 8, D_IN], BF16)
    w_out_bf = sb('w_out_bf', [128, 8, D_MODEL], BF16)
    gT_sb = sb('gT_sb', [128, 8])
    lngb_sb = sb('lngb_sb', [128, 2, 512])
    fg_sb = sb('fg_sb', [128, D_MODEL])
    wsT_bf = sb('wsT_bf', [128, 8, 128], BF16)
    wsST_bf = sb('wsST_bf', [32, 8, 32], BF16)
    bsT_sb = sb('bsT_sb', [128, 8])
    bsS_sb = sb('bsS_sb', [32, 8])
    w2_bf = sb('w2_bf', [128, 2, 64], BF16)
    cb = sb('cb', [128, 2])
    b1T_sb = sb('b1T_sb', [128, 2])
    peT_bf = sb('peT_bf', [64, 2, 32], BF16)
    xt = [sb('xt%d' % i, [128, D_MODEL]) for i in range(2)]
    ropts = [sb('ropt%d' % i, [128, 2, 64]) for i in range(2)]
    ss = sb('ss', [128, 1])
    hn = sb('hn', [128, D_MODEL], BF16)
    junk = hn
    hT = sb('hT', [128, 8, 128], BF16)
    u_g = sb('u_g', [128, 512])
    v_g = sb('v_g', [128, 512])
    v_ln = sb('v_ln', [128, 512], BF16)
    sza = sb('sza', [128, 512])
    q_f = sb('q_f', [128, 512])
    kv_f = sb('kv_f', [128, 768])
    gates = sb('gates', [128, 24])
    szb = sb('szb', [128, 512])
    st6 = sb('st6', [128, 6])
    mv = sb('mv', [128, 2])
    rs = sb('rs', [128, 1])
    a_f = sb('a_f', [128, 512])
    merged = sb('merged', [128, D_MODEL], BF16)
    mT = sb('mT', [128, 8, 128], BF16)
    rt1 = sb('rt1', [128, 512])
    rt2 = sb('rt2', [128, 512])
    qr_bf = sb('qr_bf', [128, 8, 64], BF16)
    q_bf = sb('q_bf', [128, 8, 64], BF16)
    kvb = sb('kvb', [128, 768], BF16)
    bo_f = sb('bo_f', [128, 512])
    tmpo = sb('tmpo', [128, 4, 64])

    S.op('pool', lambda e: e.memset(idf[:], 1.0), writes=['idf'])
    S.op('pool', lambda e: e.affine_select(out=idf[:], in_=idf[:], pattern=[[-1, 128]], base=0, channel_multiplier=1,
                                           compare_op=ALU.is_equal, fill=0.0), reads=['idf'], writes=['idf'])
    cp('dve', idb[:], idf[:], ['idf'], ['idb'])
    S.dma('sp', gT_sb[:], gT[:, :], writes=['gT'])
    S.dma('sp', lngb_sb[:], lngb[:, :, :], writes=['lngb'])
    S.dma('sp', fg_sb[:], fgb[:, :], writes=['fg'])
    S.dma('sp', bsT_sb[:], bsT[:, :], writes=['bsT'])
    S.dma('sp', bsS_sb[:], bsS[:, :], writes=['bsS'])
    S.dma('sp', b1T_sb[:], b1T[:, :], writes=['b1T'])

    setup_stack = ExitStack()
    stage = [setup_stack.enter_context(nc.sbuf_tensor('stage%d' % i, [128, 1024], F32)) for i in range(2)]
    sti = [0]

    def next_stage():
        i = sti[0]
        sti[0] ^= 1
        return stage[i], 'stage%d' % i

    def load_cast(dst_ap, src_ap, P, W, dkey, scale_ap=None, eng='dve'):
        st, sk = next_stage()
        S.dma('sp', st[0:P, 0:W], src_ap, writes=[sk])
        if scale_ap is not None:
            ts(eng, dst_ap, st[0:P, 0:W], scale_ap, None, ALU.mult, None, [sk, 'gT'], [dkey])
        else:
            cp(eng, dst_ap, st[0:P, 0:W], [sk], [dkey])

    for kc in range(8):
        for hf in range(4):
            c0 = hf * 838
            load_cast(w_in_bf[:, kc, c0:c0 + 838], w_in[kc * 128:(kc + 1) * 128, c0:c0 + 838], 128, 838, 'w_in_bf',
                      scale_ap=gT_sb[:, kc:kc + 1], eng=('dve' if hf % 2 == 0 else 'pool'))
    for kc in range(8):
        load_cast(w_out_bf[:, kc, :], w_out[kc * 128:(kc + 1) * 128, :], 128, 1024, 'w_out_bf',
                  eng=('dve' if kc % 2 == 0 else 'pool'))
    st, sk = next_stage()
    S.dma('sp', st[:, 0:128].rearrange("p (c d) -> p c d", c=2), w2.rearrange("c k d -> k c d"), writes=[sk])
    cp('dve', w2_bf[:].rearrange("p c d -> p (c d)"), st[:, 0:128], [sk], ['w2_bf'])
    st, sk = next_stage()
    S.dma('sp', st[0:64, 0:64].rearrange("p (c r) -> p c r", c=2), peT[:, :, :], writes=[sk])
    cp('dve', peT_bf[:].rearrange("p c r -> p (c r)"), st[0:64, 0:64], [sk], ['peT'])
    st, sk = next_stage()
    wsv = st[:, 0:1024].rearrange("p (g s) -> p g s", g=8)
    S.dma('sp', wsv, w_s.rearrange("g t s -> t g s"), writes=[sk])
    S.op('pool', lambda e: e.affine_select(out=wsv, in_=wsv, pattern=[[0, 8], [-1, 128]], base=0, channel_multiplier=1,
                                           compare_op=ALU.is_ge, fill=0.0), reads=[sk], writes=[sk])
    cp('dve', junk[:, 0:1024], st[:, 0:1024], [sk], ['hn'])
    tb, tk = next_tb()
    for g in range(8):
        tr(tb[:, g * 128:(g + 1) * 128], junk[:, g * 128:(g + 1) * 128], idb[:], ['hn', 'idb'], [tk])
    cp('dve', wsT_bf[:].rearrange("p g t -> p (g t)"), tb[:, :], [tk], ['wsT'])
    st, sk = next_stage()
    wss = st[0:32, 0:256].rearrange("p (g s) -> p g s", g=8)
    S.op('pool', lambda e: e.memset(st[0:32, 0:256], 0.0), writes=[sk])
    with nc.allow_non_contiguous_dma(reason="tiny 8x8 blocks of w_s"):
        for b in range(4):
            S.dma('sp', wss[8 * b:8 * b + 8, :, 8 * b:8 * b + 8], w_s[:, 0:8, 0:8].rearrange("g t s -> t g s"),
                  reads=[sk], writes=[sk + 'b%d' % b])
    S.op('pool', lambda e: e.affine_select(out=wss, in_=wss, pattern=[[0, 8], [-1, 32]], base=0, channel_multiplier=1,
                                           compare_op=ALU.is_ge, fill=0.0), reads=[sk] + [sk + 'b%d' % b for b in range(4)],
         writes=[sk])
    cp('dve', junk[0:32, 0:256], st[0:32, 0:256], [sk], ['hn'])
    tb, tk = next_tb()
    for g in range(8):
        tr(tb[0:32, g * 32:(g + 1) * 32], junk[0:32, g * 32:(g + 1) * 32], idb[0:32, 0:32], ['hn', 'idb'], [tk])
    cp('dve', wsST_bf[:].rearrange("p g t -> p (g t)"), tb[0:32, 0:256], [tk], ['wsST'])

    def const_bf(stack, name):
        shape = list(consts[name].shape)
        t = stack.enter_context(nc.sbuf_tensor('k_' + name, shape, BF16))
        P, W = shape[0], int(np.prod(shape[1:]))
        if len(shape) == 2:
            flat, src = t[:], cd[name]
        else:
            flat, src = t[:].rearrange("p a b -> p (a b)"), cd[name].rearrange("p a b -> p (a b)")
        for c0 in range(0, W, 1024):
            wd = min(1024, W - c0)
            load_cast(flat[:, c0:c0 + wd], src[:, c0:c0 + wd], P, wd, 'k_' + name)
        return t

    def const_f32(stack, name):
        shape = list(consts[name].shape)
        t = stack.enter_context(nc.sbuf_tensor('k_' + name, shape, F32))
        S.dma('sp', t[:], cd[name], writes=['k_' + name])
        return t

    def token_front(xb, kx, P, ws_bf, bs_sb, ropt, ropekey):
        act(junk[0:P, :], xb[0:P, :], AF.Square, [kx], ['hn', 'ss'], accum_out=ss[0:P, :])
        ts('dve', ss[0:P], ss[0:P], 1.0 / D_MODEL, EPS, ALU.mult, ALU.add, ['ss'], ['ss'])
        act(ss[0:P], ss[0:P], AF.Sqrt, ['ss'], ['ss'])
        S.op('dve', lambda e: e.reciprocal(out=ss[0:P], in_=ss[0:P]), reads=['ss'], writes=['ss'])
        ts('dve', hn[0:P], xb[0:P], ss[0:P, 0:1], None, ALU.mult, None, [kx, 'ss'], ['hn'])
        tb, tk = next_tb()
        for kc in range(8):
            tr(tb[:, kc * 128:kc * 128 + P], hn[0:P, kc * 128:(kc + 1) * 128], idb[0:P, 0:P], ['hn', 'idb'], [tk])
        cp('dve', hT[:, :, 0:P], tb[:, :].rearrange("p (k t) -> p k t", k=8)[:, :, 0:P], [tk], ['hT'])
        for ci, (c0, c1) in enumerate(CHUNKS):
            w = c1 - c0
            pw, pk = next_pw()
            for kc in range(8):
                mm(pw[0:P, 0:w], hT[:, kc, 0:P], w_in_bf[:, kc, c0:c1], kc == 0, kc == 7, ['hT', 'w_in_bf'], [pk])
            if ci == 0:
                act(u_g[0:P], pw[0:P, :], AF.Gelu_apprx_tanh, [pk], ['u_g'])
            elif ci == 1:
                act(v_g[0:P], pw[0:P, :], AF.Gelu_apprx_tanh, [pk], ['v_g'])
                S.op('dve', lambda e: e.bn_stats(out=st6[0:P], in_=v_g[0:P]), reads=['v_g'], writes=['st6'])
                S.op('dve', lambda e: e.bn_aggr(out=mv[0:P], in_=st6[0:P]), reads=['st6'], writes=['mv'])
                ts('dve', rs[0:P], mv[0:P, 1:2], EPS, None, ALU.add, None, ['mv'], ['rs'])
                act(rs[0:P], rs[0:P], AF.Sqrt, ['rs'], ['rs'])
                S.op('dve', lambda e: e.reciprocal(out=rs[0:P], in_=rs[0:P]), reads=['rs'], writes=['rs'])
                ts('dve', v_g[0:P], v_g[0:P], mv[0:P, 0:1], rs[0:P, 0:1], ALU.subtract, ALU.mult, ['v_g', 'mv', 'rs'], ['v_g'])
                tt('dve', v_g[0:P], v_g[0:P], lngb_sb[0:P, 0, :], ALU.mult, ['v_g', 'lngb'], ['v_g'])
                tt('dve', v_g[0:P], v_g[0:P], lngb_sb[0:P, 1, :], ALU.add, ['v_g', 'lngb'], ['v_g'])
                cp('act', v_ln[0:P], v_g[0:P], ['v_g'], ['v_ln'])
            elif ci == 2:
                act(sza[0:P], pw[0:P, :], AF.Silu, [pk], ['sza'])
            elif ci == 3:
                cp('act', q_f[0:P], pw[0:P, :], [pk], ['q_f'])
            elif ci == 4:
                cp('act', kv_f[0:P, 0:512], pw[0:P, :], [pk], ['kv_f'])
            elif ci == 5:
                cp('act', kv_f[0:P, 512:768], pw[0:P, 0:256], [pk], ['kv_f'])
                act(gates[0:P], pw[0:P, 256:280], AF.Sigmoid, [pk], ['gates'])
            else:
                act(szb[0:P], pw[0:P, :], AF.Silu, [pk], ['szb'])
        pw, pk = next_pw()
        for g in range(8):
            mm(pw[0:P, g * 64:(g + 1) * 64], ws_bf[0:P, g, 0:P], v_ln[0:P, g * 64:(g + 1) * 64], g == 0, g == 7,
               ['v_ln', 'wsT', 'wsST'], [pk])
        tt('dve', a_f[0:P].rearrange("p (g d) -> p g d", g=8), pw[0:P, :].rearrange("p (g d) -> p g d", g=8),
           bs_sb[0:P, :].unsqueeze(2).to_broadcast([P, 8, 64]), ALU.add, [pk, 'bsT', 'bsS'], ['a_f'])
        tt('dve', a_f[0:P], a_f[0:P], u_g[0:P], ALU.mult, ['a_f', 'u_g'], ['a_f'])
        tt('dve', merged[0:P, 0:512], a_f[0:P], sza[0:P], ALU.mult, ['a_f', 'sza'], ['merged_a'])
        cos_b = lambda H: ropt[0:P, 0, :].unsqueeze(1).to_broadcast([P, H, 64])
        sin_lo = lambda H: ropt[0:P, 1, 0:32].unsqueeze(1).to_broadcast([P, H, 32])
        sin_hi = lambda H: ropt[0:P, 1, 32:64].unsqueeze(1).to_broadcast([P, H, 32])

        def rope(src, dst, H, skey, dkey):
            t1 = rt1[0:P, 0:H * 64].rearrange("p (h d) -> p h d", h=H)
            t2 = rt2[0:P, 0:H * 64].rearrange("p (h d) -> p h d", h=H)
            tt('pool', t1, src, cos_b(H), ALU.mult, [skey, ropekey], ['rt1'])
            tt('pool', t2[:, :, 0:32], src[:, :, 32:64], sin_lo(H), ALU.mult, [skey, ropekey], ['rt2a'])
            tt('pool', t2[:, :, 32:64], src[:, :, 0:32], sin_hi(H), ALU.mult, [skey, ropekey], ['rt2b'])
            tt('pool', dst, t1, t2, ALU.add, ['rt1', 'rt2a', 'rt2b'], [dkey])

        rope(q_f[0:P].rearrange("p (h d) -> p h d", h=8), qr_bf[0:P], 8, 'q_f', 'qr_bf')
        cp('act', q_bf[0:P].rearrange("p h d -> p (h d)"), q_f[0:P], ['q_f'], ['q_bf'])
        ksv = kv_f[0:P, 256:384].rearrange("p (h d) -> p h d", h=2)
        kwv = kv_f[0:P, 512:640].rearrange("p (h d) -> p h d", h=2)
        rope(ksv, ksv, 2, 'kv_f', 'kv_f')
        rope(kwv, kwv, 2, 'kv_f', 'kv_f')
        cp('act', kvb[0:P], kv_f[0:P], ['kv_f'], ['kvb'])

    def token_back(xb, kx, P, ydst):
        tb, tk = next_tb()
        for kc in range(8):
            tr(tb[:, kc * 128:kc * 128 + P], merged[0:P, kc * 128:(kc + 1) * 128], idb[0:P, 0:P],
               ['merged_a', 'merged_b', 'idb'], [tk])
        cp('dve', mT[:, :, 0:P], tb[:, :].rearrange("p (k t) -> p k t", k=8)[:, :, 0:P], [tk], ['mT'])
        for half in range(2):
            pw, pk = next_pw()
            for kc in range(8):
                mm(pw[0:P, :], mT[:, kc, 0:P], w_out_bf[:, kc, half * 512:(half + 1) * 512], kc == 0, kc == 7,
                   ['mT', 'w_out_bf'], [pk])
            tt('dve', xb[0:P, half * 512:(half + 1) * 512], xb[0:P, half * 512:(half + 1) * 512], pw[0:P, :], ALU.add,
               [kx, pk], [kx])
        act(junk[0:P, :], xb[0:P, :], AF.Square, [kx], ['hn', 'ss'], accum_out=ss[0:P, :])
        ts('dve', ss[0:P], ss[0:P], 1.0 / D_MODEL, EPS, ALU.mult, ALU.add, ['ss'], ['ss'])
        act(ss[0:P], ss[0:P], AF.Sqrt, ['ss'], ['ss'])
        S.op('dve', lambda e: e.reciprocal(out=ss[0:P], in_=ss[0:P]), reads=['ss'], writes=['ss'])
        S.op('dve', lambda e: e.scalar_tensor_tensor(out=xb[0:P], in0=xb[0:P], scalar=ss[0:P, 0:1], in1=fg_sb[0:P],
                                                     op0=ALU.mult, op1=ALU.mult), reads=[kx, 'ss', 'fg'], writes=[kx])
        S.dma('sp', ydst, xb[0:P, :], reads=[kx])

    pst = ExitStack()

    def psb(name, shape, dt=F32):
        return pst.enter_context(nc.sbuf_tensor(name, list(shape), dt))

    w1p = psb('w1p', [64, 2, 32, 128], BF16)
    for c in range(2):
        w1v = w1[c].rearrange("(rs d) k -> d rs k", d=64)
        for r8 in range(4):
            load_cast(w1p[:, c, r8 * 8:(r8 + 1) * 8, :].rearrange("p a k -> p (a k)"),
                      w1v[:, r8 * 8:(r8 + 1) * 8, :], 64, 1024, 'w1p')
    for c in range(2):
        for rs_ in range(32):
            mm(PM[:, c:c + 1], w1p[:, c, rs_, :], peT_bf[:, c, rs_:rs_ + 1], (c == 0 and rs_ == 0), (c == 1 and rs_ == 31),
               ['w1p', 'peT'], ['PM'])
    tt('dve', cb[:], PM[:, 0:2], b1T_sb[:], ALU.add, ['PM', 'b1T'], ['cb'])

    KsT = [psb('KsT%d' % h, [64, SEQ], BF16) for h in range(2)]
    KwT = [psb('KwT%d' % h, [64, SEQ], BF16) for h in range(2)]
    Vsaug = psb('Vsaug', [128, NT, 2, 66], BF16)
    Vwaug = psb('Vwaug', [128, NT, 2, 66], BF16)
    kcT = psb('kcT', [64, 2, 2, 144], BF16)
    hidnew = psb('hidnew', [128, 2, 2, 8], BF16)
    hidV = psb('hidV', [128, 2, 128], BF16)
    KcmpT = psb('KcmpT', [64, 2, 128], BF16)
    vcaug = psb('vcaug', [128, 2, 98], BF16)
    qrT = psb('qrT', [64, 8, 128], BF16)
    qT = psb('qT', [64, 8, 128], BF16)
    pT = [psb('pT%d' % i, [128, 512], BF16) for i in range(3)]
    negsel = psb('negsel', [128, 2, 32])
    negselT = psb('negselT', [32, 2, 4, 128], BF16)
    sc = psb('sc', [128, 2, 32])
    sc2 = psb('sc2', [128, 32])
    m16 = psb('m16', [128, 16])
    rden = psb('rden', [128, 4])
    wgt = psb('wgt', [128, 4])
    k_cmask = const_bf(pst, 'cmask')
    k_m01c = const_bf(pst, 'm01c')
    k_Ep = const_bf(pst, 'Ep')
    k_vmP = const_f32(pst, 'vmP')
    k_acP = const_f32(pst, 'acP')
    st, sk = next_stage()
    S.dma('sp', st[:, 0:32], cd['ovP'], writes=[sk])
    S.op('pool', lambda e: e.memset(vcaug[:], 0.0), writes=['vcaug'])
    S.op('pool', lambda e: e.memset(vcaug[:, :, 64:65], 1.0), reads=['vcaug'], writes=['vcaug1'])
    for h in range(2):
        cp('dve', vcaug[:, h, 65:97], st[:, 0:32], [sk, 'vcaug'], ['vcaug_ov%d' % h])
    S.op('pool', lambda e: e.memset(Vsaug[:], 1.0), writes=['Vsaug'])
    S.op('pool', lambda e: e.memset(Vwaug[:], 1.0), writes=['Vwaug'])
    S.op('pool', lambda e: e.memset(kcT[:], 0.0), writes=['kcT'])
    S.op('pool', lambda e: e.memset(hidV[:], 0.0), writes=['hidV'])
    S.op('pool', lambda e: e.memset(hidnew[:], 0.0), writes=['hidnew'])
    S.op('pool', lambda e: e.memset(KcmpT[:], 0.0), writes=['KcmpT'])
    pti = [0]

    def next_pT():
        pti[0] = (pti[0] + 1) % 3
        return pT[pti[0]], 'pT%d' % pti[0]

    def attend(i, hk, KT, Vaug, vkey, kkey, chunks, po, pok, masks, selmask):
        rhs_q = qrT[:, hk * 4:(hk + 1) * 4, :].rearrange("p g t -> p (g t)")
        for ci, kc in enumerate(chunks):
            pw, pk = next_pw()
            mm(pw[:, :], KT[hk][:, kc * 128:(kc + 1) * 128], rhs_q, True, not selmask, [kkey, 'qrT'], [pk])
            if selmask:
                mm(pw[:, :], k_Ep[:, kc * 128:(kc + 1) * 128], negselT[:, hk].rearrange("p g t -> p (g t)"), False, True,
                   ['k_Ep', 'negselT'], [pk])
            p_t, ptk = next_pT()
            act(p_t[:, :], pw[:, :], AF.Exp, [pk], [ptk], scale=SCALE)
            if kc in masks:
                mk = masks[kc]
                tt('dve', p_t[:, :].rearrange("p (g t) -> p g t", g=4), p_t[:, :].rearrange("p (g t) -> p g t", g=4),
                   k_cmask[:, mk, :].unsqueeze(1).to_broadcast([128, 4, 128]), ALU.mult, [ptk, 'k_cmask'], [ptk])
            for g in range(4):
                mm(po[:, g * 65:(g + 1) * 65], p_t[:, g * 128:(g + 1) * 128], Vaug[:, kc, hk, 0:65],
                   (ci == 0 and g == 0), (ci == len(chunks) - 1 and g == 3), [ptk, vkey], [pok])

    def branch_out(po, pok, hk, br, width, first):
        pv = po[:, 0:4 * width].rearrange("p (g w) -> p g w", g=4)
        ts('dve', rden[:], pv[:, :, 64], 1e-30, None, ALU.max, None, [pok], ['rden'])
        S.op('dve', lambda e: e.reciprocal(out=rden[:], in_=rden[:]), reads=['rden'], writes=['rden'])
        gv = gates[:, hk * 12:(hk + 1) * 12].rearrange("p (g c) -> p g c", c=3)[:, :, br]
        tt('dve', wgt[:], rden[:], gv, ALU.mult, ['rden', 'gates'], ['wgt'])
        dst = bo_f[:, hk * 256:(hk + 1) * 256].rearrange("p (g d) -> p g d", g=4)
        wb = wgt[:, :].unsqueeze(2).to_broadcast([128, 4, 64])
        if first:
            tt('dve', dst, pv[:, :, 0:64], wb, ALU.mult, [pok, 'wgt'], ['bo_f'])
        else:
            tt('dve', tmpo[:], pv[:, :, 0:64], wb, ALU.mult, [pok, 'wgt'], ['tmpo'])
            tt('dve', dst, dst, tmpo[:], ALU.add, ['bo_f', 'tmpo'], ['bo_f'])

    def prefetch(i):
        S.dma('sp', xt[i % 2][:], xp[128 * i:128 * (i + 1), :], writes=['xt%d' % (i % 2)])
        S.dma('sp', ropts[i % 2][:], cd['ropeP'][128 * i:128 * (i + 1), :, :], writes=['ropt%d' % (i % 2)])

    prefetch(0)
    for i in range(NT):
        xb, kx = xt[i % 2], 'xt%d' % (i % 2)
        if i + 1 < NT:
            prefetch(i + 1)
        token_front(xb, kx, 128, wsT_bf, bsT_sb, ropts[i % 2], 'ropt%d' % (i % 2))
        S.dma('sp', kvp[128 * i:128 * (i + 1), :], kv_f[:, 0:512], reads=['kv_f'])
        if i >= 12:
            S.dma('sp', winp[128 * (i - 12):128 * (i - 11), :], kv_f[:, 512:768], reads=['kv_f'])
        if do_prompt_attn:
            SUB = os.environ.get('KSUB', 'ABCD123')
            if 'A' in SUB:
                tb, tk = next_tb()
                for h in range(8):
                    tr(tb[0:64, h * 128:(h + 1) * 128], qr_bf[:, h, :], idb[:], ['qr_bf', 'idb'], [tk])
                cp('dve', qrT[:].rearrange("p h t -> p (h t)"), tb[0:64, :], [tk], ['qrT'])
                tb, tk = next_tb()
                for h in range(8):
                    tr(tb[0:64, h * 128:(h + 1) * 128], q_bf[:, h, :], idb[:], ['q_bf', 'idb'], [tk])
                cp('act', qT[:].rearrange("p h t -> p (h t)"), tb[0:64, :], [tk], ['qT'])
            if 'B' in SUB:
                tb, tk = next_tb()
                srcs = [256, 320, 512, 576, 0, 64, 128, 192]
                for j, c0 in enumerate(srcs):
                    tr(tb[0:64, j * 128:(j + 1) * 128], kvb[:, c0:c0 + 64], idb[:], ['kvb', 'idb'], [tk])
                for h in range(2 if '1' in SUB else 0):
                    cp('dve', KsT[h][:, 128 * i:128 * (i + 1)], tb[0:64, h * 128:(h + 1) * 128], [tk], ['KsT'])
                    cp('dve', KwT[h][:, 128 * i:128 * (i + 1)], tb[0:64, (2 + h) * 128:(3 + h) * 128], [tk], ['KwT'])
                if '2' in SUB:
                    cp('dve', kcT[:, :, :, 16:144], tb[0:64, 512:1024].rearrange("p (c h t) -> p c h t", c=2, h=2), [tk], ['kcT'])
                if '3' in SUB:
                    cp('act', Vsaug[:, i, :, 0:64], kvb[:, 384:512].rearrange("p (h d) -> p h d", h=2), ['kvb', 'Vsaug'], ['Vsaug'])
                    cp('act', Vwaug[:, i, :, 0:64], kvb[:, 640:768].rearrange("p (h d) -> p h d", h=2), ['kvb', 'Vwaug'], ['Vwaug'])
            m0 = 1 if i == 0 else 0
            nb = 8 - m0
            if 'C' in SUB:
                for c in range(2):
                    for hk in range(2):
                        grp = c * 2 + hk
                        for r in range(2):
                            for s_ in range(16):
                                st0 = 16 * m0 + 16 * r + s_
                                mm(PM[:, grp * 8 + m0:grp * 8 + 8], w1p[:, c, r * 16 + s_, :],
                                   kcT[:, c, hk, st0:st0 + 16 * (nb - 1) + 1:16], (r == 0 and s_ == 0), (r == 1 and s_ == 15),
                                   ['w1p', 'kcT'], ['PM'])
                for c in range(2):
                    act(hidnew[:, c, :, m0:8], PM[:, c * 16:(c + 1) * 16].rearrange("p (h m) -> p h m", h=2)[:, :, m0:8],
                        AF.Gelu_apprx_tanh, ['PM', 'cb'], ['hidnew'], bias=cb[:, c:c + 1])
            n0 = 8 * i - 1 + m0
            if 'D' in SUB:
                cp('dve', hidV[:, :, n0:n0 + nb], hidnew[:, 1, :, m0:8], ['hidnew', 'hidV'], ['hidV'])
                cp('dve', kcT[:, :, :, 0:16], kcT[:, :, :, 128:144], ['kcT'], ['kcT'])
                mm(PM[0:64, 32:48], w2_bf[:, 0, :], hidnew[:, 0, :, :].rearrange("p h m -> p (h m)"), True, True,
                   ['w2_bf', 'hidnew'], ['PM'])
                cp('dve', KcmpT[:, :, n0:n0 + nb], PM[0:64, 32:48].rearrange("p (h m) -> p h m", h=2)[:, :, m0:8], ['PM'], ['KcmpT'])
                for hk in range(2):
                    mm(PM[:, 64 + hk * 64:128 + hk * 64], hidV[:, hk, :], w2_bf[:, 1, :], hk == 0, hk == 1, ['hidV', 'w2_bf'], ['PM'])
                cp('dve', vcaug[:, :, 0:64], PM[:, 64:192].rearrange("p (h d) -> p h d", h=2), ['PM', 'vcaug'], ['vcaug'])
            lvl = int(do_prompt_attn)
            if lvl == 1:
                S.op('pool', lambda e: e.memset(bo_f[:], 0.0), writes=['bo_f'])
            for hk in range(2 if lvl >= 2 else 0):
                pw, pk = next_pw()
                mm(pw[:, :], KcmpT[:, hk, :], qT[:, hk * 4:(hk + 1) * 4, :].rearrange("p g t -> p (g t)"), True, True,
                   ['KcmpT', 'qT'], [pk])
                p_t, ptk = next_pT()
                act(p_t[:, :], pw[:, :], AF.Exp, [pk], [ptk], scale=SCALE)
                tt('dve', p_t[:, :].rearrange("p (g t) -> p g t", g=4), p_t[:, :].rearrange("p (g t) -> p g t", g=4),
                   k_m01c[:, i, :].unsqueeze(1).to_broadcast([128, 4, 128]), ALU.mult, [ptk, 'k_m01c'], [ptk])
                for g in range(4):
                    mm(POc[:, g * 97:(g + 1) * 97], p_t[:, g * 128:(g + 1) * 128], vcaug[:, hk, 0:97], g == 0, g == 3,
                       [ptk, 'vcaug', 'vcaug1', 'vcaug_ov%d' % hk], ['POc'])
                branch_out(POc, 'POc', hk, 0, 97, True)
                sel = i >= 8
                if sel:
                    pv = POc[:, 0:388].rearrange("p (g w) -> p g w", g=4)
                    tt('dve', tmpo[:, :, 0:32], pv[:, :, 65:97], rden[:, :].unsqueeze(2).to_broadcast([128, 4, 32]), ALU.mult,
                       ['POc', 'rden'], ['tmpo'])
                    S.op('dve', lambda e: e.tensor_reduce(out=sc2[:], in_=tmpo[:, :, 0:32].rearrange("p g j -> p j g"),
                                                          axis=AX.X, op=ALU.add), reads=['tmpo'], writes=['sc2'])
                    tt('dve', sc2[:], sc2[:], k_vmP[:, i - 8, :], ALU.mult, ['sc2', 'k_vmP'], ['sc2'])
                    tt('dve', sc2[:], sc2[:], k_acP[:, i - 8, :], ALU.add, ['sc2', 'k_acP'], ['sc2'])
                    S.op('dve', lambda e: e.max(out=m16[:, 0:8], in_=sc2[:]), reads=['sc2'], writes=['m16'])
                    S.op('dve', lambda e: e.match_replace(out=sc[:, 0, :], in_to_replace=m16[:, 0:8], in_values=sc2[:],
                                                          imm_value=-3.0e38), reads=['sc2', 'm16'], writes=['sc'])
                    S.op('dve', lambda e: e.max(out=m16[:, 8:16], in_=sc[:, 0, :]), reads=['sc'], writes=['m16'])
                    ts('dve', negsel[:, hk, :], sc2[:], m16[:, 15:16], NEGM, ALU.is_lt, ALU.mult, ['sc2', 'm16'], ['negsel'])
                    tr(PM[0:32, 256:384], negsel[:, hk, :], idf[:], ['negsel', 'idf'], ['PM'])
                    cp('dve', negselT[:, hk], PM[0:32, 256:384].unsqueeze(1).to_broadcast([32, 4, 128]), ['PM'], ['negselT'])
                if lvl >= 3:
                    attend(i, hk, KsT, Vsaug, 'Vsaug', 'KsT', list(range(i + 1)), POs, 'POs', {i: 0}, sel)
                    branch_out(POs, 'POs', hk, 1, 65, False)
                if lvl < 4:
                    continue
                wch = list(range(max(0, i - 4), i + 1))
                wm = {i: 0}
                if i >= 4:
                    wm[i - 4] = 1
                attend(i, hk, KwT, Vwaug, 'Vwaug', 'KwT', wch, POw, 'POw', wm, False)
                branch_out(POw, 'POw', hk, 2, 65, False)
            tt('dve', merged[:, 512:1024], bo_f[:], szb[:], ALU.mult, ['bo_f', 'szb'], ['merged_b'])
        else:
            S.op('pool', lambda e: e.memset(merged[:, 512:1024], 0.0), writes=['merged_b'])
        token_back(xb, kx, 128, yp[128 * i:128 * (i + 1), :])

    S.barrier()
    pst.close()

    sst = ExitStack()

    def ssb(name, shape, dt=F32):
        return sst.enter_context(nc.sbuf_tensor(name, list(shape), dt))

    xb, kx = xt[0], 'xt0'
    S.dma('sp', xb[0:32, :], xs[:, :], writes=[kx])
    S.dma('sp', ropts[0][0:32], cd['ropeS'], writes=['ropt0'])
    token_front(xb, kx, 32, wsST_bf, bsS_sb, ropts[0], 'ropt0')
    S.dma('sp', kvs[:, :], kv_f[0:32, 0:512], reads=['kv_f'])
    S.dma('sp', vso[:, :], v_g[0:32, :], reads=['v_g'])
    for b in range(4):
        S.dma('sp', wins[b, 0:504, :], swin[b, 8:512, :])
        S.dma('sp', wins[b, 504:512, :], kv_f[8 * b:8 * b + 8, 512:768], reads=['kv_f'])

    if do_sample_attn:
        k_Amat = const_bf(sst, 'Amat')
        k_mnew = const_bf(sst, 'mnew')
        k_mwin0 = const_bf(sst, 'mwin0')
        k_Gm = const_f32(sst, 'Gm')
        k_vmS = const_f32(sst, 'vmS')
        k_acS = const_f32(sst, 'acS')
        k_nf = const_f32(sst, 'notfirst')
        k_pmod = const_f32(sst, 'pmod')
        w1s = ssb('w1s', [128, 2, 16, 128], BF16)
        for c in range(2):
            w1v = w1[c].rearrange("(rsp sd) k -> sd rsp k", sd=128)
            for a8 in range(2):
                load_cast(w1s[:, c, a8 * 8:(a8 + 1) * 8, :].rearrange("p a k -> p (a k)"), w1v[:, a8 * 8:(a8 + 1) * 8, :],
                          128, 1024, 'w1s')
        vcaugS = ssb('vcaugS', [128, 8, 386], BF16)
        S.op('pool', lambda e: e.memset(vcaugS[:, :, 128:129], 1.0), writes=['vcS1'])
        st, sk = next_stage()
        for ch in range(8):
            st, sk = next_stage()
            S.dma('sp', st[:, 0:257], cd['ovS'][:, ch, :], writes=[sk])
            cp('dve', vcaugS[:, ch, 129:386], st[:, 0:257], [sk], ['vcS_ov'])
        g1 = [ssb('g1_%d' % i, [128, 4, 256]) for i in range(2)]
        gint = [ssb('gint%d' % i, [128, 4, 2, 2, 64], BF16) for i in range(2)]
        idx_i = [ssb('idx_i%d' % s2_, [128, 128], I32) for s2_ in range(1)]
        ptl_sb = ssb('ptl_sb', [128, 128], I32)
        XTg = [ssb('XTg%d' % i, [128, 4, 8, 65], BF16) for i in range(2)]
        hidS = ssb('hidS', [128, 2, 2, 64], BF16)
        KcmpS = ssb('KcmpS', [128, 1024], BF16)
        qblk_r = ssb('qblk_r', [128, 4, 64], BF16)
        qblk_u = ssb('qblk_u', [128, 4, 64], BF16)
        KnT = ssb('KnT', [128, 2, 32], BF16)
        Vn = ssb('Vn', [32, 2, 130], BF16)
        wbuf = ssb('wbuf', [128, 4, 256])
        KwS = ssb('KwS', [128, 512], BF16)
        VwS = ssb('VwS', [128, 4, 130], BF16)
        pTw = ssb('pTw', [128, 5, 64], BF16)
        PcS = ssb('PcS', [128, 8, 64], BF16)
        impn = ssb('impn', [64, 257])
        scS = ssb('scS', [64, 257])
        scS2 = ssb('scS2', [64, 257])
        m16S = ssb('m16S', [64, 16])
        negS = ssb('negS', [64, 256])
        negST = ssb('negST', [128, 2, 64], BF16)
        KsTt = [ssb('KsTt%d' % i, [128, 4, 128], BF16) for i in range(2)]
        Vst = [ssb('Vst%d' % i, [128, 4, 130], BF16) for i in range(4)]
        pTs = [ssb('pTs%d' % i, [128, 8, 64], BF16) for i in range(2)]
        pTn = ssb('pTn', [32, 64], BF16)
        Ocomb = ssb('Ocomb', [64, 64])
        Otmp = ssb('Otmp', [64, 64])
        gq = ssb('gq', [64, 3])
        rdS = ssb('rdS', [64, 1])
        wgS = ssb('wgS', [64, 1])

        S.dma('sp', ptl_sb[:], ptl[:, :], writes=['ptl'])
        idxf = ssb('idxf', [128, 128])
        ts('dve', idxf[:], ptl_sb[:], 32.0, k_pmod[:, 0:1], ALU.mult, ALU.add, ['ptl', 'k_pmod'], ['idxf'])
        for s2_ in range(1):
            ts('dve', idx_i[s2_][:], idxf[:], float(s2_), None, ALU.add, None, ['idxf'], ['idx'])
        S.op('pool', lambda e: e.memset(qblk_r[:], 0.0), writes=['qblk_r'])
        S.op('pool', lambda e: e.memset(qblk_u[:], 0.0), writes=['qblk_u'])
        S.op('pool', lambda e: e.memset(Vn[:], 1.0), writes=['Vn'])
        S.op('pool', lambda e: e.memset(VwS[:], 1.0), writes=['VwS'])
        for i in range(4):
            S.op('pool', lambda e: e.memset(Vst[i][:], 1.0), writes=['Vst%d' % i])
        for i in range(2):
            S.op('pool', lambda e: e.memset(XTg[i][:], 0.0), writes=['XTg%d' % i])
        for (src, dst, skey, dkey) in ((qr_bf, qblk_r, 'qr_bf', 'qblk_r'), (q_bf, qblk_u, 'q_bf', 'qblk_u')):
            tb, tk = next_tb()
            for h in range(8):
                hk_, g = h // 4, h % 4
                tr(tb[hk_ * 64:(hk_ + 1) * 64, g * 32:(g + 1) * 32], src[0:32, h, :], idb[0:32, 0:32], [skey, 'idb'], [tk])
            tv = tb[:, 0:128].rearrange("p (g b t) -> p b g t", g=4, b=4)
            dv = dst[:].rearrange("p b (h g t) -> p b h g t", h=2, g=4)
            for b in range(4):
                cp('dve', dv[0:64, b, 0], tv[0:64, b], [tk, dkey], [dkey])
                cp('dve', dv[64:128, b, 1], tv[64:128, b], [tk, dkey], [dkey])
        tb, tk = next_tb()
        tr(tb[:, 0:32], kvb[0:32, 256:384], idb[0:32, 0:32], ['kvb', 'idb'], [tk])
        tr(tb[:, 32:64], kvb[0:32, 512:640], idb[0:32, 0:32], ['kvb', 'idb'], [tk])
        cp('dve', KnT[:].rearrange("p a t -> p (a t)"), tb[:, 0:64], [tk], ['KnT'])
        cp('dve', Vn[:, 0, 0:128], kvb[0:32, 384:512], ['kvb', 'Vn'], ['Vn'])
        cp('dve', Vn[:, 1, 0:128], kvb[0:32, 640:768], ['kvb', 'Vn'], ['Vn'])

        def s_branch(po, pok, br, first):
            ts('dve', rdS[:], po[0:64, 128:129], 1e-30, None, ALU.max, None, [pok], ['rdS'])
            S.op('dve', lambda e: e.reciprocal(out=rdS[:], in_=rdS[:]), reads=['rdS'], writes=['rdS'])
            tt('dve', wgS[:], rdS[:], gq[:, br:br + 1], ALU.mult, ['rdS'] + ['gq%d' % h for h in range(8)], ['wgS'])
            for hk in range(2):
                rows = slice(hk * 32, (hk + 1) * 32)
                src = po[rows, hk * 64:(hk + 1) * 64]
                if first:
                    ts('dve', Ocomb[rows, :], src, wgS[rows, 0:1], None, ALU.mult, None, [pok, 'wgS'], ['Ocomb'])
                else:
                    ts('dve', Otmp[rows, :], src, wgS[rows, 0:1], None, ALU.mult, None, [pok, 'wgS'], ['Otmp'])
                    tt('dve', Ocomb[rows, :], Ocomb[rows, :], Otmp[rows, :], ALU.add, ['Ocomb', 'Otmp'], ['Ocomb'])

        gi = [0]
        SIMG = os.environ.get('KSIMGATHER', '0') == '1'

        def gather(b, j, half):
            buf = g1[gi[0] % 2]
            key = 'g1_%d' % (gi[0] % 2)
            gi[0] += 1
            col = b * 32 + j
            S.dma('pool', None, None, reads=['idx'], writes=[key + 'a0'],
                  fn=lambda e: e.indirect_dma_start(out=buf[:].rearrange("p a b -> p (a b)"), out_offset=None,
                                                    in_=cacheH[half][:, :],
                                                    in_offset=bass.IndirectOffsetOnAxis(ap=idx_i[0][:, col:col + 1], axis=0)))
            for kk in ('a1', 'b0', 'b1'):
                S.lastw[key + kk] = S.lastw[key + 'a0']
                S.readers[key + kk] = {}
            S._wait('pool', S.lastw[key + 'a0'])
            return buf, key

        for b in range(4):
            for h in range(8):
                S.dma('sp', gq[h * 8:(h + 1) * 8, :], gates[8 * b:8 * b + 8, h * 3:(h + 1) * 3], reads=['gates'], writes=['gq%d' % h])
            S.dma('sp', wbuf[:], swin[b].rearrange("(c p) f -> p c f", p=128), writes=['wbuf'])
            pw, pk = next_pw()
            for ch in range(4):
                tr(pw[:, ch * 128:(ch + 1) * 128], wbuf[:, ch, 0:128], idf[:], ['wbuf', 'idf'], [pk])
            cp('dve', KwS[:], pw[:, :], [pk], ['KwS'])
            cp('act', VwS[:, :, 0:128], wbuf[:, :, 128:256], ['wbuf', 'VwS'], ['VwS'])
            pw, pk = next_pw()
            for ch in range(4):
                mm(pw[:, ch * 64:(ch + 1) * 64], KwS[:, ch * 128:(ch + 1) * 128], qblk_r[:, b, :], ch == 0, False,
                   ['KwS', 'qblk_r'], [pk])
            mm(pw[0:32, 256:320], KnT[:, 1, :], qblk_r[:, b, :], False, True, ['KnT', 'qblk_r'], [pk])
            act(pTw[:, 0:4, :].rearrange("p c q -> p (c q)"), pw[:, 0:256], AF.Exp, [pk], ['pTw'], scale=SCALE)
            act(pTw[0:32, 4, :], pw[0:32, 256:320], AF.Exp, [pk], ['pTw4'], scale=SCALE)
            tt('dve', pTw[:, 0, :], pTw[:, 0, :], k_mwin0[:], ALU.mult, ['pTw', 'k_mwin0'], ['pTw'])
            tt('dve', pTw[0:32, 4, :], pTw[0:32, 4, :], k_mnew[:, b, :], ALU.mult, ['pTw4', 'k_mnew'], ['pTw4'])
            for ch in range(4):
                mm(POw[0:64, 0:129], pTw[:, ch, :], VwS[:, ch, 0:129], ch == 0, False, ['pTw', 'VwS'], ['POw'])
            mm(POw[0:64, 0:129], pTw[0:32, 4, :], Vn[:, 1, 0:129], False, True, ['pTw4', 'Vn'], ['POw'])
            slv = int(do_sample_attn)
            for G in range(16 if slv >= 2 else 0):
                xg, xk = XTg[G % 2], 'XTg%d' % (G % 2)
                for jj in range(2):
                    j = G * 2 + jj
                    buf, key = gather(b, j, 0)
                    gi_t, gik = gint[j % 2], 'gint%d' % (j % 2)
                    kall = [key + 'a0', key + 'a1', key + 'b0', key + 'b1']
                    for a_ in range(2):
                        cp('act' if a_ == 0 else 'dve', gi_t[:, :, a_, :, :],
                           buf[:, 2 * a_:2 * a_ + 2, :].rearrange("p s (q d) -> p q s d", d=64),
                           kall + [gik + 'x%d' % (1 - a_)], [gik + 'x%d' % a_])
                    tb, tk = next_tb()
                    for q4 in range(4):
                        for a_ in range(2):
                            tr(tb[:, (q4 * 2 + a_) * 128:(q4 * 2 + a_ + 1) * 128],
                               gi_t[:, q4, a_, :, :].rearrange("p s d -> p (s d)"), idb[:], [gik + 'x0', gik + 'x1', 'idb'], [tk])
                    for q4 in range(4):
                        cp('act' if q4 % 2 == 0 else 'dve',
                           xg[:, q4, :, 1 + 32 * jj:33 + 32 * jj].rearrange("p (r a) c -> p r a c", a=2),
                           tb[:, q4 * 256:(q4 + 1) * 256].rearrange("p (a c r) -> p r a c", a=2, r=4), [tk, xk], [xk])
                for q4 in range(4):
                    c = q4 // 2
                    for r in range(2):
                        for sp in range(8):
                            mm(PM[:, q4 * 64:(q4 + 1) * 64], w1s[:, c, r * 8 + sp, :], xg[:, q4, sp, r:r + 64],
                               (r == 0 and sp == 0), (r == 1 and sp == 7), ['w1s', xk], ['PM'])
                xn_, xnk = XTg[(G + 1) % 2], 'XTg%d' % ((G + 1) % 2)
                cp('dve', xn_[:, :, :, 0:1], xg[:, :, :, 64:65], [xk, xnk], [xnk])
                for c in range(2):
                    act(hidS[:, c].rearrange("p h n -> p (h n)"), PM[:, c * 128:(c + 1) * 128], AF.Gelu_apprx_tanh,
                        ['PM', 'cb'], ['hidS'], bias=cb[:, c:c + 1])
                for hk in range(2):
                    mm(PM[hk * 64:(hk + 1) * 64, 256:320], w2_bf[:, 0, :], hidS[:, 0, hk, :], True, True,
                       ['w2_bf', 'hidS'], ['PM'])
                cp('dve', KcmpS[:, 64 * G:64 * (G + 1)], PM[:, 256:320], ['PM'], ['KcmpS'])
                half = (G % 2) * 64
                for hk in range(2):
                    mm(PM[half:half + 64, 320 + hk * 64:384 + hk * 64], hidS[:, 1, hk, :], w2_bf[:, 1, :], True, True,
                       ['hidS', 'w2_bf'], ['PM'])
                cp('act', vcaugS[half:half + 64, G // 2, 0:128], PM[half:half + 64, 320:448], ['PM', 'vcS'], ['vcS'])
            if slv >= 3:
                pw, pk = next_pw()
                for ch in range(8):
                    mm(pw[:, ch * 64:(ch + 1) * 64], KcmpS[:, ch * 128:(ch + 1) * 128], qblk_u[:, b, :], ch == 0, ch == 7,
                       ['KcmpS', 'qblk_u'], [pk])
                act(PcS[:].rearrange("p c q -> p (c q)"), pw[:, :], AF.Exp, [pk], ['PcS'], scale=SCALE)
                ts('dve', PcS[:, 0, :], PcS[:, 0, :], k_nf[:, 0:1], None, ALU.mult, None, ['PcS', 'k_notfirst'], ['PcS'])
                for ch in range(8):
                    mm(POc[0:64, 0:386], PcS[:, ch, :], vcaugS[:, ch, :], ch == 0, ch == 7, ['PcS', 'vcS', 'vcS1', 'vcS_ov'], ['POc'])
                ts('dve', rdS[:], POc[0:64, 128:129], 1e-30, None, ALU.max, None, ['POc'], ['rdS'])
                S.op('dve', lambda e: e.reciprocal(out=rdS[:], in_=rdS[:]), reads=['rdS'], writes=['rdS'])
                ts('dve', impn[:], POc[0:64, 129:386], rdS[:, 0:1], None, ALU.mult, None, ['POc', 'rdS'], ['impn'])
                s_branch(POc, 'POc', 0, True)
                pw, pk = next_pw()
                mm(pw[0:64, 0:257], k_Gm[:], impn[:], True, True, ['k_Gm', 'impn'], [pk])
                tt('dve', scS[:], pw[0:64, 0:257], k_vmS[:], ALU.mult, [pk, 'k_vmS'], ['scS'])
                tt('dve', scS[:], scS[:], k_acS[:], ALU.add, ['scS', 'k_acS'], ['scS'])
                S.op('dve', lambda e: e.max(out=m16S[:, 0:8], in_=scS[:]), reads=['scS'], writes=['m16S'])
                S.op('dve', lambda e: e.match_replace(out=scS2[:], in_to_replace=m16S[:, 0:8], in_values=scS[:], imm_value=-3.0e38),
                     reads=['scS', 'm16S'], writes=['scS2'])
                S.op('dve', lambda e: e.max(out=m16S[:, 8:16], in_=scS2[:]), reads=['scS2'], writes=['m16S'])
                ts('dve', negS[:], scS[:, 0:256], m16S[:, 15:16], NEGM, ALU.is_lt, ALU.mult, ['scS', 'm16S'], ['negS'])
                pw, pk = next_pw()
                for hf in range(2):
                    tr(pw[:, hf * 64:(hf + 1) * 64], negS[:, hf * 128:(hf + 1) * 128], idf[0:64, 0:64], ['negS', 'idf'], [pk])
                cp('dve', negST[:].rearrange("p a q -> p (a q)"), pw[:, 0:128], [pk], ['negST'])
            if slv >= 4:
                for j in range(32):
                    buf, key = gather(b, j, 1)
                    kall = [key + 'a0', key + 'a1', key + 'b0', key + 'b1']
                    kt, ktk = KsTt[j % 2], 'KsTt%d' % (j % 2)
                    vt, vtk = Vst[j % 4], 'Vst%d' % (j % 4)
                    pw, pk = next_pw()
                    for rl in range(4):
                        tr(pw[:, rl * 128:(rl + 1) * 128], buf[:, rl, 0:128], idf[:], kall + ['idf'], [pk])
                    cp('act', kt[:].rearrange("p a k -> p (a k)"), pw[:, :], [pk], [ktk])
                    cp('dve', vt[:, :, 0:128], buf[:, :, 128:256], kall + [vtk], [vtk])
                    grp = j // 2
                    pt_, ptk = pTs[grp % 2], 'pTs%d' % (grp % 2)
                    for rl in range(4):
                        slot = (j % 2) * 4 + rl
                        mm(POs[:, slot * 64:(slot + 1) * 64], kt[:, rl, :], qblk_r[:, b, :], slot == 0, False,
                           [ktk, 'qblk_r'], ['POs'])
                        mm(POs[:, slot * 64:(slot + 1) * 64], k_Amat[:, 128 * (j % 16):128 * (j % 16 + 1)], negST[:, j // 16, :],
                           False, slot == 7, ['k_Amat', 'negST'], ['POs'])
                    if j % 2 == 1:
                        act(pt_[:].rearrange("p c q -> p (c q)"), POs[:, :], AF.Exp, ['POs'], [ptk], scale=SCALE)
                        for jj in range(2):
                            j2 = j - 1 + jj
                            for rl in range(4):
                                slot = jj * 4 + rl
                                mm(POc[0:64, 0:129], pt_[:, slot, :], Vst[j2 % 4][:, rl, 0:129], (j2 == 0 and rl == 0), False,
                                   [ptk, 'Vst%d' % (j2 % 4)], ['POc'])
                pw, pk = next_pw()
                mm(pw[0:32, 0:64], KnT[:, 0, :], qblk_r[:, b, :], True, True, ['KnT', 'qblk_r'], [pk])
                act(pTn[:], pw[0:32, 0:64], AF.Exp, [pk], ['pTn'], scale=SCALE)
                tt('dve', pTn[:], pTn[:], k_mnew[:, b, :], ALU.mult, ['pTn', 'k_mnew'], ['pTn'])
                mm(POc[0:64, 0:129], pTn[:], Vn[:, 0, 0:129], False, True, ['pTn', 'Vn'], ['POc'])
                s_branch(POc, 'POc', 1, False)
            s_branch(POw, 'POw', 2, slv < 3)
            for h in range(8):
                S.dma('sp', bo_f[8 * b:8 * b + 8, h * 64:(h + 1) * 64], Ocomb[h * 8:(h + 1) * 8, :], reads=['Ocomb'],
                      writes=['bo_f%d' % (b * 8 + h)])
        tt('dve', merged[0:32, 512:1024], bo_f[0:32], szb[0:32], ALU.mult, ['szb'] + ['bo_f%d' % k for k in range(32)],
           ['merged_b'])
    else:
        S.op('pool', lambda e: e.memset(merged[0:32, 512:1024], 0.0), writes=['merged_b'])
    token_back(xb, kx, 32, ys[:, :])
    S.finish()
    sst.close()
    setup_stack.close()
    return nc, consts


_PROG = {}


def kernel(x_prompt, x_sample, cache_kv, state_win, page_table, norm_g, w_in, ln_g, ln_b, w_s, b_s, cmp_pos,
           w_cmp1, b_cmp1, w_cmp2, w_out, final_g, _flags=(4, 4)):
    f32 = np.float32
    if _flags not in _PROG:
        _PROG[_flags] = build(*_flags)
    nc, consts = _PROG[_flags]
    x_prompt = np.asarray(x_prompt, f32)
    x_sample = np.asarray(x_sample, f32)
    cache = np.asarray(cache_kv, f32).reshape(N_PHYS * 128, 2, 256)
    state_win = np.asarray(state_win, f32)
    page_table = np.asarray(page_table, np.int32)
    shared = {
        'cache0': np.ascontiguousarray(cache[:, 0, :]).reshape(N_PHYS * 32, 1024),
        'cache1': np.ascontiguousarray(cache[:, 1, :]).reshape(N_PHYS * 32, 1024),
        'gT': np.ascontiguousarray(np.asarray(norm_g, f32).reshape(8, 128).T),
        'w_in': np.asarray(w_in, f32).reshape(D_MODEL, D_IN),
        'w_out': np.asarray(w_out, f32).reshape(D_MODEL, D_MODEL),
        'lngb': np.ascontiguousarray(np.broadcast_to(
            np.stack([np.asarray(ln_g, f32).reshape(512), np.asarray(ln_b, f32).reshape(512)])[None], (128, 2, 512))),
        'fgb': np.ascontiguousarray(np.broadcast_to(np.asarray(final_g, f32).reshape(1, D_MODEL), (128, D_MODEL))),
        'w_s': np.asarray(w_s, f32).reshape(8, 128, 128),
        'bsT': np.ascontiguousarray(np.asarray(b_s, f32).reshape(8, 128).T),
        'bsS': np.ascontiguousarray(np.tile(np.asarray(b_s, f32).reshape(8, 128)[:, 0:8].T, (4, 1))),
        'peT': np.ascontiguousarray(np.asarray(cmp_pos, f32).reshape(2, 32, 64).transpose(2, 0, 1)),
        'w1': np.asarray(w_cmp1, f32).reshape(2, 2048, 128),
        'b1T': np.ascontiguousarray(np.asarray(b_cmp1, f32).reshape(2, 128).T),
        'w2': np.asarray(w_cmp2, f32).reshape(2, 128, 64),
    }
    for k, v in consts.items():
        shared['c_' + k] = v
    in_maps = []
    for c in range(8):
        m = dict(shared)
        m['xp'] = x_prompt[c]
        m['xs'] = np.ascontiguousarray(x_sample[4 * c:4 * c + 4].reshape(32, D_MODEL))
        m['swin'] = np.ascontiguousarray(state_win[0, 4 * c:4 * c + 4].reshape(4, 512, 256))
        pt = page_table[4 * c:4 * c + 4]
        ptl = pt.reshape(4, 32, 4).transpose(2, 0, 1).reshape(4, 128)
        m['ptl'] = np.ascontiguousarray(np.repeat(ptl, 32, axis=0).astype(np.int32))
        in_maps.append(m)
    res = run_bass_kernel_spmd(nc, in_maps, core_ids=list(range(8)))
    r = res.results
    y_prompt = np.stack([r[c]['yp'] for c in range(8)]).reshape(8, SEQ, D_MODEL)
    y_sample = np.concatenate([r[c]['ys'].reshape(4, 8, D_MODEL) for c in range(8)], 0)
    kv_p = np.stack([r[c]['kvp'] for c in range(8)]).reshape(1, 8, SEQ, 4, 2, 64)
    win_p = np.stack([r[c]['winp'] for c in range(8)]).reshape(1, 8, 512, 2, 2, 64)
    kv_s = np.concatenate([r[c]['kvs'].reshape(4, 8, 4, 2, 64) for c in range(8)], 0).reshape(1, 32, 8, 4, 2, 64)
    win_s = np.concatenate([r[c]['wins'] for c in range(8)], 0).reshape(1, 32, 512, 2, 2, 64)
    v_s = np.concatenate([r[c]['vso'].reshape(4, 8, 512) for c in range(8)], 0).reshape(1, 32, 8, 512)
    return (y_prompt.astype(f32), y_sample.astype(f32), kv_p.astype(f32), win_p.astype(f32), kv_s.astype(f32),
            win_s.astype(f32), v_s.astype(f32))
```
